# Optimizing a Trainium2 kernel written in Bass

```python
import math
import jax, jax.numpy as jnp
from jax import lax
import numpy as np

D_MODEL = 1024
BATCH = 4
SEQ = 8192
DEPTH = 1

N_ATTN_HEADS = 8
HEAD_DIM = 64
ATTN_WIDTH = N_ATTN_HEADS * HEAD_DIM
ROT_DIM = HEAD_DIM // 4
ROPE_THETA = 500000.0
MOBA_BLOCK = 256
MOBA_TOP_K = 3
Q_CHUNK = 64
SSM_WIDTH = D_MODEL // 2
SSM_GROUP = 16
N_SSM_GROUPS = SSM_WIDTH // SSM_GROUP
SSM_STATE = 64
DT_MIN = 1e-3
DT_MAX = 1e-1
D_FF = 2816
CONV_WIDTH = 3
IN_WIDTH = 3 * ATTN_WIDTH + SSM_WIDTH + 2 * D_MODEL
LN_EPS = 1e-5
DEEPNORM_ALPHA = (2.0 * DEPTH) ** 0.25
DEEPNORM_BETA = (8.0 * DEPTH) ** -0.25
NEG_INF = -1e30

kernel_name = 'moba_s5_gated_hybrid_deepnorm_block'


def layer_norm(t, g, b):
    tf = t.astype(jnp.float32)
    mu = jnp.mean(tf, axis=-1, keepdims=True)
    var = jnp.mean(jnp.square(tf - mu), axis=-1, keepdims=True)
    y = (tf - mu) * lax.rsqrt(var + LN_EPS) * g.astype(jnp.float32) + b.astype(jnp.float32)
    return y.astype(t.dtype)


def rotary_tables(seq_len):
    inv_freq = ROPE_THETA ** (-jnp.arange(0, ROT_DIM, 2, dtype=jnp.float32) / ROT_DIM)
    pos = jnp.arange(seq_len, dtype=jnp.float32)
    ang = pos[:, None] * inv_freq[None, :]
    return jnp.cos(ang), jnp.sin(ang)


def apply_partial_rotary(t, cos, sin):
    half = ROT_DIM // 2
    r1 = t[..., :half]
    r2 = t[..., half:ROT_DIM]
    rest = t[..., ROT_DIM:]
    c = cos[None, :, None, :].astype(t.dtype)
    s = sin[None, :, None, :].astype(t.dtype)
    return jnp.concatenate([r1 * c - r2 * s, r1 * s + r2 * c, rest], axis=-1)


def moba_attention(q, k, v):
    bsz, seq, n_heads, dh = q.shape
    n_blocks = -(-seq // MOBA_BLOCK)
    seq_pad = n_blocks * MOBA_BLOCK
    pad = ((0, 0), (0, 0), (0, seq_pad - seq), (0, 0))
    qh = jnp.pad(q.transpose(0, 2, 1, 3), pad)
    kh = jnp.pad(k.transpose(0, 2, 1, 3), pad)
    vh = jnp.pad(v.transpose(0, 2, 1, 3), pad)
    kb = kh.reshape(bsz, n_heads, n_blocks, MOBA_BLOCK, dh)
    vb = vh.reshape(bsz, n_heads, n_blocks, MOBA_BLOCK, dh)
    k_mean = jnp.mean(kb.astype(jnp.float32), axis=3)
    top_k = min(MOBA_TOP_K, n_blocks)
    n_sel = top_k + 1
    b_idx = jnp.arange(bsz)[:, None, None, None]
    h_idx = jnp.arange(n_heads)[None, :, None, None]
    blk_ids = jnp.arange(n_blocks)
    k_offs = jnp.arange(MOBA_BLOCK)
    q_offs = jnp.arange(Q_CHUNK)
    scale = dh ** -0.5

    def one_chunk(ci):
        q0 = ci * Q_CHUNK
        own = q0 // MOBA_BLOCK
        qc = lax.dynamic_slice_in_dim(qh, q0, Q_CHUNK, axis=2)
        gate = jnp.einsum('bhqd,bhnd->bhqn', qc.astype(jnp.float32), k_mean)
        gate = jnp.where(blk_ids < own, gate, NEG_INF)
        _, sel = lax.top_k(gate, top_k)
        sel_valid = sel < own
        own_idx = jnp.broadcast_to(own, sel.shape[:-1] + (1,)).astype(sel.dtype)
        idx = jnp.concatenate([sel, own_idx], axis=-1)
        kg = kb[b_idx, h_idx, idx]
        vg = vb[b_idx, h_idx, idx]
        s = jnp.einsum('bhqd,bhqnkd->bhqnk', qc, kg, preferred_element_type=jnp.float32) * scale
        causal = (own * MOBA_BLOCK + k_offs)[None, :] <= (q0 + q_offs)[:, None]
        mask = jnp.concatenate([
            jnp.broadcast_to(sel_valid[..., None], sel.shape + (MOBA_BLOCK,)),
            jnp.broadcast_to(causal[None, None, :, None, :], (bsz, n_heads, Q_CHUNK, 1, MOBA_BLOCK)),
        ], axis=3)
        s = jnp.where(mask, s, NEG_INF)
        p = jax.nn.softmax(s.reshape(bsz, n_heads, Q_CHUNK, n_sel * MOBA_BLOCK), axis=-1)
        vflat = vg.reshape(bsz, n_heads, Q_CHUNK, n_sel * MOBA_BLOCK, dh)
        return jnp.einsum('bhqm,bhqmd->bhqd', p.astype(vflat.dtype), vflat)

    out = lax.map(one_chunk, jnp.arange(seq_pad // Q_CHUNK))
    out = out.transpose(1, 2, 0, 3, 4).reshape(bsz, n_heads, seq_pad, dh)[:, :, :seq]
    return out.transpose(0, 2, 1, 3).reshape(bsz, seq, n_heads * dh)


def s5_branch(u, a_re, a_im, log_dt, b_re, b_im, c_re, c_im, d_skip, w_glu):
    bsz, seq, _ = u.shape
    f32 = jnp.float32
    uf = u.astype(f32).reshape(bsz, seq, N_SSM_GROUPS, SSM_GROUP)
    a_re = a_re.astype(f32)
    a_im = a_im.astype(f32)
    dt = jnp.exp(log_dt.astype(f32))[:, None]
    mag = jnp.exp(a_re * dt)
    lb_re = mag * jnp.cos(a_im * dt)
    lb_im = mag * jnp.sin(a_im * dt)
    den = a_re * a_re + a_im * a_im
    n_re = lb_re - 1.0
    n_im = lb_im
    z_re = (n_re * a_re + n_im * a_im) / den
    z_im = (n_im * a_re - n_re * a_im) / den
    bu_re = jnp.einsum('gph,bsgh->bsgp', b_re.astype(f32), uf)
    bu_im = jnp.einsum('gph,bsgh->bsgp', b_im.astype(f32), uf)
    x_re = z_re * bu_re - z_im * bu_im
    x_im = z_re * bu_im + z_im * bu_re
    a_el_re = jnp.broadcast_to(lb_re, (1, seq, N_SSM_GROUPS, SSM_STATE))
    a_el_im = jnp.broadcast_to(lb_im, (1, seq, N_SSM_GROUPS, SSM_STATE))

    def combine(e1, e2):
        a1r, a1i, b1r, b1i = e1
        a2r, a2i, b2r, b2i = e2
        return (a2r * a1r - a2i * a1i,
                a2r * a1i + a2i * a1r,
                a2r * b1r - a2i * b1i + b2r,
                a2r * b1i + a2i * b1r + b2i)

    _, _, h_re, h_im = lax.associative_scan(combine, (a_el_re, a_el_im, x_re, x_im), axis=1)
    y = (jnp.einsum('ghp,bsgp->bsgh', c_re.astype(f32), h_re)
         - jnp.einsum('ghp,bsgp->bsgh', c_im.astype(f32), h_im)
         + d_skip.astype(f32) * uf)
    y = jax.nn.gelu(y.reshape(bsz, seq, SSM_WIDTH))
    y = y * jax.nn.sigmoid(y @ w_glu.astype(f32))
    return y.astype(u.dtype)


def causal_depthwise_conv(t, w, b):
    seq = t.shape[1]
    tp = jnp.pad(t, ((0, 0), (CONV_WIDTH - 1, 0), (0, 0)))
    out = b + tp[:, 0:seq, :] * w[0]
    for j in range(1, CONV_WIDTH):
        out = out + tp[:, j:j + seq, :] * w[j]
    return out


def setup_inputs(seed: int = 0) -> dict:
    key = jax.random.key(seed)
    ks = jax.random.split(key, 24)
    f32 = jnp.float32
    L = DEPTH

    def dense(k, fan_in, fan_out, scale=1.0):
        return jax.random.normal(k, (L, fan_in, fan_out), f32) * (scale * fan_in ** -0.5)

    x = jax.random.normal(ks[0], (BATCH, SEQ, D_MODEL), f32)
    col_scale = jnp.concatenate([
        jnp.ones((2 * ATTN_WIDTH,), f32),
        jnp.full((ATTN_WIDTH + SSM_WIDTH,), DEEPNORM_BETA, f32),
        jnp.ones((2 * D_MODEL,), f32)])
    w_in = dense(ks[1], D_MODEL, IN_WIDTH) * col_scale
    w_attn_proj = dense(ks[2], ATTN_WIDTH, D_MODEL)
    n = jnp.arange(SSM_STATE, dtype=f32)
    gp = (L, N_SSM_GROUPS, SSM_STATE)
    ssm_a_re = -0.5 + 0.01 * jax.random.normal(ks[3], gp, f32)
    ssm_a_im = math.pi * n + 0.01 * jax.random.normal(ks[4], gp, f32)
    ssm_log_dt = jax.random.uniform(ks[5], (L, N_SSM_GROUPS), f32,
                                    minval=math.log(DT_MIN), maxval=math.log(DT_MAX))
    bshape = (L, N_SSM_GROUPS, SSM_STATE, SSM_GROUP)
    cshape = (L, N_SSM_GROUPS, SSM_GROUP, SSM_STATE)
    bscale = (2.0 * SSM_GROUP) ** -0.5
    cscale = (2.0 * SSM_STATE) ** -0.5
    ssm_b_re = jax.random.normal(ks[6], bshape, f32) * bscale
    ssm_b_im = jax.random.normal(ks[7], bshape, f32) * bscale
    ssm_c_re = jax.random.normal(ks[8], cshape, f32) * cscale
    ssm_c_im = jax.random.normal(ks[9], cshape, f32) * cscale
    ssm_d = jax.random.normal(ks[10], (L, N_SSM_GROUPS, SSM_GROUP), f32)
    w_glu = dense(ks[11], SSM_WIDTH, SSM_WIDTH)
    w_ssm_proj = dense(ks[12], SSM_WIDTH, D_MODEL)
    w_out = dense(ks[13], D_MODEL, D_MODEL, DEEPNORM_BETA)
    ln1_g = 1.0 + 0.02 * jax.random.normal(ks[14], (L, D_MODEL), f32)
    ln1_b = 0.02 * jax.random.normal(ks[15], (L, D_MODEL), f32)
    w_up = dense(ks[16], D_MODEL, 2 * D_FF, DEEPNORM_BETA)
    conv_w = jax.random.normal(ks[17], (L, CONV_WIDTH, 2 * D_FF), f32) * CONV_WIDTH ** -0.5
    conv_b = 0.01 * jax.random.normal(ks[18], (L, 2 * D_FF), f32)
    w_down = dense(ks[19], D_FF, D_MODEL, DEEPNORM_BETA)
    ln2_g = 1.0 + 0.02 * jax.random.normal(ks[20], (L, D_MODEL), f32)
    ln2_b = 0.02 * jax.random.normal(ks[21], (L, D_MODEL), f32)
    return {'x': x, 'w_in': w_in, 'w_attn_proj': w_attn_proj,
            'ssm_a_re': ssm_a_re, 'ssm_a_im': ssm_a_im, 'ssm_log_dt': ssm_log_dt,
            'ssm_b_re': ssm_b_re, 'ssm_b_im': ssm_b_im, 'ssm_c_re': ssm_c_re, 'ssm_c_im': ssm_c_im,
            'ssm_d': ssm_d, 'w_glu': w_glu, 'w_ssm_proj': w_ssm_proj, 'w_out': w_out,
            'ln1_g': ln1_g, 'ln1_b': ln1_b, 'w_up': w_up, 'conv_w': conv_w, 'conv_b': conv_b,
            'w_down': w_down, 'ln2_g': ln2_g, 'ln2_b': ln2_b}


def reference(x, w_in, w_attn_proj, ssm_a_re, ssm_a_im, ssm_log_dt, ssm_b_re, ssm_b_im,
              ssm_c_re, ssm_c_im, ssm_d, w_glu, w_ssm_proj, w_out, ln1_g, ln1_b,
              w_up, conv_w, conv_b, w_down, ln2_g, ln2_b):
    bsz, seq, _ = x.shape
    cos, sin = rotary_tables(seq)
    splits = np.cumsum([ATTN_WIDTH, ATTN_WIDTH, ATTN_WIDTH, SSM_WIDTH, D_MODEL])
    h = x
    for l in range(DEPTH):
        proj = h @ w_in[l]
        q, k, v, u, g_attn, g_ssm = jnp.split(proj, splits, axis=-1)
        q = apply_partial_rotary(q.reshape(bsz, seq, N_ATTN_HEADS, HEAD_DIM), cos, sin)
        k = apply_partial_rotary(k.reshape(bsz, seq, N_ATTN_HEADS, HEAD_DIM), cos, sin)
        v = v.reshape(bsz, seq, N_ATTN_HEADS, HEAD_DIM)
        attn = moba_attention(q, k, v)
        ssm = s5_branch(u, ssm_a_re[l], ssm_a_im[l], ssm_log_dt[l], ssm_b_re[l], ssm_b_im[l],
                        ssm_c_re[l], ssm_c_im[l], ssm_d[l], w_glu[l])
        merged = (jax.nn.sigmoid(g_attn) * (attn @ w_attn_proj[l])
                  + jax.nn.sigmoid(g_ssm) * (ssm @ w_ssm_proj[l]))
        mix = merged @ w_out[l]
        h = layer_norm(DEEPNORM_ALPHA * h + mix, ln1_g[l], ln1_b[l])
        up = causal_depthwise_conv(h @ w_up[l], conv_w[l], conv_b[l])
        val, gate = jnp.split(up, 2, axis=-1)
        ff = (jax.nn.gelu(gate) * val) @ w_down[l]
        h = layer_norm(DEEPNORM_ALPHA * h + ff, ln2_g[l], ln2_b[l])
    return h
```

```python
import math
from contextlib import ExitStack

import numpy as np
import ml_dtypes
import concourse.bass as bass
import concourse.mybir as mybir
from concourse.bass_utils import run_bass_kernel_spmd

F32 = mybir.dt.float32
BF16 = mybir.dt.bfloat16
AF = mybir.ActivationFunctionType
ALU = mybir.AluOpType
AX = mybir.AxisListType

D = 1024
NH = 8
DH = 64
FF = 2816
ALPHA = 2.0 ** 0.25
LN_EPS = 1e-5
GC = 0.7978845608028654


class Buf:
    __slots__ = ("name", "w", "r", "excl")

    def __init__(self, name, excl=False):
        self.name = name
        self.w = None
        self.r = []
        self.excl = excl


class KB:
    def __init__(self, nc, es):
        self.nc = nc
        self.es = es
        self.engs = {"pe": nc.tensor, "act": nc.scalar, "dve": nc.vector, "pool": nc.gpsimd, "sp": nc.sync}
        self.cur = {}
        self.waited = {e: {} for e in self.engs}
        self.nsem = 0
        self.rings = {}
        self.ridx = {}
        self.last = {}
        self.banks = []
        self.bank_i = 0
        self.ninst = 0

    def newsem(self, name):
        self.nsem += 1
        return self.es.enter_context(self.nc.semaphore(f"{name}{self.nsem}"))

    def _wait(self, e, tok):
        sem, val, _ = tok
        w = self.waited[e]
        key = id(sem)
        if w.get(key, 0) >= val:
            return
        self.engs[e].wait_ge(sem, val)
        w[key] = val

    def _deps(self, e, reads, writes):
        deps = []
        for b in reads:
            if b.w is not None:
                deps.append(b.w)
            if b.excl:
                deps.extend(b.r)
        for b in writes:
            if b.w is not None:
                deps.append(b.w)
            deps.extend(b.r)
        for t in deps:
            if e == "pe" and t[2] == "pe":
                continue
            self._wait(e, t)

    def _upd(self, tok, reads, writes):
        for b in reads:
            if b.excl:
                b.w = tok
                b.r = []
            else:
                b.r.append(tok)
        for b in writes:
            b.w = tok
            b.r = []

    def op(self, e, fns, reads=(), writes=()):
        if callable(fns):
            fns = [fns]
        self._deps(e, reads, writes)
        ins = None
        for f in fns:
            ins = f()
            self.ninst += 1
        c = self.cur.get(e)
        if c is None or c[1] >= 30000:
            c = [self.newsem("s" + e), 0]
            self.cur[e] = c
        c[1] += 1
        ins.then_inc(c[0], 1)
        tok = (c[0], c[1], e)
        self.last[e] = tok
        self._upd(tok, reads, writes)
        return tok

    def dma(self, out, in_, reads=(), writes=(), q="sp", **kw):
        ring = self.rings.get(q)
        if ring is None:
            ring = [[self.newsem("d" + q), 0] for _ in range(8)]
            self.rings[q] = ring
        i = self.ridx.get(q, 0)
        self.ridx[q] = i + 1
        slot = ring[i % 8]
        if slot[1] > 0:
            self._wait(q, (slot[0], slot[1], "dma"))
        if slot[1] >= 30000:
            slot[0] = self.newsem("d" + q)
            slot[1] = 0
        self._deps(q, reads, writes)
        ins = self.engs[q].dma_start(out=out, in_=in_, **kw)
        self.ninst += 1
        slot[1] += 16
        ins.then_inc(slot[0], 16)
        tok = (slot[0], slot[1], "dma")
        self._upd(tok, reads, writes)
        return tok

    def barrier(self):
        toks = list(self.last.values())
        for q, ring in self.rings.items():
            for s in ring:
                if s[1] > 0:
                    toks.append((s[0], s[1], "dma"))
        for e in self.engs:
            for t in toks:
                self._wait(e, t)

    def bank(self):
        b = self.banks[self.bank_i % len(self.banks)]
        self.bank_i += 1
        return b


def _mm(nc, out, lhsT, rhs, start, stop):
    return lambda: nc.tensor.matmul(out, lhsT, rhs, start=start, stop=stop)


def build(nc, NBLK=32, NCTX=15, dbg=(), phases=("proj", "attn", "ssm", "merge", "ffn")):
    NOWN = NBLK - NCTX
    T = NBLK * 256
    TO = NOWN * 256
    NOUT = (NOWN - 1) * 256

    def din(name, shape, dt=F32):
        return nc.dram_tensor(name, list(shape), dt, kind="ExternalInput").ap()

    def dscr(name, shape, dt):
        kind = "ExternalOutput" if name in dbg else "Internal"
        return nc.dram_tensor(name, list(shape), dt, kind=kind).ap()

    I = dict(
        xcat=din("xcat", [T, D]), w_in=din("w_in", [D, 4096]), w_attn_proj=din("w_attn_proj", [512, D]),
        ssm_a_re=din("ssm_a_re", [32, 64]), ssm_a_im=din("ssm_a_im", [32, 64]), ssm_log_dt=din("ssm_log_dt", [32]),
        ssm_b_re=din("ssm_b_re", [32, 64, 16]), ssm_b_im=din("ssm_b_im", [32, 64, 16]),
        ssm_c_re=din("ssm_c_re", [32, 16, 64]), ssm_c_im=din("ssm_c_im", [32, 16, 64]),
        ssm_d=din("ssm_d", [32, 16]), w_glu=din("w_glu", [512, 512]), w_ssm_proj=din("w_ssm_proj", [512, D]),
        w_out=din("w_out", [D, D]), ln1_g=din("ln1_g", [D]), ln1_b=din("ln1_b", [D]),
        w_up=din("w_up", [D, 2 * FF]), conv_w=din("conv_w", [3, 2 * FF]), conv_b=din("conv_b", [2 * FF]),
        w_down=din("w_down", [FF, D]), ln2_g=din("ln2_g", [D]), ln2_b=din("ln2_b", [D]),
        rot_cos=din("rot_cos", [128, T]), rot_sin=din("rot_sin", [128, T]),
        gbias=din("gbias", [NOWN, 128, 512]), has_prev=din("has_prev", [128, 1]),
        ident=din("ident", [128, 128]), cmask=din("cmask", [128, 512], BF16), blkind=din("blkind", [32, T], BF16),
    )
    out = nc.dram_tensor("out", [NOUT, D], F32, kind="ExternalOutput").ap()
    S = dict(
        KT=dscr("KT", [4, 128, T], BF16), QT=dscr("QT", [4, 128, TO], BF16),
        VP=dscr("VP", [T, NH, 65], BF16), UT=dscr("UT", [4, 128, T], BF16),
        SG=dscr("SG", [16, 128, TO], BF16), SEL=dscr("SEL", [NH, 128, NOWN * 64], F32),
        ATT=dscr("ATT", [TO, 512], BF16), SSMY=dscr("SSMY", [4, 128, TO], BF16),
        H1=dscr("H1", [TO, D], F32), SBT=dscr("SBT", [NH, 32, TO], BF16),
    )
    SB = {k: Buf("scr_" + k) for k in S}
    cfg = dict(NBLK=NBLK, NCTX=NCTX, NOWN=NOWN, T=T, TO=TO, NOUT=NOUT)

    with ExitStack() as es:
        k = KB(nc, es)
        for i in range(8):
            t = es.enter_context(nc.psum_tensor(f"bank{i}", [128, 512], F32))
            k.banks.append((t, Buf(f"bank{i}", excl=True)))
        identf = es.enter_context(nc.sbuf_tensor("identf", [128, 128], F32))
        identb = es.enter_context(nc.sbuf_tensor("identb", [128, 128], BF16))
        cb = Buf("consts")
        k.dma(identf[:], I["ident"], writes=[cb])
        k.op("dve", lambda: nc.vector.tensor_copy(identb[:], identf[:]), reads=[cb], writes=[cb])
        C = dict(identf=identf, identb=identb, cb=cb)
        with ExitStack() as esx:
            pre = None
            if "ssm" in phases and "proj" in phases:
                pass
            with ExitStack() as esw:
                def load_win():
                    win = esw.enter_context(nc.sbuf_tensor("win", [128, 8, 4096], BF16))
                    winb = [Buf(f"win{kt}") for kt in range(8)]
                    C["win"], C["winb"] = win, winb
                    load_weight_bf16(k, win, winb, I["w_in"], 8, 4096, None, None, None)
                if "ssm" in phases:
                    pre = ssm_pre(k, I, S, SB, C, cfg, esx, after_alloc=(load_win if "proj" in phases else None))
                    k.barrier()
                elif "proj" in phases:
                    load_win()
                if "proj" in phases:
                    phase_proj(k, I, S, SB, C, cfg)
                    k.barrier()
            if "attn" in phases or "ssm" in phases:
                gens = []
                if "attn" in phases:
                    n_att = NH * sum(NCTX + j + 1 for j in range(NOWN)) + 2
                    gens.append([attn_gen(k, I, S, SB, C, cfg, esx, k.banks[0:3], k.banks[3:5]), n_att, 0])
                if "ssm" in phases:
                    gens.append([ssm_gen(k, I, S, SB, C, cfg, pre, esx, k.banks[5:8]), NBLK * 16 + 12, 0])
                while gens:
                    g = min(gens, key=lambda t: t[2] / t[1])
                    try:
                        next(g[0])
                        g[2] += 1
                    except StopIteration:
                        gens.remove(g)
                k.barrier()
        if "merge" in phases:
            phase_merge(k, I, S, SB, C, cfg)
            k.barrier()
        if "ffn" in phases:
            phase_ffn(k, I, S, SB, C, cfg, out)
            k.barrier()
    return nc


def load_weight_bf16(k, dst, dst_bufs, src, nkt, ncols, stage, stage_bufs, cnt, rows=128):
    CH = 2048
    for kt in range(nkt):
        for c0 in range(0, ncols, CH):
            w = min(CH, ncols - c0)
            k.dma(dst[0:rows, kt, c0:c0 + w], src[kt * rows:(kt + 1) * rows, c0:c0 + w], writes=[dst_bufs[kt]], q="pool")


def phase_proj(k, I, S, SB, C, cfg):
    nc = k.nc
    NBLK, NCTX, NOWN = cfg["NBLK"], cfg["NCTX"], cfg["NOWN"]
    identf, cb = C["identf"], C["cb"]
    with ExitStack() as es:
        def sb(n, shp, dt):
            return es.enter_context(nc.sbuf_tensor(n, shp, dt)), Buf(n)
        win, winb = C["win"], C["winb"]
        wsw, wswb = sb("wsw", [128, 8, 1024], BF16)
        k.op("pool", lambda: nc.gpsimd.memset(wsw[:], 0.0), writes=[wswb])
        for kt in range(8):
            src = win[:, kt, 0:1024].rearrange("p (h d) -> p h d", d=64)
            dst = wsw[:, kt, :].rearrange("p (h d) -> p h d", d=64)
            k.op("dve", lambda dst=dst, src=src: nc.vector.tensor_scalar_mul(dst[:, :, 0:8], src[:, :, 8:16], -1.0),
                 reads=[winb[kt]], writes=[wswb])
            k.op("dve", lambda dst=dst, src=src: nc.vector.tensor_copy(dst[:, :, 8:16], src[:, :, 0:8]),
                 reads=[winb[kt]], writes=[wswb])
        xt = [sb(f"xt{i}", [128, 2, 1024], F32) for i in range(2)]
        xT = [sb(f"xT{i}", [128, 8, 256], BF16) for i in range(2)]
        rc = [sb(f"rc{i}", [128, 256], F32) for i in range(2)]
        rs = [sb(f"rs{i}", [128, 256], F32) for i in range(2)]
        t1 = [sb(f"t1_{i}", [128, 256], F32) for i in range(2)]
        t2 = [sb(f"t2_{i}", [128, 256], F32) for i in range(2)]
        rotk = [sb(f"rotk{i}", [128, 256], F32) for i in range(2)]
        rotq = [sb(f"rotq{i}", [128, 256], F32) for i in range(8)]
        kb16 = [sb(f"kb16_{i}", [128, 256], BF16) for i in range(2)]
        ub = [sb(f"ub{i}", [128, 2, 256], BF16) for i in range(2)]
        vp = [sb(f"vp{i}", [128, 8, 65], BF16) for i in range(2)]
        kmeanT, kmb = sb("kmeanT", [128, 4, 64], F32)
        kms, kmsb = sb("kms", [128, 1], F32)
        gb, gbb = sb("gb", [128, 512], F32)
        sc, scb = sb("sc", [128, 512], F32)
        m8, m8b = sb("m8", [128, 16, 8], F32)
        selt, seltb = sb("selt", [128, 512], F32)
        vm, vmb = sb("vm", [128, 512], F32)
        sbT, sbTb = sb("sbT", [32, 16, 128], BF16)
        k.op("dve", lambda: nc.vector.memset(kmeanT[:], 0.0), writes=[kmb])
        for v_, vb_ in vp:
            k.op("pool", lambda v_=v_: nc.gpsimd.memset(v_[:], 1.0), writes=[vb_])
        ctr = [0]

        def evac_eng():
            ctr[0] += 1
            return "act" if ctr[0] % 2 else "dve"

        def copy_op(e, o, i_):
            if e == "act":
                return lambda: nc.scalar.copy(o, i_)
            if e == "dve":
                return lambda: nc.vector.tensor_copy(o, i_)
            return lambda: nc.gpsimd.tensor_copy(o, i_)

        def fm_group(region, W, wb, col0, xTt, xTb, bankb):
            fns = [_mm(nc, region, W[:, kt, col0:col0 + 128], xTt[:, kt, :], kt == 0, kt == 7) for kt in range(8)]
            k.op("pe", fns, reads=list(wb) + [xTb], writes=[bankb])

        import os
        STOP = int(os.environ.get("PROJ_STOP", "9"))

        def gating_a(i):
            j = i - NCTX
            rq = [rotq[(i % 2) * 4 + p] for p in range(4)]
            bt, bb = k.bank()
            fns = []
            for p in range(4):
                for qs in range(2):
                    idx = p * 2 + qs
                    fns.append(_mm(nc, bt[:, idx * 64:(idx + 1) * 64], rq[p][0][:, qs * 128:(qs + 1) * 128],
                                   kmeanT[:, p, :], True, True))
            k.op("pe", fns, reads=[rq[p][1] for p in range(4)] + [kmb], writes=[bb])
            k.dma(gb[:], I["gbias"][j], writes=[gbb])
            k.op("dve", lambda: nc.vector.tensor_tensor(out=sc[:], in0=bt[:, :], in1=gb[:], op=ALU.add),
                 reads=[bb, gbb], writes=[scb])
            for idx in range(16):
                k.op("dve", lambda idx=idx: nc.vector.max(out=m8[:, idx, :], in_=sc[:, idx * 32:(idx + 1) * 32]),
                     reads=[scb], writes=[m8b])
            sc3 = sc[:, :].rearrange("p (a b) -> p a b", b=32)
            sel3 = selt[:, :].rearrange("p (a b) -> p a b", b=32)
            k.op("dve", lambda: nc.vector.tensor_tensor(out=sel3, in0=sc3, in1=m8[:, :, 2:3].to_broadcast([128, 16, 32]),
                                                        op=ALU.is_ge), reads=[scb, m8b], writes=[seltb])
            k.op("dve", lambda: nc.vector.tensor_scalar(vm[:], gb[:], -1.0, None, op0=ALU.is_ge), reads=[gbb], writes=[vmb])
            k.op("dve", lambda: nc.vector.tensor_tensor(out=selt[:], in0=selt[:], in1=vm[:], op=ALU.mult),
                 reads=[vmb, seltb], writes=[seltb])
            k.op("dve", lambda: nc.vector.tensor_scalar(selt[:], selt[:], 30000.0, -30000.0, op0=ALU.mult, op1=ALU.add),
                 reads=[seltb], writes=[seltb])
            k.op("dve", lambda: nc.vector.memset(sel3[:, :, i:i + 1], 0.0), reads=[seltb], writes=[seltb])

        def gating_b(i):
            j = i - NCTX
            for g in range(4):
                bt2, bb2 = k.bank()
                fns = [(lambda q=q: nc.tensor.transpose(out=bt2[0:32, q * 128:(q + 1) * 128],
                                                        in_=selt[:, (4 * g + q) * 32:(4 * g + q + 1) * 32],
                                                        identity=identf[:])) for q in range(4)]
                k.op("pe", fns, reads=[seltb, cb], writes=[bb2])
                k.op("act", lambda bt2=bt2, g=g: nc.scalar.copy(sbT[:, 4 * g:4 * g + 4, :],
                                                              bt2[0:32, :].rearrange("p (a b) -> p a b", b=128)),
                     reads=[bb2], writes=[sbTb])
            for p in range(4):
                for hh in range(2):
                    k.dma(S["SBT"][2 * p + hh, :, j * 256:(j + 1) * 256].rearrange("n (q t) -> n q t", q=2),
                          sbT[:, 4 * p + hh:4 * p + hh + 3:2, :], reads=[sbTb])
        for i in range(NBLK if STOP > 0 else 0):
            own = i >= NCTX
            j = i - NCTX
            s = i % 2
            xt_t, xt_b = xt[s]
            xT_t, xT_b = xT[s]

            def load_blk(i2):
                s2 = i2 % 2
                k.dma(xt[s2][0][:], I["xcat"][i2 * 256:(i2 + 1) * 256, :].rearrange("(t p) f -> p t f", p=128), writes=[xt[s2][1]])
                k.dma(rc[s2][0][:], I["rot_cos"][:, i2 * 256:(i2 + 1) * 256], writes=[rc[s2][1]])
                k.dma(rs[s2][0][:], I["rot_sin"][:, i2 * 256:(i2 + 1) * 256], writes=[rs[s2][1]])

            if i == 0:
                load_blk(0)
            if i + 1 < NBLK:
                load_blk(i + 1)
            for tt in range(2):
                for g in range(2):
                    bt, bb = k.bank()
                    fns = [(lambda q=q: nc.tensor.transpose(out=bt[:, q * 128:(q + 1) * 128],
                                                            in_=xt_t[:, tt, (4 * g + q) * 128:(4 * g + q + 1) * 128],
                                                            identity=identf[:])) for q in range(4)]
                    k.op("pe", fns, reads=[xt_b, cb], writes=[bb])
                    e = evac_eng()
                    k.op(e, copy_op(e, xT_t[:, 4 * g:4 * g + 4, tt * 128:(tt + 1) * 128],
                                    bt[:, :].rearrange("p (a b) -> p a b", b=128)), reads=[bb], writes=[xT_b])
            if STOP < 2:
                continue
            rot_jobs = [("k", p) for p in range(4)] + ([("q", p) for p in range(4)] if own else [])
            for n_, (kind, p) in enumerate(rot_jobs):
                col0 = (512 if kind == "k" else 0) + p * 128
                bt, bb = k.bank()
                fm_group(bt[:, 0:256], win, winb, col0, xT_t, xT_b, bb)
                fm_group(bt[:, 256:512], wsw, [wswb], col0, xT_t, xT_b, bb)
                a1, a1b = t1[n_ % 2]
                a2, a2b = t2[n_ % 2]
                k.op("dve", lambda a1=a1, bt=bt: nc.vector.tensor_tensor(out=a1[:], in0=bt[:, 0:256], in1=rc[s][0][:], op=ALU.mult),
                     reads=[bb, rc[s][1]], writes=[a1b])
                k.op("dve", lambda a2=a2, bt=bt: nc.vector.tensor_tensor(out=a2[:], in0=bt[:, 256:512], in1=rs[s][0][:], op=ALU.mult),
                     reads=[bb, rs[s][1]], writes=[a2b])
                ro, rob = rotk[p % 2] if kind == "k" else rotq[(i % 2) * 4 + p]
                k.op("pool", lambda ro=ro, a1=a1, a2=a2: nc.gpsimd.tensor_tensor(out=ro[:], in0=a1[:], in1=a2[:], op=ALU.add),
                     reads=[a1b, a2b], writes=[rob])
                o16, o16b = kb16[n_ % 2]
                if kind == "k":
                    k.op("act", lambda o16=o16, ro=ro: nc.scalar.copy(o16[:], ro[:]), reads=[rob], writes=[o16b])
                    k.dma(S["KT"][p, :, i * 256:(i + 1) * 256], o16[:], reads=[o16b])
                    k.op("dve", lambda ro=ro: nc.vector.reduce_sum(out=kms[:], in_=ro[:], axis=AX.X), reads=[rob], writes=[kmsb])
                    k.op("dve", lambda p=p: nc.vector.tensor_scalar_mul(kmeanT[0:64, p, i:i + 1], kms[0:64, :], 1.0 / 256.0),
                         reads=[kmsb], writes=[kmb])
                    k.op("dve", lambda p=p: nc.vector.tensor_scalar_mul(kmeanT[64:128, p, 32 + i:32 + i + 1], kms[64:128, :], 1.0 / 256.0),
                         reads=[kmsb], writes=[kmb])
                else:
                    k.op("act", lambda o16=o16, ro=ro: nc.scalar.mul(o16[:], ro[:], 0.125), reads=[rob], writes=[o16b])
                    k.dma(S["QT"][p, :, j * 256:(j + 1) * 256], o16[:], reads=[o16b])
            if STOP >= 5 and i - 1 >= NCTX:
                gating_a(i - 1)
            if STOP < 3:
                continue
            plain = [("u", 0), ("u", 1)] + ([("g", g) for g in range(8)] if own else [])
            for n_, (kind, g) in enumerate(plain):
                col0 = (1536 + g * 256) if kind == "u" else (2048 + g * 256)
                bt, bb = k.bank()
                fm_group(bt[:, 0:256], win, winb, col0, xT_t, xT_b, bb)
                fm_group(bt[:, 256:512], win, winb, col0 + 128, xT_t, xT_b, bb)
                o, ob = ub[n_ % 2]
                ov = o[:, :, :].rearrange("p a b -> p (a b)")
                if kind == "u":
                    k.op("act", lambda ov=ov, bt=bt: nc.scalar.copy(ov, bt[:, :]), reads=[bb], writes=[ob])
                    k.dma(S["UT"][2 * g:2 * g + 2, :, i * 256:(i + 1) * 256].rearrange("u p t -> p u t"), o[:], reads=[ob])
                else:
                    k.op("act", lambda ov=ov, bt=bt: nc.scalar.activation(out=ov, in_=bt[:, :], func=AF.Sigmoid),
                         reads=[bb], writes=[ob])
                    k.dma(S["SG"][2 * g:2 * g + 2, :, j * 256:(j + 1) * 256].rearrange("u p t -> p u t"), o[:], reads=[ob])
            if STOP < 4:
                continue
            for tt in range(2):
                bt, bb = k.bank()
                fns = [_mm(nc, bt[:, :], xT_t[:, kt, tt * 128:(tt + 1) * 128], win[:, kt, 1024:1536], kt == 0, kt == 7)
                       for kt in range(8)]
                k.op("pe", fns, reads=winb + [xT_b], writes=[bb])
                v_, vb_ = vp[tt]
                e = evac_eng()
                k.op(e, copy_op(e, v_[:, :, 0:64], bt[:, :].rearrange("p (h d) -> p h d", d=64)), reads=[bb], writes=[vb_])
                r0 = (i * 2 + tt) * 128
                k.dma(S["VP"][r0:r0 + 128, :, :], v_[:], reads=[vb_])
            if own and STOP >= 5 and i - 1 >= NCTX:
                gating_b(i - 1)
        if STOP >= 5:
            gating_a(NBLK - 1)
            gating_b(NBLK - 1)


def attn_gen(k, I, S, SB, C, cfg, es, sbanks, obanks):
    nc = k.nc
    NBLK, NCTX, NOWN, T, TO = cfg["NBLK"], cfg["NCTX"], cfg["NOWN"], cfg["T"], cfg["TO"]
    if True:
        def sb(n, shp, dt):
            return es.enter_context(nc.sbuf_tensor(n, shp, dt)), Buf(n)
        ktsb = [sb(f"ktsb{i}", [96, T], BF16) for i in range(1)]
        qtsb = [sb(f"qtsb{i}", [96, TO], BF16) for i in range(1)]
        vpsb = [sb(f"vpsb{i}", [128, NBLK * 2, 65], BF16) for i in range(2)]
        cm, cmb = sb("cm", [128, 512], BF16)
        pT = [sb(f"pT{i}", [128, 512], BF16) for i in range(4)]
        rcp, rcpb = sb("rcp", [128, 2], F32)
        ostg = [sb(f"ostg{i}", [128, 2, 64], BF16) for i in range(2)]
        k.dma(cm[:], I["cmask"], writes=[cmb])
        for kt_t, kt_b in ktsb:
            k.dma(kt_t[64:96, :], I["blkind"], writes=[kt_b])
        NSB = len(sbanks)
        LA = 2
        its = [(h, j, n) for h in range(NH) for j in range(NOWN) for n in range(NCTX + j + 1)]
        hbuf = {}
        jcount = [0]
        jslot = {}

        def load_head(h):
            s = 0
            p, hh = h // 2, h % 2
            kt_t, kt_b = ktsb[s]
            qt_t, qt_b = qtsb[s]
            vp_t, vp_b = vpsb[h % 2]
            k.dma(kt_t[0:64, :], S["KT"][p, hh * 64:(hh + 1) * 64, :], writes=[kt_b])
            k.dma(qt_t[0:64, :], S["QT"][p, hh * 64:(hh + 1) * 64, :], writes=[qt_b])
            k.dma(qt_t[64:96, :], S["SBT"][h], writes=[qt_b])
            vsrc = S["VP"][:, h, :].rearrange("(t p) c -> p t c", p=128)
            for t0 in range(0, NBLK * 2, 8):
                k.dma(vp_t[:, t0:t0 + 8, :], vsrc[:, t0:t0 + 8, :], writes=[vp_b])
            hbuf[h] = (kt_t, kt_b, qt_t, qt_b, vp_t, vp_b)

        def emit_front(idx):
            h, j, n = its[idx]
            if h not in hbuf:
                load_head(h)
            kt_t, kt_b, qt_t, qt_b, vp_t, vp_b = hbuf[h]
            diag = NCTX + j
            bs, bsb = sbanks[idx % NSB]
            fns = [_mm(nc, bs[:, kt * 256:(kt + 1) * 256], kt_t[:, n * 256 + kt * 128:n * 256 + (kt + 1) * 128],
                       qt_t[:, j * 256:(j + 1) * 256], True, True) for kt in range(2)]
            k.op("pe", fns, reads=[kt_b, qt_b], writes=[bsb])
            pt_t, pt_b = pT[idx % 4]
            k.op("act", lambda: nc.scalar.activation(out=pt_t[:], in_=bs[:, :], func=AF.Exp), reads=[bsb], writes=[pt_b])
            if n == diag:
                k.op("pool", lambda: nc.gpsimd.tensor_tensor(out=pt_t[:], in0=pt_t[:], in1=cm[:], op=ALU.mult),
                     reads=[cmb, pt_b], writes=[pt_b])

        def emit_back(idx):
            h, j, n = its[idx]
            kt_t, kt_b, qt_t, qt_b, vp_t, vp_b = hbuf[h]
            diag = NCTX + j
            pt_t, pt_b = pT[idx % 4]
            if n == 0:
                jslot[(h, j)] = jcount[0] % (len(obanks) // 2)
                jcount[0] += 1
            sl = jslot[(h, j)]
            fns = []
            for qs in range(2):
                bo, bob = obanks[sl * 2 + qs]
                fns += [_mm(nc, bo[:, 0:65], pt_t[:, kt * 256 + qs * 128:kt * 256 + (qs + 1) * 128], vp_t[:, n * 2 + kt, :],
                            n == 0 and kt == 0, n == diag and kt == 1) for kt in range(2)]
            k.op("pe", fns, reads=[pt_b, vp_b], writes=[obanks[sl * 2][1], obanks[sl * 2 + 1][1]])
            if n == diag:
                og_t, og_b = ostg[jcount[0] % 2]
                for qs in range(2):
                    bo, bob = obanks[sl * 2 + qs]
                    k.op("dve", lambda bo=bo, qs=qs: nc.vector.reciprocal(rcp[:, qs:qs + 1], bo[:, 64:65]), reads=[bob], writes=[rcpb])
                    k.op("dve", lambda bo=bo, qs=qs: nc.vector.tensor_scalar(
                        og_t[:, qs, :], bo[:, 0:64], rcp[:, qs:qs + 1], None, op0=ALU.mult),
                        reads=[bob, rcpb], writes=[og_b])
                k.dma(S["ATT"][j * 256:(j + 1) * 256, h * 64:(h + 1) * 64].rearrange("(q p) c -> p q c", p=128), og_t[:], reads=[og_b])

        for idx in range(len(its) + LA):
            if idx < len(its):
                emit_front(idx)
            if idx - LA >= 0:
                emit_back(idx - LA)
            yield


def ssm_pre(k, I, S, SB, C, cfg, esp, after_alloc=None):
    nc = k.nc
    NBLK, NCTX, NOWN = cfg["NBLK"], cfg["NCTX"], cfg["NOWN"]
    identf, cb = C["identf"], C["cb"]
    N = 256
    with ExitStack() as es:
        def sb(n, shp, dt):
            return es.enter_context(nc.sbuf_tensor(n, shp, dt)), Buf(n)

        def sbp(n, shp, dt):
            return esp.enter_context(nc.sbuf_tensor(n, shp, dt)), Buf(n)
        P = Buf("ssm_pre")
        mag = sbp("mag", [128, 16], F32)[0]
        BzT = [sbp(f"BzT{i}", [128, 16, 128], BF16)[0] for i in range(2)]
        CT = [sbp(f"CT{i}", [128, 16, 128], BF16)[0] for i in range(3)]
        dcol, _ = sbp("dcol", [128, 4], F32)
        Ec, _ = sbp("Ec", [128, 16, N + 1], F32)
        Es, _ = sbp("Es", [128, 16, N + 1], F32)
        if after_alloc is not None:
            after_alloc()

        def small(n):
            return sb(n, [128, 16], F32)[0]
        a_re, a_im, ldt, dt_, ang = [small(n) for n in ("a_re", "a_im", "ldt", "dt_", "ang")]
        cc, ss, cs, c_, s_ = [small(n) for n in ("cc", "ss", "cs", "c_", "s_")]
        lr, li, den, nre, zr, zi, q1, q2 = [small(n) for n in ("lr", "li", "den", "nre", "zr", "zi", "q1", "q2")]
        Bre, _ = sb("Bre", [128, 16, 16], F32)
        Bim, _ = sb("Bim", [128, 16, 16], F32)
        Bzr, _ = sb("Bzr", [128, 16, 16], F32)
        Bzi, _ = sb("Bzi", [128, 16, 16], F32)
        Bt1, _ = sb("Bt1", [128, 16, 16], F32)
        Bt2, _ = sb("Bt2", [128, 16, 16], F32)
        ZP, _ = sb("ZP", [128, 16, 128], F32)
        tmp = [sb(f"tmpE{i}", [128, 16, 128], F32)[0] for i in range(4)]

        def dv(f, r=(), w=()):
            k.op("dve", f, reads=[P] + list(r), writes=[P] + list(w))

        def tt(o, a, b, op):
            dv(lambda: nc.vector.tensor_tensor(out=o, in0=a, in1=b, op=op))

        def ts(o, a, s1, op0, s2=None, op1=None):
            if op1 is None:
                dv(lambda: nc.vector.tensor_scalar(o, a, s1, None, op0=op0))
            else:
                dv(lambda: nc.vector.tensor_scalar(o, a, s1, s2, op0=op0, op1=op1))

        def act(o, a, func, **kw):
            k.op("act", lambda: nc.scalar.activation(out=o, in_=a, func=func, **kw), reads=[P], writes=[P])

        for gp in range(2):
            rows = slice(gp * 64, (gp + 1) * 64)
            k.dma(a_re[rows, :], I["ssm_a_re"].rearrange("(i two) p -> two p i", two=2)[gp], writes=[P],
                  allow_slow_non_contiguous=True)
            k.dma(a_im[rows, :], I["ssm_a_im"].rearrange("(i two) p -> two p i", two=2)[gp], writes=[P],
                  allow_slow_non_contiguous=True)
            k.dma(ldt[rows, :], I["ssm_log_dt"].rearrange("(i two) -> two i", two=2)[gp].partition_broadcast(64), writes=[P],
                  allow_slow_non_contiguous=True)
            k.dma(Bre[rows, :, :], I["ssm_b_re"].rearrange("(i two) p h -> two p i h", two=2)[gp], writes=[P])
            k.dma(Bim[rows, :, :], I["ssm_b_im"].rearrange("(i two) p h -> two p i h", two=2)[gp], writes=[P])
        k.dma(dcol[:, :], I["ssm_d"].rearrange("(ut gl) h -> (gl h) ut", gl=8), writes=[P], allow_slow_non_contiguous=True)
        act(dt_[:], ldt[:], AF.Exp)
        tt(q1[:], a_re[:], dt_[:], ALU.mult)
        act(mag[:], q1[:], AF.Exp)
        tt(ang[:], a_im[:], dt_[:], ALU.mult)
        act(s_[:], ang[:], AF.Sin, scale=1.0 / 16.0)
        ts(q2[:], ang[:], -1.0 / 16.0, ALU.mult, math.pi / 2.0, ALU.add)
        act(c_[:], q2[:], AF.Sin)
        for _ in range(4):
            tt(cc[:], c_[:], c_[:], ALU.mult)
            tt(ss[:], s_[:], s_[:], ALU.mult)
            tt(cs[:], c_[:], s_[:], ALU.mult)
            tt(c_[:], cc[:], ss[:], ALU.subtract)
            ts(s_[:], cs[:], 2.0, ALU.mult)
        tt(lr[:], mag[:], c_[:], ALU.mult)
        tt(li[:], mag[:], s_[:], ALU.mult)
        tt(q1[:], a_re[:], a_re[:], ALU.mult)
        tt(q2[:], a_im[:], a_im[:], ALU.mult)
        tt(den[:], q1[:], q2[:], ALU.add)
        dv(lambda: nc.vector.reciprocal(den[:], den[:]))
        ts(nre[:], lr[:], -1.0, ALU.add)
        tt(q1[:], nre[:], a_re[:], ALU.mult)
        tt(q2[:], li[:], a_im[:], ALU.mult)
        tt(q1[:], q1[:], q2[:], ALU.add)
        tt(zr[:], q1[:], den[:], ALU.mult)
        tt(q1[:], li[:], a_re[:], ALU.mult)
        tt(q2[:], nre[:], a_im[:], ALU.mult)
        tt(q1[:], q1[:], q2[:], ALU.subtract)
        tt(zi[:], q1[:], den[:], ALU.mult)
        zrb = zr[:, :].unsqueeze(2).to_broadcast([128, 16, 16])
        zib = zi[:, :].unsqueeze(2).to_broadcast([128, 16, 16])
        tt(Bt1[:], Bre[:], zrb, ALU.mult)
        tt(Bt2[:], Bim[:], zib, ALU.mult)
        tt(Bzr[:], Bt1[:], Bt2[:], ALU.subtract)
        tt(Bt1[:], Bim[:], zrb, ALU.mult)
        tt(Bt2[:], Bre[:], zib, ALU.mult)
        tt(Bzi[:], Bt1[:], Bt2[:], ALU.add)

        def transpose16(src, dst, scale=None):
            for g in range(4):
                bt, bb = k.bank()
                fns = [(lambda q=q: nc.tensor.transpose(out=bt[:, q * 128:(q + 1) * 128], in_=src[:, 4 * g + q, :],
                                                        identity=identf[:])) for q in range(4)]
                k.op("pe", fns, reads=[P, cb], writes=[bb])
                o = dst[:, 4 * g:4 * g + 4, :]
                i_ = bt[:, :].rearrange("p (a b) -> p a b", b=128)
                if scale is None:
                    k.op("dve", lambda o=o, i_=i_: nc.vector.tensor_copy(o, i_), reads=[bb, P], writes=[P])
                else:
                    k.op("dve", lambda o=o, i_=i_: nc.vector.tensor_scalar_mul(o, i_, scale), reads=[bb, P], writes=[P])

        for ri, Bz in enumerate((Bzr, Bzi)):
            dv(lambda: nc.vector.memset(ZP[:], 0.0))
            for gp in range(2):
                for kq in range(4):
                    o = ZP[gp * 64:(gp + 1) * 64, kq::4, 32 * kq + 16 * gp:32 * kq + 16 * gp + 16]
                    i_ = Bz[gp * 64:(gp + 1) * 64, kq::4, :]
                    dv(lambda o=o, i_=i_: nc.vector.tensor_copy(o, i_))
            transpose16(ZP, BzT[ri])
        for ri, nm in enumerate(("ssm_c_re", "ssm_c_im")):
            dv(lambda: nc.vector.memset(ZP[:], 0.0))
            for kq in range(4):
                for gp in range(2):
                    r0 = (2 * kq + gp) * 16
                    k.dma(ZP[r0:r0 + 16, kq::4, gp * 64:(gp + 1) * 64],
                          I[nm].rearrange("(ut r) h p -> r h ut p", r=8)[2 * kq + gp], reads=[P], writes=[P])
            transpose16(ZP, CT[ri], scale=(None if ri == 0 else -1.0))
            if ri == 0:
                transpose16(ZP, CT[2], scale=-1.0)
        dv(lambda: nc.vector.memset(Ec[:, :, 0:1], 1.0))
        dv(lambda: nc.vector.memset(Es[:, :, 0:1], 0.0))
        dv(lambda: nc.vector.tensor_copy(Ec[:, :, 1], c_[:]))
        dv(lambda: nc.vector.tensor_copy(Es[:, :, 1], s_[:]))
        m = 1
        while m < N:
            cmb_ = Ec[:, :, m:m + 1].to_broadcast([128, 16, m])
            smb_ = Es[:, :, m:m + 1].to_broadcast([128, 16, m])
            e1c = Ec[:, :, 1:m + 1]
            e1s = Es[:, :, 1:m + 1]
            tA, tB, tC, tD = [t[:, :, 0:m] for t in tmp]
            tt(tA, e1c, cmb_, ALU.mult)
            tt(tB, e1s, smb_, ALU.mult)
            tt(tC, e1c, smb_, ALU.mult)
            tt(tD, e1s, cmb_, ALU.mult)
            tt(Ec[:, :, m + 1:2 * m + 1], tA, tB, ALU.subtract)
            tt(Es[:, :, m + 1:2 * m + 1], tC, tD, ALU.add)
            m *= 2

    return dict(P=P, mag=mag, BzT=BzT, CT=CT, dcol=dcol, Ec=Ec, Es=Es)


def ssm_gen(k, I, S, SB, C, cfg, pre, es, banks):
    nc = k.nc
    NBLK, NCTX, NOWN = cfg["NBLK"], cfg["NCTX"], cfg["NOWN"]
    N = 256
    P, mag, BzT, CT, dcol, Ec, Es = pre["P"], pre["mag"], pre["BzT"], pre["CT"], pre["dcol"], pre["Ec"], pre["Es"]
    bki = [0]

    xbanks = banks[0:len(banks) - 1]

    def nbank():
        bki[0] += 1
        return xbanks[bki[0] % len(xbanks)]
    if True:
        def sb(n, shp, dt):
            return es.enter_context(nc.sbuf_tensor(n, shp, dt)), Buf(n)
        wg_t, _ = sb("wglu", [128, 4, 512], BF16)
        wgb = [Buf(f"wglu{i}") for i in range(4)]
        load_weight_bf16(k, wg_t, wgb, I["w_glu"], 4, 512, None, None, None)
        NS = 3
        uc = [sb(f"uc{i}", [128, 4, N], BF16) for i in range(3)]
        xs = [sb(f"xs{i}", [128, 4, N], F32) for i in range(NS)]
        p1 = [sb(f"p1_{i}", [128, 2, N], F32) for i in range(NS)]
        p2 = [sb(f"p2_{i}", [128, 2, N], F32) for i in range(NS)]
        RBN = 6
        YD = 3
        rb = [sb(f"rb{i}", [128, 2, N], BF16) for i in range(RBN)]
        rb2 = [sb(f"rb2{i}", [128, 2, N], BF16) for i in range(RBN)]
        wi_ = [sb(f"win_{i}", [128, 2, N], F32) for i in range(NS)]
        W, _ = sb("Wst", [128, 16, 2, N], F32)
        wb = [Buf(f"W{i}") for i in range(16)]
        car = [sb(f"car{i}", [128, 16], F32) for i in range(2)]
        cai = [sb(f"cai{i}", [128, 16], F32) for i in range(2)]
        ct = [sb(f"ct{i}", [128, 16], F32) for i in range(4)]
        yv = [sb(f"yv{i}", [128, N], F32) for i in range(2)]
        ygf = [sb(f"ygf{i}", [128, 4, N], F32) for i in range(2)]
        ygb16, ygbb = sb("ygb16", [128, 4, N], BF16)
        sg = [sb(f"sg{i}", [128, N], F32) for i in range(2)]
        so, sob = sb("so", [128, 4, N], BF16)
        k.op("dve", lambda: nc.vector.memset(car[0][0][:], 0.0), writes=[car[0][1]])
        k.op("dve", lambda: nc.vector.memset(cai[0][0][:], 0.0), writes=[cai[0][1]])

        def load_u(c):
            u_t, u_b = uc[c % 3]
            k.dma(u_t[:], S["UT"][:, :, c * N:(c + 1) * N].rearrange("u p t -> p u t"), writes=[u_b])

        def in_a(g):
            c, i = divmod(g, 16)
            own = c >= NCTX
            if i == 0 and c + 1 < NBLK:
                load_u(c + 1)
            u_t, u_b = uc[c % 3]
            bt, bb = nbank()
            fns = [_mm(nc, bt[:, 0:N], BzT[0][:, i, :], u_t[:, i // 4, :], True, True),
                   _mm(nc, bt[:, N:2 * N], BzT[1][:, i, :], u_t[:, i // 4, :], True, True)]
            k.op("pe", fns, reads=[P, u_b], writes=[bb])
            x_t, x_b = xs[g % NS]
            k.op("act", lambda: nc.scalar.copy(x_t[:, 0:2, :].rearrange("p a b -> p (a b)"), bt[:, :]), reads=[bb], writes=[x_b])
            k.op("act", lambda: nc.scalar.copy(x_t[:, 2, :], bt[:, N:2 * N]), reads=[bb], writes=[x_b])
            k.op("act", lambda: nc.scalar.mul(x_t[:, 3, :], bt[:, 0:N], -1.0), reads=[bb], writes=[x_b])
            ecb = Ec[:, i:i + 1, 0:N].to_broadcast([128, 2, N])
            esb = Es[:, i:i + 1, 0:N].to_broadcast([128, 2, N])
            a_t, a_b = p1[g % NS]
            b_t, b_b = p2[g % NS]
            k.op("pool", lambda: nc.gpsimd.tensor_tensor(out=a_t[:], in0=x_t[:, 0:2, :], in1=ecb, op=ALU.mult),
                 reads=[x_b, P], writes=[a_b])
            k.op("pool", lambda: nc.gpsimd.tensor_tensor(out=b_t[:], in0=x_t[:, 2:4, :], in1=esb, op=ALU.mult),
                 reads=[x_b, P], writes=[b_b])

        def in_b(g):
            a_t, a_b = p1[g % NS]
            b_t, b_b = p2[g % NS]
            w_t, w_b = wi_[g % NS]
            k.op("dve", lambda: nc.vector.tensor_tensor(out=w_t[:], in0=a_t[:], in1=b_t[:], op=ALU.add),
                 reads=[a_b, b_b], writes=[w_b])

        def in_c(g):
            c, i = divmod(g, 16)
            cr_t, cr_b = car[c % 2]
            ci_t, ci_b = cai[c % 2]
            w_t, w_b = wi_[g % NS]
            magb = mag[:, i:i + 1].to_broadcast([128, N])
            k.op("dve", lambda: nc.vector.tensor_tensor_scan(
                out=W[:, i, 0, :], data0=magb, data1=w_t[:, 0, :], initial=cr_t[:, i:i + 1], op0=ALU.mult, op1=ALU.add),
                reads=[w_b, cr_b, P], writes=[wb[i]])
            k.op("dve", lambda: nc.vector.tensor_tensor_scan(
                out=W[:, i, 1, :], data0=magb, data1=w_t[:, 1, :], initial=ci_t[:, i:i + 1], op0=ALU.mult, op1=ALU.add),
                reads=[w_b, ci_b, P], writes=[wb[i]])
            if i == 15:
                ncr_t, ncr_b = car[(c + 1) % 2]
                nci_t, nci_b = cai[(c + 1) % 2]
                wlr = W[:, :, 0, N - 1]
                wli = W[:, :, 1, N - 1]
                enc = Ec[:, :, N]
                ens = Es[:, :, N]
                cts = [t[0] for t in ct]
                ctb = ct[0][1]
                k.op("dve", lambda: nc.vector.tensor_tensor(out=cts[0][:], in0=wlr, in1=enc, op=ALU.mult), reads=wb + [P], writes=[ctb])
                k.op("dve", lambda: nc.vector.tensor_tensor(out=cts[1][:], in0=wli, in1=ens, op=ALU.mult), reads=wb + [P, ctb], writes=[ctb])
                k.op("dve", lambda: nc.vector.tensor_tensor(out=cts[2][:], in0=wlr, in1=ens, op=ALU.mult), reads=wb + [P, ctb], writes=[ctb])
                k.op("dve", lambda: nc.vector.tensor_tensor(out=cts[3][:], in0=wli, in1=enc, op=ALU.mult), reads=wb + [P, ctb], writes=[ctb])
                k.op("dve", lambda: nc.vector.tensor_tensor(out=ncr_t[:], in0=cts[0][:], in1=cts[1][:], op=ALU.subtract),
                     reads=[ctb], writes=[ncr_b])
                k.op("dve", lambda: nc.vector.tensor_tensor(out=nci_t[:], in0=cts[2][:], in1=cts[3][:], op=ALU.add),
                     reads=[ctb], writes=[nci_b])

        def out_p(g):
            c, i = divmod(g, 16)
            b_t, b_b = rb[g % RBN]
            b2_t, b2_b = rb2[g % RBN]
            ecb = Ec[:, i:i + 1, 0:N].to_broadcast([128, 2, N])
            esb = Es[:, i:i + 1, 0:N].to_broadcast([128, 2, N])
            k.op("pool", lambda: nc.gpsimd.tensor_tensor(out=b_t[:], in0=W[:, i, :, :], in1=ecb, op=ALU.mult),
                 reads=[wb[i], P], writes=[b_b])
            k.op("dve", lambda: nc.vector.tensor_tensor(out=b2_t[:], in0=W[:, i, :, :], in1=esb, op=ALU.mult),
                 reads=[wb[i], P], writes=[b2_b])

        def out_y(g):
            c, i = divmod(g, 16)
            jc = c - NCTX
            u_t, u_b = uc[c % 3]
            b_t, b_b = rb[g % RBN]
            b2_t, b2_b = rb2[g % RBN]
            yg_t, yg_b = ygf[c % 2]
            ut, kq = i // 4, i % 4
            bt, bb = banks[-1]
            fns = [_mm(nc, bt[:, 0:N], CT[0][:, i, :], b_t[:, 0, :], kq == 0, False),
                   _mm(nc, bt[:, 0:N], CT[1][:, i, :], b_t[:, 1, :], False, False),
                   _mm(nc, bt[:, 0:N], CT[1][:, i, :], b2_t[:, 0, :], False, False),
                   _mm(nc, bt[:, 0:N], CT[2][:, i, :], b2_t[:, 1, :], False, kq == 3)]
            k.op("pe", fns, reads=[P, b_b, b2_b], writes=[bb])
            if kq == 3:
                y_t, y_b = yv[ut % 2]
                k.op("dve", lambda: nc.vector.scalar_tensor_tensor(
                    out=y_t[:], in0=u_t[:, ut, :], scalar=dcol[:, ut:ut + 1], in1=bt[:, 0:N], op0=ALU.mult, op1=ALU.add),
                    reads=[bb, u_b, P], writes=[y_b])
                k.op("act", lambda: nc.scalar.activation(out=yg_t[:, ut, :], in_=y_t[:], func=AF.Gelu_apprx_tanh),
                     reads=[y_b], writes=[yg_b])

        def glu(c):
            jc = c - NCTX
            yg_t, yg_b = ygf[c % 2]
            k.op("pool", lambda: nc.gpsimd.tensor_copy(ygb16[:], yg_t[:]), reads=[yg_b], writes=[ygbb])
            for mt in range(4):
                bt, bb = nbank()
                fns = [_mm(nc, bt[:, 0:N], wg_t[:, ut, mt * 128:(mt + 1) * 128], ygb16[:, ut, :], ut == 0, ut == 3) for ut in range(4)]
                k.op("pe", fns, reads=wgb + [ygbb], writes=[bb])
                s_t, s_b = sg[mt % 2]
                k.op("act", lambda s_t=s_t, bt=bt: nc.scalar.activation(out=s_t[:], in_=bt[:, 0:N], func=AF.Sigmoid),
                     reads=[bb], writes=[s_b])
                k.op("dve", lambda mt=mt, s_t=s_t: nc.vector.tensor_tensor(out=so[:, mt, :], in0=yg_t[:, mt, :], in1=s_t[:], op=ALU.mult),
                     reads=[s_b, yg_b], writes=[sob])
            k.dma(S["SSMY"][:, :, jc * N:(jc + 1) * N].rearrange("u p t -> p u t"), so[:], reads=[sob])

        G = NBLK * 16
        g0own = NCTX * 16
        load_u(0)
        GD = 4
        for step in range(G + 3 + YD + GD + 1):
            if step < G:
                in_a(step)
            if 0 <= step - 1 < G:
                in_b(step - 1)
            if 0 <= step - 2 < G:
                in_c(step - 2)
            g = step - 3
            if g0own <= g < G:
                out_p(g)
            g = step - 3 - YD
            if g0own <= g < G:
                out_y(g)
            g = step - 3 - YD - GD
            if g0own <= g < G and g % 16 == 15:
                glu(g // 16)
            yield


def ln_tile(k, ts_t, ts_b, g_t, b_t, gbuf, o_t, o_b, st, mv, lb):
    nc = k.nc
    st_t, mv_t = st, mv
    k.op("dve", lambda: nc.vector.bn_stats(out=st_t[:, 0, :], in_=ts_t[:, 0:512]), reads=[ts_b], writes=[lb])
    k.op("dve", lambda: nc.vector.bn_stats(out=st_t[:, 1, :], in_=ts_t[:, 512:1024]), reads=[ts_b, lb], writes=[lb])
    k.op("dve", lambda: nc.vector.bn_aggr(out=mv_t[:, 0:2], in_=st_t[:, :, :].rearrange("p a b -> p (a b)")), reads=[lb], writes=[lb])
    k.op("dve", lambda: nc.vector.tensor_scalar(mv_t[:, 2:3], mv_t[:, 1:2], LN_EPS, None, op0=ALU.add), reads=[lb], writes=[lb])
    k.op("act", lambda: nc.scalar.sqrt(mv_t[:, 3:4], mv_t[:, 2:3]), reads=[lb], writes=[lb])
    k.op("dve", lambda: nc.vector.reciprocal(mv_t[:, 4:5], mv_t[:, 3:4]), reads=[lb], writes=[lb])
    k.op("dve", lambda: nc.vector.tensor_scalar(o_t, ts_t[:, :], mv_t[:, 0:1], mv_t[:, 4:5], op0=ALU.subtract, op1=ALU.mult),
         reads=[ts_b, lb], writes=[o_b])
    k.op("pool", lambda: nc.gpsimd.tensor_tensor(out=o_t, in0=o_t, in1=g_t[:], op=ALU.mult), reads=[gbuf, o_b], writes=[o_b])
    k.op("pool", lambda: nc.gpsimd.tensor_tensor(out=o_t, in0=o_t, in1=b_t[:], op=ALU.add), reads=[gbuf, o_b], writes=[o_b])


def phase_merge(k, I, S, SB, C, cfg):
    nc = k.nc
    NBLK, NCTX, NOWN = cfg["NBLK"], cfg["NCTX"], cfg["NOWN"]
    identb, cb = C["identb"], C["cb"]
    with ExitStack() as es:
        def sb(n, shp, dt):
            return es.enter_context(nc.sbuf_tensor(n, shp, dt)), Buf(n)
        wap, _ = sb("wap", [128, 4, 1024], BF16)
        wsp, _ = sb("wsp", [128, 4, 1024], BF16)
        wout, _ = sb("wout", [128, 8, 1024], BF16)
        wapb = [Buf(f"wap{i}") for i in range(4)]
        wspb = [Buf(f"wsp{i}") for i in range(4)]
        woutb = [Buf(f"wout{i}") for i in range(8)]
        load_weight_bf16(k, wap, wapb, I["w_attn_proj"], 4, 1024, None, None, None)
        load_weight_bf16(k, wsp, wspb, I["w_ssm_proj"], 4, 1024, None, None, None)
        load_weight_bf16(k, wout, woutb, I["w_out"], 8, 1024, None, None, None)
        g1, gbuf = sb("g1", [128, 1024], F32)
        b1, _ = sb("b1", [128, 1024], F32)
        k.dma(g1[:], I["ln1_g"].partition_broadcast(128), writes=[gbuf])
        k.dma(b1[:], I["ln1_b"].partition_broadcast(128), writes=[gbuf])
        att = [sb(f"att{i}", [128, 2, 512], BF16) for i in range(2)]
        attT = [sb(f"attT{i}", [128, 4, 256], BF16) for i in range(2)]
        ssmy = [sb(f"ssmy{i}", [128, 4, 256], BF16) for i in range(2)]
        sgt = [sb(f"sgt{i}", [128, 16, 256], BF16) for i in range(2)]
        xo = [sb(f"xo{i}", [128, 2, 1024], F32) for i in range(2)]
        tA = [sb(f"tA{i}", [128, 256], F32) for i in range(2)]
        tB = [sb(f"tB{i}", [128, 256], F32) for i in range(2)]
        mg = [sb(f"mg{i}", [128, 8, 256], BF16) for i in range(2)]
        tsum = [sb(f"tsum{i}", [128, 1024], F32) for i in range(2)]
        h1 = [sb(f"h1_{i}", [128, 1024], F32) for i in range(2)]
        st, lb = sb("lnst", [128, 2, 6], F32)
        mv, _ = sb("lnmv", [128, 8], F32)
        def load_m(j):
            s = j % 2
            k.dma(att[s][0][:], S["ATT"][j * 256:(j + 1) * 256, :].rearrange("(t p) c -> p t c", p=128), writes=[att[s][1]])
            k.dma(ssmy[s][0][:], S["SSMY"][:, :, j * 256:(j + 1) * 256].rearrange("u p t -> p u t"), writes=[ssmy[s][1]])
            for u0 in range(0, 16, 4):
                k.dma(sgt[s][0][:, u0:u0 + 4, :], S["SG"][u0:u0 + 4, :, j * 256:(j + 1) * 256].rearrange("u p t -> p u t"),
                      writes=[sgt[s][1]])
            k.dma(xo[s][0][:], I["xcat"][(NCTX + j) * 256:(NCTX + j + 1) * 256, :].rearrange("(t p) f -> p t f", p=128),
                  writes=[xo[s][1]])

        load_m(0)
        for j in range(NOWN):
            s = j % 2
            if j + 1 < NOWN:
                load_m(j + 1)
            for tt in range(2):
                bt, bb = k.bank()
                bv = bt[:, :].bitcast(BF16)
                fns = [(lambda q=q: nc.tensor.transpose(out=bv[:, q * 128:(q + 1) * 128], in_=att[s][0][:, tt, q * 128:(q + 1) * 128],
                                                        identity=identb[:])) for q in range(4)]
                k.op("pe", fns, reads=[att[s][1], cb], writes=[bb])
                k.op("act", lambda bv=bv, tt=tt: nc.scalar.copy(attT[s][0][:, :, tt * 128:(tt + 1) * 128],
                                                                 bv[:, 0:512].rearrange("p (a b) -> p a b", b=128)),
                     reads=[bb], writes=[attT[s][1]])
            for m in range(8):
                bt, bb = k.bank()
                fa = [_mm(nc, bt[:, 0:256], wap[:, kt, m * 128:(m + 1) * 128], attT[s][0][:, kt, :], kt == 0, kt == 3) for kt in range(4)]
                k.op("pe", fa, reads=wapb + [attT[s][1]], writes=[bb])
                fb = [_mm(nc, bt[:, 256:512], wsp[:, kt, m * 128:(m + 1) * 128], ssmy[s][0][:, kt, :], kt == 0, kt == 3) for kt in range(4)]
                k.op("pe", fb, reads=wspb + [ssmy[s][1]], writes=[bb])
                a_t, a_b = tA[m % 2]
                b_t, b_b = tB[m % 2]
                k.op("dve", lambda a_t=a_t, bt=bt, m=m: nc.vector.tensor_tensor(out=a_t[:], in0=bt[:, 0:256], in1=sgt[s][0][:, m, :], op=ALU.mult),
                     reads=[bb, sgt[s][1]], writes=[a_b])
                k.op("dve", lambda b_t=b_t, bt=bt, m=m: nc.vector.tensor_tensor(out=b_t[:], in0=bt[:, 256:512], in1=sgt[s][0][:, 8 + m, :], op=ALU.mult),
                     reads=[bb, sgt[s][1]], writes=[b_b])
                k.op("pool", lambda a_t=a_t, b_t=b_t, m=m: nc.gpsimd.tensor_tensor(out=mg[s][0][:, m, :], in0=a_t[:], in1=b_t[:], op=ALU.add),
                     reads=[a_b, b_b], writes=[mg[s][1]])
            for tt in range(2):
                ts_t, ts_b = tsum[tt]
                for half in range(2):
                    bt, bb = k.bank()
                    fns = [_mm(nc, bt[:, :], mg[s][0][:, kt, tt * 128:(tt + 1) * 128], wout[:, kt, half * 512:(half + 1) * 512], kt == 0, kt == 7)
                           for kt in range(8)]
                    k.op("pe", fns, reads=woutb + [mg[s][1]], writes=[bb])
                    k.op("dve", lambda bt=bt, tt=tt, half=half, ts_t=ts_t: nc.vector.scalar_tensor_tensor(
                        out=ts_t[:, half * 512:(half + 1) * 512], in0=xo[s][0][:, tt, half * 512:(half + 1) * 512], scalar=ALPHA,
                        in1=bt[:, :], op0=ALU.mult, op1=ALU.add), reads=[bb, xo[s][1]], writes=[ts_b])
                h_t, h_b = h1[tt]
                ln_tile(k, ts_t, ts_b, g1, b1, gbuf, h_t[:, :], h_b, st, mv, lb)
                r0 = j * 256 + tt * 128
                k.dma(S["H1"][r0:r0 + 128, :], h_t[:], reads=[h_b])


def phase_ffn(k, I, S, SB, C, cfg, out):
    nc = k.nc
    NBLK, NCTX, NOWN = cfg["NBLK"], cfg["NCTX"], cfg["NOWN"]
    identf, cb = C["identf"], C["cb"]
    NF = 2 * FF // 128
    NP = FF // 128
    with ExitStack() as es:
        def sb(n, shp, dt):
            return es.enter_context(nc.sbuf_tensor(n, shp, dt)), Buf(n)
        wup, _ = sb("wup", [128, 8, 2 * FF], BF16)
        wdn, _ = sb("wdn", [128, NP, 1024], BF16)
        wupb = [Buf(f"wup{i}") for i in range(8)]
        wdnb = [Buf(f"wdn{i}") for i in range(NP)]
        load_weight_bf16(k, wup, wupb, I["w_up"], 8, 2 * FF, None, None, None)
        load_weight_bf16(k, wdn, wdnb, I["w_down"], NP, 1024, None, None, None)
        g2, gbuf = sb("g2", [128, 1024], F32)
        b2, _ = sb("b2", [128, 1024], F32)
        cw, _ = sb("cw", [128, NF, 3], F32)
        cbi, _ = sb("cbi", [128, NF], F32)
        hp, _ = sb("hp", [128, 1], F32)
        k.dma(g2[:], I["ln2_g"].partition_broadcast(128), writes=[gbuf])
        k.dma(b2[:], I["ln2_b"].partition_broadcast(128), writes=[gbuf])
        for jj in range(3):
            for f0 in range(0, NF, 11):
                k.dma(cw[:, f0:f0 + 11, jj], I["conv_w"][jj].rearrange("(f p) -> p f", p=128)[:, f0:f0 + 11], writes=[gbuf],
                      allow_slow_non_contiguous=True)
        for f0 in range(0, NF, 11):
            k.dma(cbi[:, f0:f0 + 11], I["conv_b"].rearrange("(f p) -> p f", p=128)[:, f0:f0 + 11], writes=[gbuf],
                  allow_slow_non_contiguous=True)
        k.dma(hp[:, :], I["has_prev"], writes=[gbuf])
        halo, halob = sb("halo", [128, NF, 2], F32)
        h1 = [sb(f"h1f{i}", [128, 2, 1024], F32) for i in range(2)]
        h1Ts = [sb(f"h1T{i}", [128, 8, 256], BF16) for i in range(2)]
        CVN = 3
        raw = [sb(f"raw{i}", [128, 2, 258], F32) for i in range(CVN)]
        cv = [sb(f"cv{i}", [128, 2, 256], F32) for i in range(CVN)]
        gl = [sb(f"gl{i}", [128, 256], F32) for i in range(2)]
        actT, _ = sb("actT", [128, NP, 256], BF16)
        actTb = [Buf(f"actT{p}") for p in range(NP)]
        ptmp = [sb(f"ptmp{i}", [128, 256], F32) for i in range(2)]
        tsums = [sb(f"tsumf{i}", [128, 1024], F32) for i in range(2)]
        o_, ob = sb("of", [128, 1024], F32)
        st, lb = sb("lnst2", [128, 2, 6], F32)
        mv, _ = sb("lnmv2", [128, 8], F32)
        upbanks = k.banks[0:4]
        accbanks = k.banks[4:8]
        ubi = [0]

        def upbank():
            ubi[0] += 1
            return upbanks[ubi[0] % 4]

        def load_h1(j):
            h_t, h_b = h1[j % 2]
            k.dma(h_t[:], S["H1"][j * 256:(j + 1) * 256, :].rearrange("(t p) f -> p t f", p=128), writes=[h_b])

        LAG = 9
        load_h1(0)
        if NOWN > 1:
            load_h1(1)

        def transposes(j):
            h_t, h_b = h1[j % 2]
            hT, hTb = h1Ts[j % 2]
            for tt in range(2):
                for g in range(2):
                    bt, bb = upbank()
                    fns = [(lambda q=q: nc.tensor.transpose(out=bt[:, q * 128:(q + 1) * 128],
                                                            in_=h_t[:, tt, (4 * g + q) * 128:(4 * g + q + 1) * 128],
                                                            identity=identf[:])) for q in range(4)]
                    k.op("pe", fns, reads=[h_b, cb], writes=[bb])
                    k.op("act", lambda bt=bt, g=g, tt=tt: nc.scalar.copy(hT[:, 4 * g:4 * g + 4, tt * 128:(tt + 1) * 128],
                                                                      bt[:, :].rearrange("p (a b) -> p a b", b=128)),
                         reads=[bb], writes=[hTb])

        transposes(0)
        for j in range(1):
            s = j % 2
            h_t, h_b = h1[s]
            h1T, h1Tb = h1Ts[s]
            if j == 0:
                bt, bb = upbank()
                for f in range(NF):
                    fns = [_mm(nc, bt[:, 2 * f:2 * f + 2], wup[:, kt, f * 128:(f + 1) * 128], h1T[:, kt, 254:256], kt == 0, kt == 7)
                           for kt in range(8)]
                    k.op("pe", fns, reads=wupb + [h1Tb], writes=[bb])
                k.op("dve", lambda bt=bt: nc.vector.tensor_scalar(halo[:, :, :].rearrange("p a b -> p (a b)"), bt[:, 0:2 * NF],
                                                                  hp[:, 0:1], None, op0=ALU.mult),
                     reads=[bb, gbuf], writes=[halob])
                if NOWN > 1:
                    transposes(1)
                if 2 < NOWN:
                    load_h1(2)
                continue

        G = (NOWN - 1) * NP
        halobs = [Buf(f"halo{p}") for p in range(NP)]
        for hb_ in halobs:
            hb_.w = halob.w

        def halo_in(gs):
            p = gs % NP
            r_t, r_b = raw[gs % CVN]
            k.op("pool", lambda: nc.gpsimd.tensor_copy(r_t[:, :, 0:2], halo[:, p::NP, :]), reads=[halobs[p]], writes=[r_b])

        def up_stage(j, p, gs):
            h1T, h1Tb = h1Ts[j % 2]
            bt, bb = upbank()
            for v, f in enumerate((p, NP + p)):
                fns = [_mm(nc, bt[:, v * 256:(v + 1) * 256], wup[:, kt, f * 128:(f + 1) * 128], h1T[:, kt, :], kt == 0, kt == 7)
                       for kt in range(8)]
                k.op("pe", fns, reads=wupb + [h1Tb], writes=[bb])
            r_t, r_b = raw[gs % CVN]
            c_t, c_b = cv[gs % CVN]
            if gs == 0:
                halo_in(gs)
            k.op("act", lambda: nc.scalar.copy(r_t[:, :, 2:258], bt[:, :].rearrange("p (a b) -> p a b", b=256)),
                 reads=[bb], writes=[r_b])
            for v, f in enumerate((p, NP + p)):
                k.op("act", lambda v=v, f=f: nc.scalar.activation(out=c_t[:, v, :], in_=bt[:, v * 256:(v + 1) * 256],
                                                                  func=AF.Identity, scale=cw[:, f, 2:3], bias=cbi[:, f:f + 1]),
                     reads=[bb, gbuf], writes=[c_b])
            k.op("pool", lambda: nc.gpsimd.tensor_copy(halo[:, p::NP, :], r_t[:, :, 256:258]), reads=[r_b], writes=[halobs[p]])
            if gs + 1 < G:
                halo_in(gs + 1)
            for jj in (1, 0):
                k.op("dve", lambda jj=jj: nc.vector.scalar_tensor_tensor(
                    out=c_t[:, 0, :], in0=r_t[:, 0, jj:jj + 256], scalar=cw[:, p, jj:jj + 1], in1=c_t[:, 0, :],
                    op0=ALU.mult, op1=ALU.add), reads=[r_b, gbuf, c_b], writes=[c_b])
            t_t, t_b = ptmp[gs % 2]
            k.op("pool", lambda: nc.gpsimd.tensor_scalar(
                t_t[:], r_t[:, 1, 1:257], cw[:, NP + p, 1:2], 0.0, op0=ALU.mult, op1=ALU.add),
                reads=[r_b, gbuf], writes=[t_b])
            k.op("pool", lambda: nc.gpsimd.tensor_tensor(out=c_t[:, 1, :], in0=c_t[:, 1, :], in1=t_t[:], op=ALU.add),
                 reads=[t_b, c_b], writes=[c_b])
            k.op("dve", lambda: nc.vector.scalar_tensor_tensor(
                out=c_t[:, 1, :], in0=r_t[:, 1, 0:256], scalar=cw[:, NP + p, 0:1], in1=c_t[:, 1, :],
                op0=ALU.mult, op1=ALU.add), reads=[r_b, gbuf, c_b], writes=[c_b])

        def up_stage_b(j, p, gs):
            c_t, c_b = cv[gs % CVN]
            g_t, g_b = gl[gs % 2]
            k.op("act", lambda: nc.scalar.activation(out=g_t[:], in_=c_t[:, 1, :], func=AF.Gelu_apprx_tanh),
                 reads=[c_b], writes=[g_b])
            k.op("dve", lambda: nc.vector.tensor_tensor(out=actT[:, p, :], in0=g_t[:], in1=c_t[:, 0, :], op=ALU.mult),
                 reads=[g_b, c_b], writes=[actTb[p]])

        def down_stage(j, p):
            fns = []
            for tt in range(2):
                for half in range(2):
                    bt, bb = accbanks[tt * 2 + half]
                    fns.append(_mm(nc, bt[:, :], actT[:, p, tt * 128:(tt + 1) * 128], wdn[:, p, half * 512:(half + 1) * 512],
                                   p == 0, p == NP - 1))
            k.op("pe", fns, reads=[wdnb[p], actTb[p]], writes=[b_[1] for b_ in accbanks])
            if p == NP - 1:
                tail(j)

        def tail(j):
            h_t, h_b = h1[j % 2]
            for tt in range(2):
                ts_t, ts_b = tsums[tt]
                for half in range(2):
                    bt, bb = accbanks[tt * 2 + half]
                    k.op("dve", lambda bt=bt, tt=tt, half=half, ts_t=ts_t: nc.vector.scalar_tensor_tensor(
                        out=ts_t[:, half * 512:(half + 1) * 512], in0=h_t[:, tt, half * 512:(half + 1) * 512], scalar=ALPHA,
                        in1=bt[:, :], op0=ALU.mult, op1=ALU.add), reads=[bb, h_b], writes=[ts_b])
            for tt in range(2):
                ts_t, ts_b = tsums[tt]
                ln_tile(k, ts_t, ts_b, g2, b2, gbuf, o_[:, :], ob, st, mv, lb)
                r0 = (j - 1) * 256 + tt * 128
                k.dma(out[r0:r0 + 128, :], o_[:], reads=[ob])
            if j + 2 < NOWN:
                load_h1(j + 2)

        for gs in range(G + LAG + 2):
            if gs < G:
                j, p = 1 + gs // NP, gs % NP
                if p == NP - 8 and j + 1 < NOWN:
                    transposes(j + 1)
                up_stage(j, p, gs)
            g1_ = gs - 1
            if 0 <= g1_ < G:
                up_stage_b(1 + g1_ // NP, g1_ % NP, g1_)
            g2_ = gs - 1 - LAG
            if 0 <= g2_ < G:
                down_stage(1 + g2_ // NP, g2_ % NP)


_CACHE = {}


def _consts(NBLK, NCTX, half, full):
    T = NBLK * 256
    NOWN = NBLK - NCTX
    off = 0 if (half == 1 or not full) else -(NBLK // 2) * 256
    if not full:
        off = 0
    pos = (np.arange(T, dtype=np.float64) + off)
    inv = 500000.0 ** (-np.arange(0, 16, 2, dtype=np.float64) / 16.0)
    ang = pos[None, :] * inv[:, None]
    cos = np.ones((64, T)); sin = np.zeros((64, T))
    cos[0:8] = np.cos(ang); cos[8:16] = np.cos(ang)
    sin[0:8] = np.sin(ang); sin[8:16] = np.sin(ang)
    rot_cos = np.concatenate([cos, cos], 0).astype(np.float32)
    rot_sin = np.concatenate([sin, sin], 0).astype(np.float32)
    first_valid = 0 if (half == 1 or not full) else NBLK // 2
    gb = np.full((NOWN, 32), -1e30, np.float32)
    for j in range(NOWN):
        own = NCTX + j
        for n in range(min(own, 32)):
            if n >= first_valid:
                gb[j, n] = 0.0
    gbias = np.ascontiguousarray(np.broadcast_to(np.tile(gb[:, None, :], (1, 16, 1)).reshape(NOWN, 1, 512), (NOWN, 128, 512)))
    hasp = np.full((128, 1), 1.0 if (half == 1 or not full) else 0.0, np.float32)
    key = np.arange(128)[:, None]
    q = np.arange(256)[None, :]
    cm = np.concatenate([(key <= q), (key + 128 <= q)], 1).astype(np.float32)
    blk = np.zeros((32, T), np.float32)
    for n in range(min(NBLK, 32)):
        blk[n, n * 256:(n + 1) * 256] = 1.0
    return dict(rot_cos=rot_cos, rot_sin=rot_sin, gbias=gbias, has_prev=hasp,
                ident=np.eye(128, dtype=np.float32), cmask=cm.astype(ml_dtypes.bfloat16), blkind=blk.astype(ml_dtypes.bfloat16))


def kernel(**inputs):
    x = np.asarray(inputs["x"], np.float32)
    B, SEQ, _ = x.shape
    NBLK, NCTX = 32, 15
    if "nc" not in _CACHE:
        nc = bass.Bass("TRN2", target_bir_lowering=False)
        build(nc, NBLK, NCTX)
        _CACHE["nc"] = nc
    nc = _CACHE["nc"]
    wnames = ["w_in", "w_attn_proj", "ssm_a_re", "ssm_a_im", "ssm_log_dt", "ssm_b_re", "ssm_b_im", "ssm_c_re", "ssm_c_im",
              "ssm_d", "w_glu", "w_ssm_proj", "w_out", "ln1_g", "ln1_b", "w_up", "conv_w", "conv_b", "w_down", "ln2_g", "ln2_b"]
    w = {n: np.ascontiguousarray(np.asarray(inputs[n], np.float32)[0]) for n in wnames}
    in_maps = []
    for c in range(8):
        b, half = c // 2, c % 2
        if half == 1:
            xcat = np.ascontiguousarray(x[b])
        else:
            xcat = np.concatenate([np.zeros((SEQ // 2, D), np.float32), x[b, :SEQ // 2]], 0)
        m = dict(w)
        m["xcat"] = xcat
        m.update(_consts(NBLK, NCTX, half, True))
        in_maps.append(m)
    res = run_bass_kernel_spmd(nc, in_maps, core_ids=list(range(8)))
    outp = np.empty((B, SEQ, D), np.float32)
    for c in range(8):
        b, half = c // 2, c % 2
        outp[b, half * (SEQ // 2):(half + 1) * (SEQ // 2)] = res.results[c]["out"]
    return outp
```

```python
import math
from contextlib import ExitStack

import numpy as np
import ml_dtypes
import concourse.bass as bass
import concourse.mybir as mybir
from concourse.bass_utils import run_bass_kernel_spmd

F32 = mybir.dt.float32
BF16 = mybir.dt.bfloat16
AF = mybir.ActivationFunctionType
ALU = mybir.AluOpType
AX = mybir.AxisListType

D = 1024
NH = 8
DH = 64
FF = 2816
ALPHA = 2.0 ** 0.25
LN_EPS = 1e-5
GC = 0.7978845608028654


class Buf:
    __slots__ = ("name", "w", "r", "excl")

    def __init__(self, name, excl=False):
        self.name = name
        self.w = None
        self.r = []
        self.excl = excl


class KB:
    def __init__(self, nc, es):
        self.nc = nc
        self.es = es
        self.engs = {"pe": nc.tensor, "act": nc.scalar, "dve": nc.vector, "pool": nc.gpsimd, "sp": nc.sync}
        self.cur = {}
        self.waited = {e: {} for e in self.engs}
        self.nsem = 0
        self.rings = {}
        self.ridx = {}
        self.last = {}
        self.banks = []
        self.bank_i = 0
        self.ninst = 0

    def newsem(self, name):
        self.nsem += 1
        return self.es.enter_context(self.nc.semaphore(f"{name}{self.nsem}"))

    def _wait(self, e, tok):
        sem, val, _ = tok
        w = self.waited[e]
        key = id(sem)
        if w.get(key, 0) >= val:
            return
        self.engs[e].wait_ge(sem, val)
        w[key] = val

    def _deps(self, e, reads, writes):
        deps = []
        for b in reads:
            if b.w is not None:
                deps.append(b.w)
            if b.excl:
                deps.extend(b.r)
        for b in writes:
            if b.w is not None:
                deps.append(b.w)
            deps.extend(b.r)
        for t in deps:
            if e == "pe" and t[2] == "pe":
                continue
            self._wait(e, t)

    def _upd(self, tok, reads, writes):
        for b in reads:
            if b.excl:
                b.w = tok
                b.r = []
            else:
                b.r.append(tok)
        for b in writes:
            b.w = tok
            b.r = []

    def op(self, e, fns, reads=(), writes=()):
        if callable(fns):
            fns = [fns]
        self._deps(e, reads, writes)
        ins = None
        for f in fns:
            ins = f()
            self.ninst += 1
        c = self.cur.get(e)
        if c is None or c[1] >= 30000:
            c = [self.newsem("s" + e), 0]
            self.cur[e] = c
        c[1] += 1
        ins.then_inc(c[0], 1)
        tok = (c[0], c[1], e)
        self.last[e] = tok
        self._upd(tok, reads, writes)
        return tok

    def dma(self, out, in_, reads=(), writes=(), q="sp", **kw):
        ring = self.rings.get(q)
        if ring is None:
            ring = [[self.newsem("d" + q), 0] for _ in range(8)]
            self.rings[q] = ring
        i = self.ridx.get(q, 0)
        self.ridx[q] = i + 1
        slot = ring[i % 8]
        if slot[1] > 0:
            self._wait(q, (slot[0], slot[1], "dma"))
        if slot[1] >= 30000:
            slot[0] = self.newsem("d" + q)
            slot[1] = 0
        self._deps(q, reads, writes)
        ins = self.engs[q].dma_start(out=out, in_=in_, **kw)
        self.ninst += 1
        slot[1] += 16
        ins.then_inc(slot[0], 16)
        tok = (slot[0], slot[1], "dma")
        self._upd(tok, reads, writes)
        return tok

    def barrier(self):
        toks = list(self.last.values())
        for q, ring in self.rings.items():
            for s in ring:
                if s[1] > 0:
                    toks.append((s[0], s[1], "dma"))
        for e in self.engs:
            for t in toks:
                self._wait(e, t)

    def bank(self):
        b = self.banks[self.bank_i % len(self.banks)]
        self.bank_i += 1
        return b


def _mm(nc, out, lhsT, rhs, start, stop):
    return lambda: nc.tensor.matmul(out, lhsT, rhs, start=start, stop=stop)


def build(nc, NBLK=32, NCTX=15, dbg=(), phases=("proj", "attn", "ssm", "merge", "ffn")):
    NOWN = NBLK - NCTX
    T = NBLK * 256
    TO = NOWN * 256
    NOUT = (NOWN - 1) * 256

    def din(name, shape, dt=F32):
        return nc.dram_tensor(name, list(shape), dt, kind="ExternalInput").ap()

    def dscr(name, shape, dt):
        kind = "ExternalOutput" if name in dbg else "Internal"
        return nc.dram_tensor(name, list(shape), dt, kind=kind).ap()

    I = dict(
        xcat=din("xcat", [T, D]), w_in=din("w_in", [D, 4096]), w_attn_proj=din("w_attn_proj", [512, D]),
        ssm_a_re=din("ssm_a_re", [32, 64]), ssm_a_im=din("ssm_a_im", [32, 64]), ssm_log_dt=din("ssm_log_dt", [32]),
        ssm_b_re=din("ssm_b_re", [32, 64, 16]), ssm_b_im=din("ssm_b_im", [32, 64, 16]),
        ssm_c_re=din("ssm_c_re", [32, 16, 64]), ssm_c_im=din("ssm_c_im", [32, 16, 64]),
        ssm_d=din("ssm_d", [32, 16]), w_glu=din("w_glu", [512, 512]), w_ssm_proj=din("w_ssm_proj", [512, D]),
        w_out=din("w_out", [D, D]), ln1_g=din("ln1_g", [D]), ln1_b=din("ln1_b", [D]),
        w_up=din("w_up", [D, 2 * FF]), conv_w=din("conv_w", [3, 2 * FF]), conv_b=din("conv_b", [2 * FF]),
        w_down=din("w_down", [FF, D]), ln2_g=din("ln2_g", [D]), ln2_b=din("ln2_b", [D]),
        rot_cos=din("rot_cos", [128, T]), rot_sin=din("rot_sin", [128, T]),
        gbias=din("gbias", [NOWN, 128, 512]), has_prev=din("has_prev", [128, 1]),
        ident=din("ident", [128, 128]), cmask=din("cmask", [128, 512], BF16), blkind=din("blkind", [32, T], BF16),
    )
    out = nc.dram_tensor("out", [NOUT, D], F32, kind="ExternalOutput").ap()
    S = dict(
        KT=dscr("KT", [4, 128, T], BF16), QT=dscr("QT", [4, 128, TO], BF16),
        VP=dscr("VP", [T, NH, 65], BF16), UT=dscr("UT", [4, 128, T], BF16),
        SG=dscr("SG", [16, 128, TO], BF16), SEL=dscr("SEL", [NH, 128, NOWN * 64], F32),
        ATT=dscr("ATT", [TO, 512], BF16), SSMY=dscr("SSMY", [4, 128, TO], BF16),
        H1=dscr("H1", [TO, D], F32), SBT=dscr("SBT", [NH, 32, TO], BF16),
    )
    SB = {k: Buf("scr_" + k) for k in S}
    cfg = dict(NBLK=NBLK, NCTX=NCTX, NOWN=NOWN, T=T, TO=TO, NOUT=NOUT)

    with ExitStack() as es:
        k = KB(nc, es)
        for i in range(8):
            t = es.enter_context(nc.psum_tensor(f"bank{i}", [128, 512], F32))
            k.banks.append((t, Buf(f"bank{i}", excl=True)))
        identf = es.enter_context(nc.sbuf_tensor("identf", [128, 128], F32))
        identb = es.enter_context(nc.sbuf_tensor("identb", [128, 128], BF16))
        cb = Buf("consts")
        k.dma(identf[:], I["ident"], writes=[cb])
        k.op("dve", lambda: nc.vector.tensor_copy(identb[:], identf[:]), reads=[cb], writes=[cb])
        C = dict(identf=identf, identb=identb, cb=cb)
        with ExitStack() as esx:
            pre = None
            if "ssm" in phases and "proj" in phases:
                pass
            with ExitStack() as esw:
                def load_win():
                    win = esw.enter_context(nc.sbuf_tensor("win", [128, 8, 4096], BF16))
                    winb = [Buf(f"win{kt}") for kt in range(8)]
                    C["win"], C["winb"] = win, winb
                    load_weight_bf16(k, win, winb, I["w_in"], 8, 4096, None, None, None)
                if "ssm" in phases:
                    pre = ssm_pre(k, I, S, SB, C, cfg, esx, after_alloc=(load_win if "proj" in phases else None))
                    k.barrier()
                elif "proj" in phases:
                    load_win()
                if "proj" in phases:
                    phase_proj(k, I, S, SB, C, cfg)
                    k.barrier()
            if "attn" in phases or "ssm" in phases:
                gens = []
                if "attn" in phases:
                    n_att = NH * sum(NCTX + j + 1 for j in range(NOWN)) + 2
                    gens.append([attn_gen(k, I, S, SB, C, cfg, esx, k.banks[0:3], k.banks[3:5]), n_att, 0])
                if "ssm" in phases:
                    gens.append([ssm_gen(k, I, S, SB, C, cfg, pre, esx, k.banks[5:8]), NBLK * 16 + 12, 0])
                while gens:
                    g = min(gens, key=lambda t: t[2] / t[1])
                    try:
                        next(g[0])
                        g[2] += 1
                    except StopIteration:
                        gens.remove(g)
                k.barrier()
        if "merge" in phases:
            phase_merge(k, I, S, SB, C, cfg)
            k.barrier()
        if "ffn" in phases:
            phase_ffn(k, I, S, SB, C, cfg, out)
            k.barrier()
    return nc


def load_weight_bf16(k, dst, dst_bufs, src, nkt, ncols, stage, stage_bufs, cnt, rows=128):
    CH = 2048
    for kt in range(nkt):
        for c0 in range(0, ncols, CH):
            w = min(CH, ncols - c0)
            k.dma(dst[0:rows, kt, c0:c0 + w], src[kt * rows:(kt + 1) * rows, c0:c0 + w], writes=[dst_bufs[kt]], q="pool")


def phase_proj(k, I, S, SB, C, cfg):
    nc = k.nc
    NBLK, NCTX, NOWN = cfg["NBLK"], cfg["NCTX"], cfg["NOWN"]
    identf, cb = C["identf"], C["cb"]
    with ExitStack() as es:
        def sb(n, shp, dt):
            return es.enter_context(nc.sbuf_tensor(n, shp, dt)), Buf(n)
        win, winb = C["win"], C["winb"]
        wsw, wswb = sb("wsw", [128, 8, 1024], BF16)
        k.op("pool", lambda: nc.gpsimd.memset(wsw[:], 0.0), writes=[wswb])
        for kt in range(8):
            src = win[:, kt, 0:1024].rearrange("p (h d) -> p h d", d=64)
            dst = wsw[:, kt, :].rearrange("p (h d) -> p h d", d=64)
            k.op("dve", lambda dst=dst, src=src: nc.vector.tensor_scalar_mul(dst[:, :, 0:8], src[:, :, 8:16], -1.0),
                 reads=[winb[kt]], writes=[wswb])
            k.op("dve", lambda dst=dst, src=src: nc.vector.tensor_copy(dst[:, :, 8:16], src[:, :, 0:8]),
                 reads=[winb[kt]], writes=[wswb])
        xt = [sb(f"xt{i}", [128, 2, 1024], F32) for i in range(2)]
        xT = [sb(f"xT{i}", [128, 8, 256], BF16) for i in range(2)]
        rc = [sb(f"rc{i}", [128, 256], F32) for i in range(2)]
        rs = [sb(f"rs{i}", [128, 256], F32) for i in range(2)]
        t1 = [sb(f"t1_{i}", [128, 256], F32) for i in range(2)]
        t2 = [sb(f"t2_{i}", [128, 256], F32) for i in range(2)]
        rotk = [sb(f"rotk{i}", [128, 256], F32) for i in range(2)]
        rotq = [sb(f"rotq{i}", [128, 256], F32) for i in range(8)]
        kb16 = [sb(f"kb16_{i}", [128, 256], BF16) for i in range(2)]
        ub = [sb(f"ub{i}", [128, 2, 256], BF16) for i in range(2)]
        vp = [sb(f"vp{i}", [128, 8, 65], BF16) for i in range(2)]
        kmeanT, kmb = sb("kmeanT", [128, 4, 64], F32)
        kms, kmsb = sb("kms", [128, 1], F32)
        gb, gbb = sb("gb", [128, 512], F32)
        sc, scb = sb("sc", [128, 512], F32)
        m8, m8b = sb("m8", [128, 16, 8], F32)
        selt, seltb = sb("selt", [128, 512], F32)
        vm, vmb = sb("vm", [128, 512], F32)
        sbT, sbTb = sb("sbT", [32, 16, 128], BF16)
        k.op("dve", lambda: nc.vector.memset(kmeanT[:], 0.0), writes=[kmb])
        for v_, vb_ in vp:
            k.op("pool", lambda v_=v_: nc.gpsimd.memset(v_[:], 1.0), writes=[vb_])
        ctr = [0]

        def evac_eng():
            ctr[0] += 1
            return "act" if ctr[0] % 2 else "dve"

        def copy_op(e, o, i_):
            if e == "act":
                return lambda: nc.scalar.copy(o, i_)
            if e == "dve":
                return lambda: nc.vector.tensor_copy(o, i_)
            return lambda: nc.gpsimd.tensor_copy(o, i_)

        def fm_group(region, W, wb, col0, xTt, xTb, bankb):
            fns = [_mm(nc, region, W[:, kt, col0:col0 + 128], xTt[:, kt, :], kt == 0, kt == 7) for kt in range(8)]
            k.op("pe", fns, reads=list(wb) + [xTb], writes=[bankb])

        import os
        STOP = int(os.environ.get("PROJ_STOP", "9"))

        def gating_a(i):
            j = i - NCTX
            rq = [rotq[(i % 2) * 4 + p] for p in range(4)]
            bt, bb = k.bank()
            fns = []
            for p in range(4):
                for qs in range(2):
                    idx = p * 2 + qs
                    fns.append(_mm(nc, bt[:, idx * 64:(idx + 1) * 64], rq[p][0][:, qs * 128:(qs + 1) * 128],
                                   kmeanT[:, p, :], True, True))
            k.op("pe", fns, reads=[rq[p][1] for p in range(4)] + [kmb], writes=[bb])
            k.dma(gb[:], I["gbias"][j], writes=[gbb])
            k.op("dve", lambda: nc.vector.tensor_tensor(out=sc[:], in0=bt[:, :], in1=gb[:], op=ALU.add),
                 reads=[bb, gbb], writes=[scb])
            for idx in range(16):
                k.op("dve", lambda idx=idx: nc.vector.max(out=m8[:, idx, :], in_=sc[:, idx * 32:(idx + 1) * 32]),
                     reads=[scb], writes=[m8b])
            sc3 = sc[:, :].rearrange("p (a b) -> p a b", b=32)
            sel3 = selt[:, :].rearrange("p (a b) -> p a b", b=32)
            k.op("dve", lambda: nc.vector.tensor_tensor(out=sel3, in0=sc3, in1=m8[:, :, 2:3].to_broadcast([128, 16, 32]),
                                                        op=ALU.is_ge), reads=[scb, m8b], writes=[seltb])
            k.op("dve", lambda: nc.vector.tensor_scalar(vm[:], gb[:], -1.0, None, op0=ALU.is_ge), reads=[gbb], writes=[vmb])
            k.op("dve", lambda: nc.vector.tensor_tensor(out=selt[:], in0=selt[:], in1=vm[:], op=ALU.mult),
                 reads=[vmb, seltb], writes=[seltb])
            k.op("dve", lambda: nc.vector.tensor_scalar(selt[:], selt[:], 30000.0, -30000.0, op0=ALU.mult, op1=ALU.add),
                 reads=[seltb], writes=[seltb])
            k.op("dve", lambda: nc.vector.memset(sel3[:, :, i:i + 1], 0.0), reads=[seltb], writes=[seltb])

        def gating_b(i):
            j = i - NCTX
            for g in range(4):
                bt2, bb2 = k.bank()
                fns = [(lambda q=q: nc.tensor.transpose(out=bt2[0:32, q * 128:(q + 1) * 128],
                                                        in_=selt[:, (4 * g + q) * 32:(4 * g + q + 1) * 32],
                                                        identity=identf[:])) for q in range(4)]
                k.op("pe", fns, reads=[seltb, cb], writes=[bb2])
                k.op("act", lambda bt2=bt2, g=g: nc.scalar.copy(sbT[:, 4 * g:4 * g + 4, :],
                                                              bt2[0:32, :].rearrange("p (a b) -> p a b", b=128)),
                     reads=[bb2], writes=[sbTb])
            for p in range(4):
                for hh in range(2):
                    k.dma(S["SBT"][2 * p + hh, :, j * 256:(j + 1) * 256].rearrange("n (q t) -> n q t", q=2),
                          sbT[:, 4 * p + hh:4 * p + hh + 3:2, :], reads=[sbTb])
        for i in range(NBLK if STOP > 0 else 0):
            own = i >= NCTX
            j = i - NCTX
            s = i % 2
            xt_t, xt_b = xt[s]
            xT_t, xT_b = xT[s]

            def load_blk(i2):
                s2 = i2 % 2
                k.dma(xt[s2][0][:], I["xcat"][i2 * 256:(i2 + 1) * 256, :].rearrange("(t p) f -> p t f", p=128), writes=[xt[s2][1]])
                k.dma(rc[s2][0][:], I["rot_cos"][:, i2 * 256:(i2 + 1) * 256], writes=[rc[s2][1]])
                k.dma(rs[s2][0][:], I["rot_sin"][:, i2 * 256:(i2 + 1) * 256], writes=[rs[s2][1]])

            if i == 0:
                load_blk(0)
            if i + 1 < NBLK:
                load_blk(i + 1)
            for tt in range(2):
                for g in range(2):
                    bt, bb = k.bank()
                    fns = [(lambda q=q: nc.tensor.transpose(out=bt[:, q * 128:(q + 1) * 128],
                                                            in_=xt_t[:, tt, (4 * g + q) * 128:(4 * g + q + 1) * 128],
                                                            identity=identf[:])) for q in range(4)]
                    k.op("pe", fns, reads=[xt_b, cb], writes=[bb])
                    e = evac_eng()
                    k.op(e, copy_op(e, xT_t[:, 4 * g:4 * g + 4, tt * 128:(tt + 1) * 128],
                                    bt[:, :].rearrange("p (a b) -> p a b", b=128)), reads=[bb], writes=[xT_b])
            if STOP < 2:
                continue
            rot_jobs = [("k", p) for p in range(4)] + ([("q", p) for p in range(4)] if own else [])
            for n_, (kind, p) in enumerate(rot_jobs):
                col0 = (512 if kind == "k" else 0) + p * 128
                bt, bb = k.bank()
                fm_group(bt[:, 0:256], win, winb, col0, xT_t, xT_b, bb)
                fm_group(bt[:, 256:512], wsw, [wswb], col0, xT_t, xT_b, bb)
                a1, a1b = t1[n_ % 2]
                a2, a2b = t2[n_ % 2]
                k.op("dve", lambda a1=a1, bt=bt: nc.vector.tensor_tensor(out=a1[:], in0=bt[:, 0:256], in1=rc[s][0][:], op=ALU.mult),
                     reads=[bb, rc[s][1]], writes=[a1b])
                k.op("dve", lambda a2=a2, bt=bt: nc.vector.tensor_tensor(out=a2[:], in0=bt[:, 256:512], in1=rs[s][0][:], op=ALU.mult),
                     reads=[bb, rs[s][1]], writes=[a2b])
                ro, rob = rotk[p % 2] if kind == "k" else rotq[(i % 2) * 4 + p]
                k.op("pool", lambda ro=ro, a1=a1, a2=a2: nc.gpsimd.tensor_tensor(out=ro[:], in0=a1[:], in1=a2[:], op=ALU.add),
                     reads=[a1b, a2b], writes=[rob])
                o16, o16b = kb16[n_ % 2]
                if kind == "k":
                    k.op("act", lambda o16=o16, ro=ro: nc.scalar.copy(o16[:], ro[:]), reads=[rob], writes=[o16b])
                    k.dma(S["KT"][p, :, i * 256:(i + 1) * 256], o16[:], reads=[o16b])
                    k.op("dve", lambda ro=ro: nc.vector.reduce_sum(out=kms[:], in_=ro[:], axis=AX.X), reads=[rob], writes=[kmsb])
                    k.op("dve", lambda p=p: nc.vector.tensor_scalar_mul(kmeanT[0:64, p, i:i + 1], kms[0:64, :], 1.0 / 256.0),
                         reads=[kmsb], writes=[kmb])
                    k.op("dve", lambda p=p: nc.vector.tensor_scalar_mul(kmeanT[64:128, p, 32 + i:32 + i + 1], kms[64:128, :], 1.0 / 256.0),
                         reads=[kmsb], writes=[kmb])
                else:
                    k.op("act", lambda o16=o16, ro=ro: nc.scalar.mul(o16[:], ro[:], 0.125), reads=[rob], writes=[o16b])
                    k.dma(S["QT"][p, :, j * 256:(j + 1) * 256], o16[:], reads=[o16b])
            if STOP >= 5 and i - 1 >= NCTX:
                gating_a(i - 1)
            if STOP < 3:
                continue
            plain = [("u", 0), ("u", 1)] + ([("g", g) for g in range(8)] if own else [])
            for n_, (kind, g) in enumerate(plain):
                col0 = (1536 + g * 256) if kind == "u" else (2048 + g * 256)
                bt, bb = k.bank()
                fm_group(bt[:, 0:256], win, winb, col0, xT_t, xT_b, bb)
                fm_group(bt[:, 256:512], win, winb, col0 + 128, xT_t, xT_b, bb)
                o, ob = ub[n_ % 2]
                ov = o[:, :, :].rearrange("p a b -> p (a b)")
                if kind == "u":
                    k.op("act", lambda ov=ov, bt=bt: nc.scalar.copy(ov, bt[:, :]), reads=[bb], writes=[ob])
                    k.dma(S["UT"][2 * g:2 * g + 2, :, i * 256:(i + 1) * 256].rearrange("u p t -> p u t"), o[:], reads=[ob])
                else:
                    k.op("act", lambda ov=ov, bt=bt: nc.scalar.activation(out=ov, in_=bt[:, :], func=AF.Sigmoid),
                         reads=[bb], writes=[ob])
                    k.dma(S["SG"][2 * g:2 * g + 2, :, j * 256:(j + 1) * 256].rearrange("u p t -> p u t"), o[:], reads=[ob])
            if STOP < 4:
                continue
            for tt in range(2):
                bt, bb = k.bank()
                fns = [_mm(nc, bt[:, :], xT_t[:, kt, tt * 128:(tt + 1) * 128], win[:, kt, 1024:1536], kt == 0, kt == 7)
                       for kt in range(8)]
                k.op("pe", fns, reads=winb + [xT_b], writes=[bb])
                v_, vb_ = vp[tt]
                e = evac_eng()
                k.op(e, copy_op(e, v_[:, :, 0:64], bt[:, :].rearrange("p (h d) -> p h d", d=64)), reads=[bb], writes=[vb_])
                r0 = (i * 2 + tt) * 128
                k.dma(S["VP"][r0:r0 + 128, :, :], v_[:], reads=[vb_])
            if own and STOP >= 5 and i - 1 >= NCTX:
                gating_b(i - 1)
        if STOP >= 5:
            gating_a(NBLK - 1)
            gating_b(NBLK - 1)


def attn_gen(k, I, S, SB, C, cfg, es, sbanks, obanks):
    nc = k.nc
    NBLK, NCTX, NOWN, T, TO = cfg["NBLK"], cfg["NCTX"], cfg["NOWN"], cfg["T"], cfg["TO"]
    if True:
        def sb(n, shp, dt):
            return es.enter_context(nc.sbuf_tensor(n, shp, dt)), Buf(n)
        ktsb = [sb(f"ktsb{i}", [96, T], BF16) for i in range(1)]
        qtsb = [sb(f"qtsb{i}", [96, TO], BF16) for i in range(1)]
        vpsb = [sb(f"vpsb{i}", [128, NBLK * 2, 65], BF16) for i in range(2)]
        cm, cmb = sb("cm", [128, 512], BF16)
        pT = [sb(f"pT{i}", [128, 512], BF16) for i in range(4)]
        rcp, rcpb = sb("rcp", [128, 2], F32)
        ostg = [sb(f"ostg{i}", [128, 2, 64], BF16) for i in range(2)]
        k.dma(cm[:], I["cmask"], writes=[cmb])
        for kt_t, kt_b in ktsb:
            k.dma(kt_t[64:96, :], I["blkind"], writes=[kt_b])
        NSB = len(sbanks)
        LA = 2
        its = [(h, j, n) for h in range(NH) for j in range(NOWN) for n in range(NCTX + j + 1)]
        hbuf = {}
        jcount = [0]
        jslot = {}

        def load_head(h):
            s = 0
            p, hh = h // 2, h % 2
            kt_t, kt_b = ktsb[s]
            qt_t, qt_b = qtsb[s]
            vp_t, vp_b = vpsb[h % 2]
            k.dma(kt_t[0:64, :], S["KT"][p, hh * 64:(hh + 1) * 64, :], writes=[kt_b])
            k.dma(qt_t[0:64, :], S["QT"][p, hh * 64:(hh + 1) * 64, :], writes=[qt_b])
            k.dma(qt_t[64:96, :], S["SBT"][h], writes=[qt_b])
            vsrc = S["VP"][:, h, :].rearrange("(t p) c -> p t c", p=128)
            for t0 in range(0, NBLK * 2, 8):
                k.dma(vp_t[:, t0:t0 + 8, :], vsrc[:, t0:t0 + 8, :], writes=[vp_b])
            hbuf[h] = (kt_t, kt_b, qt_t, qt_b, vp_t, vp_b)

        def emit_front(idx):
            h, j, n = its[idx]
            if h not in hbuf:
                load_head(h)
            kt_t, kt_b, qt_t, qt_b, vp_t, vp_b = hbuf[h]
            diag = NCTX + j
            bs, bsb = sbanks[idx % NSB]
            fns = [_mm(nc, bs[:, kt * 256:(kt + 1) * 256], kt_t[:, n * 256 + kt * 128:n * 256 + (kt + 1) * 128],
                       qt_t[:, j * 256:(j + 1) * 256], True, True) for kt in range(2)]
            k.op("pe", fns, reads=[kt_b, qt_b], writes=[bsb])
            pt_t, pt_b = pT[idx % 4]
            k.op("act", lambda: nc.scalar.activation(out=pt_t[:], in_=bs[:, :], func=AF.Exp), reads=[bsb], writes=[pt_b])
            if n == diag:
                k.op("pool", lambda: nc.gpsimd.tensor_tensor(out=pt_t[:], in0=pt_t[:], in1=cm[:], op=ALU.mult),
                     reads=[cmb, pt_b], writes=[pt_b])

        def emit_back(idx):
            h, j, n = its[idx]
            kt_t, kt_b, qt_t, qt_b, vp_t, vp_b = hbuf[h]
            diag = NCTX + j
            pt_t, pt_b = pT[idx % 4]
            if n == 0:
                jslot[(h, j)] = jcount[0] % (len(obanks) // 2)
                jcount[0] += 1
            sl = jslot[(h, j)]
            fns = []
            for qs in range(2):
                bo, bob = obanks[sl * 2 + qs]
                fns += [_mm(nc, bo[:, 0:65], pt_t[:, kt * 256 + qs * 128:kt * 256 + (qs + 1) * 128], vp_t[:, n * 2 + kt, :],
                            n == 0 and kt == 0, n == diag and kt == 1) for kt in range(2)]
            k.op("pe", fns, reads=[pt_b, vp_b], writes=[obanks[sl * 2][1], obanks[sl * 2 + 1][1]])
            if n == diag:
                og_t, og_b = ostg[jcount[0] % 2]
                for qs in range(2):
                    bo, bob = obanks[sl * 2 + qs]
                    k.op("dve", lambda bo=bo, qs=qs: nc.vector.reciprocal(rcp[:, qs:qs + 1], bo[:, 64:65]), reads=[bob], writes=[rcpb])
                    k.op("dve", lambda bo=bo, qs=qs: nc.vector.tensor_scalar(
                        og_t[:, qs, :], bo[:, 0:64], rcp[:, qs:qs + 1], None, op0=ALU.mult),
                        reads=[bob, rcpb], writes=[og_b])
                k.dma(S["ATT"][j * 256:(j + 1) * 256, h * 64:(h + 1) * 64].rearrange("(q p) c -> p q c", p=128), og_t[:], reads=[og_b])

        for idx in range(len(its) + LA):
            if idx < len(its):
                emit_front(idx)
            if idx - LA >= 0:
                emit_back(idx - LA)
            yield


def ssm_pre(k, I, S, SB, C, cfg, esp, after_alloc=None):
    nc = k.nc
    NBLK, NCTX, NOWN = cfg["NBLK"], cfg["NCTX"], cfg["NOWN"]
    identf, cb = C["identf"], C["cb"]
    N = 256
    with ExitStack() as es:
        def sb(n, shp, dt):
            return es.enter_context(nc.sbuf_tensor(n, shp, dt)), Buf(n)

        def sbp(n, shp, dt):
            return esp.enter_context(nc.sbuf_tensor(n, shp, dt)), Buf(n)
        P = Buf("ssm_pre")
        mag = sbp("mag", [128, 16], F32)[0]
        BzT = [sbp(f"BzT{i}", [128, 16, 128], BF16)[0] for i in range(2)]
        CT = [sbp(f"CT{i}", [128, 16, 128], BF16)[0] for i in range(3)]
        dcol, _ = sbp("dcol", [128, 4], F32)
        Ec, _ = sbp("Ec", [128, 16, N + 1], F32)
        Es, _ = sbp("Es", [128, 16, N + 1], F32)
        if after_alloc is not None:
            after_alloc()

        def small(n):
            return sb(n, [128, 16], F32)[0]
        a_re, a_im, ldt, dt_, ang = [small(n) for n in ("a_re", "a_im", "ldt", "dt_", "ang")]
        cc, ss, cs, c_, s_ = [small(n) for n in ("cc", "ss", "cs", "c_", "s_")]
        lr, li, den, nre, zr, zi, q1, q2 = [small(n) for n in ("lr", "li", "den", "nre", "zr", "zi", "q1", "q2")]
        Bre, _ = sb("Bre", [128, 16, 16], F32)
        Bim, _ = sb("Bim", [128, 16, 16], F32)
        Bzr, _ = sb("Bzr", [128, 16, 16], F32)
        Bzi, _ = sb("Bzi", [128, 16, 16], F32)
        Bt1, _ = sb("Bt1", [128, 16, 16], F32)
        Bt2, _ = sb("Bt2", [128, 16, 16], F32)
        ZP, _ = sb("ZP", [128, 16, 128], F32)
        tmp = [sb(f"tmpE{i}", [128, 16, 128], F32)[0] for i in range(4)]

        def dv(f, r=(), w=()):
            k.op("dve", f, reads=[P] + list(r), writes=[P] + list(w))

        def tt(o, a, b, op):
            dv(lambda: nc.vector.tensor_tensor(out=o, in0=a, in1=b, op=op))

        def ts(o, a, s1, op0, s2=None, op1=None):
            if op1 is None:
                dv(lambda: nc.vector.tensor_scalar(o, a, s1, None, op0=op0))
            else:
                dv(lambda: nc.vector.tensor_scalar(o, a, s1, s2, op0=op0, op1=op1))

        def act(o, a, func, **kw):
            k.op("act", lambda: nc.scalar.activation(out=o, in_=a, func=func, **kw), reads=[P], writes=[P])

        for gp in range(2):
            rows = slice(gp * 64, (gp + 1) * 64)
            k.dma(a_re[rows, :], I["ssm_a_re"].rearrange("(i two) p -> two p i", two=2)[gp], writes=[P],
                  allow_slow_non_contiguous=True)
            k.dma(a_im[rows, :], I["ssm_a_im"].rearrange("(i two) p -> two p i", two=2)[gp], writes=[P],
                  allow_slow_non_contiguous=True)
            k.dma(ldt[rows, :], I["ssm_log_dt"].rearrange("(i two) -> two i", two=2)[gp].partition_broadcast(64), writes=[P],
                  allow_slow_non_contiguous=True)
            k.dma(Bre[rows, :, :], I["ssm_b_re"].rearrange("(i two) p h -> two p i h", two=2)[gp], writes=[P])
            k.dma(Bim[rows, :, :], I["ssm_b_im"].rearrange("(i two) p h -> two p i h", two=2)[gp], writes=[P])
        k.dma(dcol[:, :], I["ssm_d"].rearrange("(ut gl) h -> (gl h) ut", gl=8), writes=[P], allow_slow_non_contiguous=True)
        act(dt_[:], ldt[:], AF.Exp)
        tt(q1[:], a_re[:], dt_[:], ALU.mult)
        act(mag[:], q1[:], AF.Exp)
        tt(ang[:], a_im[:], dt_[:], ALU.mult)
        act(s_[:], ang[:], AF.Sin, scale=1.0 / 16.0)
        ts(q2[:], ang[:], -1.0 / 16.0, ALU.mult, math.pi / 2.0, ALU.add)
        act(c_[:], q2[:], AF.Sin)
        for _ in range(4):
            tt(cc[:], c_[:], c_[:], ALU.mult)
            tt(ss[:], s_[:], s_[:], ALU.mult)
            tt(cs[:], c_[:], s_[:], ALU.mult)
            tt(c_[:], cc[:], ss[:], ALU.subtract)
            ts(s_[:], cs[:], 2.0, ALU.mult)
        tt(lr[:], mag[:], c_[:], ALU.mult)
        tt(li[:], mag[:], s_[:], ALU.mult)
        tt(q1[:], a_re[:], a_re[:], ALU.mult)
        tt(q2[:], a_im[:], a_im[:], ALU.mult)
        tt(den[:], q1[:], q2[:], ALU.add)
        dv(lambda: nc.vector.reciprocal(den[:], den[:]))
        ts(nre[:], lr[:], -1.0, ALU.add)
        tt(q1[:], nre[:], a_re[:], ALU.mult)
        tt(q2[:], li[:], a_im[:], ALU.mult)
        tt(q1[:], q1[:], q2[:], ALU.add)
        tt(zr[:], q1[:], den[:], ALU.mult)
        tt(q1[:], li[:], a_re[:], ALU.mult)
        tt(q2[:], nre[:], a_im[:], ALU.mult)
        tt(q1[:], q1[:], q2[:], ALU.subtract)
        tt(zi[:], q1[:], den[:], ALU.mult)
        zrb = zr[:, :].unsqueeze(2).to_broadcast([128, 16, 16])
        zib = zi[:, :].unsqueeze(2).to_broadcast([128, 16, 16])
        tt(Bt1[:], Bre[:], zrb, ALU.mult)
        tt(Bt2[:], Bim[:], zib, ALU.mult)
        tt(Bzr[:], Bt1[:], Bt2[:], ALU.subtract)
        tt(Bt1[:], Bim[:], zrb, ALU.mult)
        tt(Bt2[:], Bre[:], zib, ALU.mult)
        tt(Bzi[:], Bt1[:], Bt2[:], ALU.add)

        def transpose16(src, dst, scale=None):
            for g in range(4):
                bt, bb = k.bank()
                fns = [(lambda q=q: nc.tensor.transpose(out=bt[:, q * 128:(q + 1) * 128], in_=src[:, 4 * g + q, :],
                                                        identity=identf[:])) for q in range(4)]
                k.op("pe", fns, reads=[P, cb], writes=[bb])
                o = dst[:, 4 * g:4 * g + 4, :]
                i_ = bt[:, :].rearrange("p (a b) -> p a b", b=128)
                if scale is None:
                    k.op("dve", lambda o=o, i_=i_: nc.vector.tensor_copy(o, i_), reads=[bb, P], writes=[P])
                else:
                    k.op("dve", lambda o=o, i_=i_: nc.vector.tensor_scalar_mul(o, i_, scale), reads=[bb, P], writes=[P])

        for ri, Bz in enumerate((Bzr, Bzi)):
            dv(lambda: nc.vector.memset(ZP[:], 0.0))
            for gp in range(2):
                for kq in range(4):
                    o = ZP[gp * 64:(gp + 1) * 64, kq::4, 32 * kq + 16 * gp:32 * kq + 16 * gp + 16]
                    i_ = Bz[gp * 64:(gp + 1) * 64, kq::4, :]
                    dv(lambda o=o, i_=i_: nc.vector.tensor_copy(o, i_))
            transpose16(ZP, BzT[ri])
        for ri, nm in enumerate(("ssm_c_re", "ssm_c_im")):
            dv(lambda: nc.vector.memset(ZP[:], 0.0))
            for kq in range(4):
                for gp in range(2):
                    r0 = (2 * kq + gp) * 16
                    k.dma(ZP[r0:r0 + 16, kq::4, gp * 64:(gp + 1) * 64],
                          I[nm].rearrange("(ut r) h p -> r h ut p", r=8)[2 * kq + gp], reads=[P], writes=[P])
            transpose16(ZP, CT[ri], scale=(None if ri == 0 else -1.0))
            if ri == 0:
                transpose16(ZP, CT[2], scale=-1.0)
        dv(lambda: nc.vector.memset(Ec[:, :, 0:1], 1.0))
        dv(lambda: nc.vector.memset(Es[:, :, 0:1], 0.0))
        dv(lambda: nc.vector.tensor_copy(Ec[:, :, 1], c_[:]))
        dv(lambda: nc.vector.tensor_copy(Es[:, :, 1], s_[:]))
        m = 1
        while m < N:
            cmb_ = Ec[:, :, m:m + 1].to_broadcast([128, 16, m])
            smb_ = Es[:, :, m:m + 1].to_broadcast([128, 16, m])
            e1c = Ec[:, :, 1:m + 1]
            e1s = Es[:, :, 1:m + 1]
            tA, tB, tC, tD = [t[:, :, 0:m] for t in tmp]
            tt(tA, e1c, cmb_, ALU.mult)
            tt(tB, e1s, smb_, ALU.mult)
            tt(tC, e1c, smb_, ALU.mult)
            tt(tD, e1s, cmb_, ALU.mult)
            tt(Ec[:, :, m + 1:2 * m + 1], tA, tB, ALU.subtract)
            tt(Es[:, :, m + 1:2 * m + 1], tC, tD, ALU.add)
            m *= 2

    return dict(P=P, mag=mag, BzT=BzT, CT=CT, dcol=dcol, Ec=Ec, Es=Es)


def ssm_gen(k, I, S, SB, C, cfg, pre, es, banks):
    nc = k.nc
    NBLK, NCTX, NOWN = cfg["NBLK"], cfg["NCTX"], cfg["NOWN"]
    N = 256
    P, mag, BzT, CT, dcol, Ec, Es = pre["P"], pre["mag"], pre["BzT"], pre["CT"], pre["dcol"], pre["Ec"], pre["Es"]
    bki = [0]

    xbanks = banks[0:len(banks) - 1]

    def nbank():
        bki[0] += 1
        return xbanks[bki[0] % len(xbanks)]
    if True:
        def sb(n, shp, dt):
            return es.enter_context(nc.sbuf_tensor(n, shp, dt)), Buf(n)
        wg_t, _ = sb("wglu", [128, 4, 512], BF16)
        wgb = [Buf(f"wglu{i}") for i in range(4)]
        load_weight_bf16(k, wg_t, wgb, I["w_glu"], 4, 512, None, None, None)
        NS = 3
        uc = [sb(f"uc{i}", [128, 4, N], BF16) for i in range(3)]
        xs = [sb(f"xs{i}", [128, 4, N], F32) for i in range(NS)]
        p1 = [sb(f"p1_{i}", [128, 2, N], F32) for i in range(NS)]
        p2 = [sb(f"p2_{i}", [128, 2, N], F32) for i in range(NS)]
        RBN = 6
        YD = 3
        rb = [sb(f"rb{i}", [128, 2, N], BF16) for i in range(RBN)]
        rb2 = [sb(f"rb2{i}", [128, 2, N], BF16) for i in range(RBN)]
        wi_ = [sb(f"win_{i}", [128, 2, N], F32) for i in range(NS)]
        W, _ = sb("Wst", [128, 16, 2, N], F32)
        wb = [Buf(f"W{i}") for i in range(16)]
        car = [sb(f"car{i}", [128, 16], F32) for i in range(2)]
        cai = [sb(f"cai{i}", [128, 16], F32) for i in range(2)]
        ct = [sb(f"ct{i}", [128, 16], F32) for i in range(4)]
        yv = [sb(f"yv{i}", [128, N], F32) for i in range(2)]
        ygf = [sb(f"ygf{i}", [128, 4, N], F32) for i in range(2)]
        ygb16, ygbb = sb("ygb16", [128, 4, N], BF16)
        sg = [sb(f"sg{i}", [128, N], F32) for i in range(2)]
        so, sob = sb("so", [128, 4, N], BF16)
        k.op("dve", lambda: nc.vector.memset(car[0][0][:], 0.0), writes=[car[0][1]])
        k.op("dve", lambda: nc.vector.memset(cai[0][0][:], 0.0), writes=[cai[0][1]])

        def load_u(c):
            u_t, u_b = uc[c % 3]
            k.dma(u_t[:], S["UT"][:, :, c * N:(c + 1) * N].rearrange("u p t -> p u t"), writes=[u_b])

        def in_a(g):
            c, i = divmod(g, 16)
            own = c >= NCTX
            if i == 0 and c + 1 < NBLK:
                load_u(c + 1)
            u_t, u_b = uc[c % 3]
            bt, bb = nbank()
            fns = [_mm(nc, bt[:, 0:N], BzT[0][:, i, :], u_t[:, i // 4, :], True, True),
                   _mm(nc, bt[:, N:2 * N], BzT[1][:, i, :], u_t[:, i // 4, :], True, True)]
            k.op("pe", fns, reads=[P, u_b], writes=[bb])
            x_t, x_b = xs[g % NS]
            k.op("act", lambda: nc.scalar.copy(x_t[:, 0:2, :].rearrange("p a b -> p (a b)"), bt[:, :]), reads=[bb], writes=[x_b])
            k.op("act", lambda: nc.scalar.copy(x_t[:, 2, :], bt[:, N:2 * N]), reads=[bb], writes=[x_b])
            k.op("act", lambda: nc.scalar.mul(x_t[:, 3, :], bt[:, 0:N], -1.0), reads=[bb], writes=[x_b])
            ecb = Ec[:, i:i + 1, 0:N].to_broadcast([128, 2, N])
            esb = Es[:, i:i + 1, 0:N].to_broadcast([128, 2, N])
            a_t, a_b = p1[g % NS]
            b_t, b_b = p2[g % NS]
            k.op("pool", lambda: nc.gpsimd.tensor_tensor(out=a_t[:], in0=x_t[:, 0:2, :], in1=ecb, op=ALU.mult),
                 reads=[x_b, P], writes=[a_b])
            k.op("pool", lambda: nc.gpsimd.tensor_tensor(out=b_t[:], in0=x_t[:, 2:4, :], in1=esb, op=ALU.mult),
                 reads=[x_b, P], writes=[b_b])

        def in_b(g):
            a_t, a_b = p1[g % NS]
            b_t, b_b = p2[g % NS]
            w_t, w_b = wi_[g % NS]
            k.op("dve", lambda: nc.vector.tensor_tensor(out=w_t[:], in0=a_t[:], in1=b_t[:], op=ALU.add),
                 reads=[a_b, b_b], writes=[w_b])

        def in_c(g):
            c, i = divmod(g, 16)
            cr_t, cr_b = car[c % 2]
            ci_t, ci_b = cai[c % 2]
            w_t, w_b = wi_[g % NS]
            magb = mag[:, i:i + 1].to_broadcast([128, N])
            k.op("dve", lambda: nc.vector.tensor_tensor_scan(
                out=W[:, i, 0, :], data0=magb, data1=w_t[:, 0, :], initial=cr_t[:, i:i + 1], op0=ALU.mult, op1=ALU.add),
                reads=[w_b, cr_b, P], writes=[wb[i]])
            k.op("dve", lambda: nc.vector.tensor_tensor_scan(
                out=W[:, i, 1, :], data0=magb, data1=w_t[:, 1, :], initial=ci_t[:, i:i + 1], op0=ALU.mult, op1=ALU.add),
                reads=[w_b, ci_b, P], writes=[wb[i]])
            if i == 15:
                ncr_t, ncr_b = car[(c + 1) % 2]
                nci_t, nci_b = cai[(c + 1) % 2]
                wlr = W[:, :, 0, N - 1]
                wli = W[:, :, 1, N - 1]
                enc = Ec[:, :, N]
                ens = Es[:, :, N]
                cts = [t[0] for t in ct]
                ctb = ct[0][1]
                k.op("dve", lambda: nc.vector.tensor_tensor(out=cts[0][:], in0=wlr, in1=enc, op=ALU.mult), reads=wb + [P], writes=[ctb])
                k.op("dve", lambda: nc.vector.tensor_tensor(out=cts[1][:], in0=wli, in1=ens, op=ALU.mult), reads=wb + [P, ctb], writes=[ctb])
                k.op("dve", lambda: nc.vector.tensor_tensor(out=cts[2][:], in0=wlr, in1=ens, op=ALU.mult), reads=wb + [P, ctb], writes=[ctb])
                k.op("dve", lambda: nc.vector.tensor_tensor(out=cts[3][:], in0=wli, in1=enc, op=ALU.mult), reads=wb + [P, ctb], writes=[ctb])
                k.op("dve", lambda: nc.vector.tensor_tensor(out=ncr_t[:], in0=cts[0][:], in1=cts[1][:], op=ALU.subtract),
                     reads=[ctb], writes=[ncr_b])
                k.op("dve", lambda: nc.vector.tensor_tensor(out=nci_t[:], in0=cts[2][:], in1=cts[3][:], op=ALU.add),
                     reads=[ctb], writes=[nci_b])

        def out_p(g):
            c, i = divmod(g, 16)
            b_t, b_b = rb[g % RBN]
            b2_t, b2_b = rb2[g % RBN]
            ecb = Ec[:, i:i + 1, 0:N].to_broadcast([128, 2, N])
            esb = Es[:, i:i + 1, 0:N].to_broadcast([128, 2, N])
            k.op("pool", lambda: nc.gpsimd.tensor_tensor(out=b_t[:], in0=W[:, i, :, :], in1=ecb, op=ALU.mult),
                 reads=[wb[i], P], writes=[b_b])
            k.op("dve", lambda: nc.vector.tensor_tensor(out=b2_t[:], in0=W[:, i, :, :], in1=esb, op=ALU.mult),
                 reads=[wb[i], P], writes=[b2_b])

        def out_y(g):
            c, i = divmod(g, 16)
            jc = c - NCTX
            u_t, u_b = uc[c % 3]
            b_t, b_b = rb[g % RBN]
            b2_t, b2_b = rb2[g % RBN]
            yg_t, yg_b = ygf[c % 2]
            ut, kq = i // 4, i % 4
            bt, bb = banks[-1]
            fns = [_mm(nc, bt[:, 0:N], CT[0][:, i, :], b_t[:, 0, :], kq == 0, False),
                   _mm(nc, bt[:, 0:N], CT[1][:, i, :], b_t[:, 1, :], False, False),
                   _mm(nc, bt[:, 0:N], CT[1][:, i, :], b2_t[:, 0, :], False, False),
                   _mm(nc, bt[:, 0:N], CT[2][:, i, :], b2_t[:, 1, :], False, kq == 3)]
            k.op("pe", fns, reads=[P, b_b, b2_b], writes=[bb])
            if kq == 3:
                y_t, y_b = yv[ut % 2]
                k.op("dve", lambda: nc.vector.scalar_tensor_tensor(
                    out=y_t[:], in0=u_t[:, ut, :], scalar=dcol[:, ut:ut + 1], in1=bt[:, 0:N], op0=ALU.mult, op1=ALU.add),
                    reads=[bb, u_b, P], writes=[y_b])
                k.op("act", lambda: nc.scalar.activation(out=yg_t[:, ut, :], in_=y_t[:], func=AF.Gelu_apprx_tanh),
                     reads=[y_b], writes=[yg_b])

        def glu(c):
            jc = c - NCTX
            yg_t, yg_b = ygf[c % 2]
            k.op("pool", lambda: nc.gpsimd.tensor_copy(ygb16[:], yg_t[:]), reads=[yg_b], writes=[ygbb])
            for mt in range(4):
                bt, bb = nbank()
                fns = [_mm(nc, bt[:, 0:N], wg_t[:, ut, mt * 128:(mt + 1) * 128], ygb16[:, ut, :], ut == 0, ut == 3) for ut in range(4)]
                k.op("pe", fns, reads=wgb + [ygbb], writes=[bb])
                s_t, s_b = sg[mt % 2]
                k.op("act", lambda s_t=s_t, bt=bt: nc.scalar.activation(out=s_t[:], in_=bt[:, 0:N], func=AF.Sigmoid),
                     reads=[bb], writes=[s_b])
                k.op("dve", lambda mt=mt, s_t=s_t: nc.vector.tensor_tensor(out=so[:, mt, :], in0=yg_t[:, mt, :], in1=s_t[:], op=ALU.mult),
                     reads=[s_b, yg_b], writes=[sob])
            k.dma(S["SSMY"][:, :, jc * N:(jc + 1) * N].rearrange("u p t -> p u t"), so[:], reads=[sob])

        G = NBLK * 16
        g0own = NCTX * 16
        load_u(0)
        GD = 4
        for step in range(G + 3 + YD + GD + 1):
            if step < G:
                in_a(step)
            if 0 <= step - 1 < G:
                in_b(step - 1)
            if 0 <= step - 2 < G:
                in_c(step - 2)
            g = step - 3
            if g0own <= g < G:
                out_p(g)
            g = step - 3 - YD
            if g0own <= g < G:
                out_y(g)
            g = step - 3 - YD - GD
            if g0own <= g < G and g % 16 == 15:
                glu(g // 16)
            yield


def ln_tile(k, ts_t, ts_b, g_t, b_t, gbuf, o_t, o_b, st, mv, lb):
    nc = k.nc
    st_t, mv_t = st, mv
    k.op("dve", lambda: nc.vector.bn_stats(out=st_t[:, 0, :], in_=ts_t[:, 0:512]), reads=[ts_b], writes=[lb])
    k.op("dve", lambda: nc.vector.bn_stats(out=st_t[:, 1, :], in_=ts_t[:, 512:1024]), reads=[ts_b, lb], writes=[lb])
    k.op("dve", lambda: nc.vector.bn_aggr(out=mv_t[:, 0:2], in_=st_t[:, :, :].rearrange("p a b -> p (a b)")), reads=[lb], writes=[lb])
    k.op("dve", lambda: nc.vector.tensor_scalar(mv_t[:, 2:3], mv_t[:, 1:2], LN_EPS, None, op0=ALU.add), reads=[lb], writes=[lb])
    k.op("act", lambda: nc.scalar.sqrt(mv_t[:, 3:4], mv_t[:, 2:3]), reads=[lb], writes=[lb])
    k.op("dve", lambda: nc.vector.reciprocal(mv_t[:, 4:5], mv_t[:, 3:4]), reads=[lb], writes=[lb])
    k.op("dve", lambda: nc.vector.tensor_scalar(o_t, ts_t[:, :], mv_t[:, 0:1], mv_t[:, 4:5], op0=ALU.subtract, op1=ALU.mult),
         reads=[ts_b, lb], writes=[o_b])
    k.op("dve", lambda: nc.vector.tensor_tensor(out=o_t, in0=o_t, in1=g_t[:], op=ALU.mult), reads=[gbuf, o_b], writes=[o_b])
    k.op("dve", lambda: nc.vector.tensor_tensor(out=o_t, in0=o_t, in1=b_t[:], op=ALU.add), reads=[gbuf, o_b], writes=[o_b])


def phase_merge(k, I, S, SB, C, cfg):
    nc = k.nc
    NBLK, NCTX, NOWN = cfg["NBLK"], cfg["NCTX"], cfg["NOWN"]
    identb, cb = C["identb"], C["cb"]
    with ExitStack() as es:
        def sb(n, shp, dt):
            return es.enter_context(nc.sbuf_tensor(n, shp, dt)), Buf(n)
        wap, _ = sb("wap", [128, 4, 1024], BF16)
        wsp, _ = sb("wsp", [128, 4, 1024], BF16)
        wout, _ = sb("wout", [128, 8, 1024], BF16)
        wapb = [Buf(f"wap{i}") for i in range(4)]
        wspb = [Buf(f"wsp{i}") for i in range(4)]
        woutb = [Buf(f"wout{i}") for i in range(8)]
        load_weight_bf16(k, wap, wapb, I["w_attn_proj"], 4, 1024, None, None, None)
        load_weight_bf16(k, wsp, wspb, I["w_ssm_proj"], 4, 1024, None, None, None)
        load_weight_bf16(k, wout, woutb, I["w_out"], 8, 1024, None, None, None)
        g1, gbuf = sb("g1", [128, 1024], F32)
        b1, _ = sb("b1", [128, 1024], F32)
        k.dma(g1[:], I["ln1_g"].partition_broadcast(128), writes=[gbuf])
        k.dma(b1[:], I["ln1_b"].partition_broadcast(128), writes=[gbuf])
        att = [sb(f"att{i}", [128, 2, 512], BF16) for i in range(2)]
        attT = [sb(f"attT{i}", [128, 4, 256], BF16) for i in range(2)]
        ssmy = [sb(f"ssmy{i}", [128, 4, 256], BF16) for i in range(2)]
        sgt = [sb(f"sgt{i}", [128, 16, 256], BF16) for i in range(2)]
        xo = [sb(f"xo{i}", [128, 2, 1024], F32) for i in range(2)]
        tA = [sb(f"tA{i}", [128, 256], F32) for i in range(2)]
        tB = [sb(f"tB{i}", [128, 256], F32) for i in range(2)]
        mg = [sb(f"mg{i}", [128, 8, 256], BF16) for i in range(2)]
        tsum = [sb(f"tsum{i}", [128, 1024], F32) for i in range(2)]
        h1 = [sb(f"h1_{i}", [128, 1024], F32) for i in range(2)]
        st, lb = sb("lnst", [128, 2, 6], F32)
        mv, _ = sb("lnmv", [128, 8], F32)
        def load_m(j):
            s = j % 2
            k.dma(att[s][0][:], S["ATT"][j * 256:(j + 1) * 256, :].rearrange("(t p) c -> p t c", p=128), writes=[att[s][1]])
            k.dma(ssmy[s][0][:], S["SSMY"][:, :, j * 256:(j + 1) * 256].rearrange("u p t -> p u t"), writes=[ssmy[s][1]])
            for u0 in range(0, 16, 4):
                k.dma(sgt[s][0][:, u0:u0 + 4, :], S["SG"][u0:u0 + 4, :, j * 256:(j + 1) * 256].rearrange("u p t -> p u t"),
                      writes=[sgt[s][1]])
            k.dma(xo[s][0][:], I["xcat"][(NCTX + j) * 256:(NCTX + j + 1) * 256, :].rearrange("(t p) f -> p t f", p=128),
                  writes=[xo[s][1]])

        load_m(0)
        for j in range(NOWN):
            s = j % 2
            if j + 1 < NOWN:
                load_m(j + 1)
            for tt in range(2):
                bt, bb = k.bank()
                bv = bt[:, :].bitcast(BF16)
                fns = [(lambda q=q: nc.tensor.transpose(out=bv[:, q * 128:(q + 1) * 128], in_=att[s][0][:, tt, q * 128:(q + 1) * 128],
                                                        identity=identb[:])) for q in range(4)]
                k.op("pe", fns, reads=[att[s][1], cb], writes=[bb])
                k.op("act", lambda bv=bv, tt=tt: nc.scalar.copy(attT[s][0][:, :, tt * 128:(tt + 1) * 128],
                                                                 bv[:, 0:512].rearrange("p (a b) -> p a b", b=128)),
                     reads=[bb], writes=[attT[s][1]])
            for m in range(8):
                bt, bb = k.bank()
                fa = [_mm(nc, bt[:, 0:256], wap[:, kt, m * 128:(m + 1) * 128], attT[s][0][:, kt, :], kt == 0, kt == 3) for kt in range(4)]
                k.op("pe", fa, reads=wapb + [attT[s][1]], writes=[bb])
                fb = [_mm(nc, bt[:, 256:512], wsp[:, kt, m * 128:(m + 1) * 128], ssmy[s][0][:, kt, :], kt == 0, kt == 3) for kt in range(4)]
                k.op("pe", fb, reads=wspb + [ssmy[s][1]], writes=[bb])
                a_t, a_b = tA[m % 2]
                b_t, b_b = tB[m % 2]
                k.op("dve", lambda a_t=a_t, bt=bt, m=m: nc.vector.tensor_tensor(out=a_t[:], in0=bt[:, 0:256], in1=sgt[s][0][:, m, :], op=ALU.mult),
                     reads=[bb, sgt[s][1]], writes=[a_b])
                k.op("dve", lambda b_t=b_t, bt=bt, m=m: nc.vector.tensor_tensor(out=b_t[:], in0=bt[:, 256:512], in1=sgt[s][0][:, 8 + m, :], op=ALU.mult),
                     reads=[bb, sgt[s][1]], writes=[b_b])
                k.op("pool", lambda a_t=a_t, b_t=b_t, m=m: nc.gpsimd.tensor_tensor(out=mg[s][0][:, m, :], in0=a_t[:], in1=b_t[:], op=ALU.add),
                     reads=[a_b, b_b], writes=[mg[s][1]])
            for tt in range(2):
                ts_t, ts_b = tsum[tt]
                for half in range(2):
                    bt, bb = k.bank()
                    fns = [_mm(nc, bt[:, :], mg[s][0][:, kt, tt * 128:(tt + 1) * 128], wout[:, kt, half * 512:(half + 1) * 512], kt == 0, kt == 7)
                           for kt in range(8)]
                    k.op("pe", fns, reads=woutb + [mg[s][1]], writes=[bb])
                    k.op("dve", lambda bt=bt, tt=tt, half=half, ts_t=ts_t: nc.vector.scalar_tensor_tensor(
                        out=ts_t[:, half * 512:(half + 1) * 512], in0=xo[s][0][:, tt, half * 512:(half + 1) * 512], scalar=ALPHA,
                        in1=bt[:, :], op0=ALU.mult, op1=ALU.add), reads=[bb, xo[s][1]], writes=[ts_b])
                h_t, h_b = h1[tt]
                ln_tile(k, ts_t, ts_b, g1, b1, gbuf, h_t[:, :], h_b, st, mv, lb)
                r0 = j * 256 + tt * 128
                k.dma(S["H1"][r0:r0 + 128, :], h_t[:], reads=[h_b])


def phase_ffn(k, I, S, SB, C, cfg, out):
    nc = k.nc
    NBLK, NCTX, NOWN = cfg["NBLK"], cfg["NCTX"], cfg["NOWN"]
    identf, cb = C["identf"], C["cb"]
    NF = 2 * FF // 128
    NP = FF // 128
    with ExitStack() as es:
        def sb(n, shp, dt):
            return es.enter_context(nc.sbuf_tensor(n, shp, dt)), Buf(n)
        wup, _ = sb("wup", [128, 8, 2 * FF], BF16)
        wdn, _ = sb("wdn", [128, NP, 1024], BF16)
        wupb = [Buf(f"wup{i}") for i in range(8)]
        wdnb = [Buf(f"wdn{i}") for i in range(NP)]
        load_weight_bf16(k, wup, wupb, I["w_up"], 8, 2 * FF, None, None, None)
        load_weight_bf16(k, wdn, wdnb, I["w_down"], NP, 1024, None, None, None)
        g2, gbuf = sb("g2", [128, 1024], F32)
        b2, _ = sb("b2", [128, 1024], F32)
        cw, _ = sb("cw", [128, NF, 3], F32)
        cbi, _ = sb("cbi", [128, NF], F32)
        hp, _ = sb("hp", [128, 1], F32)
        k.dma(g2[:], I["ln2_g"].partition_broadcast(128), writes=[gbuf])
        k.dma(b2[:], I["ln2_b"].partition_broadcast(128), writes=[gbuf])
        for jj in range(3):
            for f0 in range(0, NF, 11):
                k.dma(cw[:, f0:f0 + 11, jj], I["conv_w"][jj].rearrange("(f p) -> p f", p=128)[:, f0:f0 + 11], writes=[gbuf],
                      allow_slow_non_contiguous=True)
        for f0 in range(0, NF, 11):
            k.dma(cbi[:, f0:f0 + 11], I["conv_b"].rearrange("(f p) -> p f", p=128)[:, f0:f0 + 11], writes=[gbuf],
                  allow_slow_non_contiguous=True)
        k.dma(hp[:, :], I["has_prev"], writes=[gbuf])
        halo, halob = sb("halo", [128, NF, 2], F32)
        h1 = [sb(f"h1f{i}", [128, 2, 1024], F32) for i in range(2)]
        h1Ts = [sb(f"h1T{i}", [128, 8, 256], BF16) for i in range(2)]
        CVN = 3
        raw = [sb(f"raw{i}", [128, 2, 258], F32) for i in range(CVN)]
        cv = [sb(f"cv{i}", [128, 2, 256], F32) for i in range(CVN)]
        gl = [sb(f"gl{i}", [128, 256], F32) for i in range(2)]
        actT, _ = sb("actT", [128, NP, 256], BF16)
        actTb = [Buf(f"actT{p}") for p in range(NP)]
        ptmp = [sb(f"ptmp{i}", [128, 256], F32) for i in range(2)]
        tsums = [sb(f"tsumf{i}", [128, 1024], F32) for i in range(2)]
        o_, ob = sb("of", [128, 1024], F32)
        st, lb = sb("lnst2", [128, 2, 6], F32)
        mv, _ = sb("lnmv2", [128, 8], F32)
        upbanks = k.banks[0:4]
        accbanks = k.banks[4:8]
        ubi = [0]

        def upbank():
            ubi[0] += 1
            return upbanks[ubi[0] % 4]

        def load_h1(j):
            h_t, h_b = h1[j % 2]
            k.dma(h_t[:], S["H1"][j * 256:(j + 1) * 256, :].rearrange("(t p) f -> p t f", p=128), writes=[h_b])

        LAG = 9
        load_h1(0)
        if NOWN > 1:
            load_h1(1)

        def transposes(j):
            h_t, h_b = h1[j % 2]
            hT, hTb = h1Ts[j % 2]
            for tt in range(2):
                for g in range(2):
                    bt, bb = upbank()
                    fns = [(lambda q=q: nc.tensor.transpose(out=bt[:, q * 128:(q + 1) * 128],
                                                            in_=h_t[:, tt, (4 * g + q) * 128:(4 * g + q + 1) * 128],
                                                            identity=identf[:])) for q in range(4)]
                    k.op("pe", fns, reads=[h_b, cb], writes=[bb])
                    k.op("act", lambda bt=bt, g=g, tt=tt: nc.scalar.copy(hT[:, 4 * g:4 * g + 4, tt * 128:(tt + 1) * 128],
                                                                      bt[:, :].rearrange("p (a b) -> p a b", b=128)),
                         reads=[bb], writes=[hTb])

        transposes(0)
        for j in range(1):
            s = j % 2
            h_t, h_b = h1[s]
            h1T, h1Tb = h1Ts[s]
            if j == 0:
                bt, bb = upbank()
                for f in range(NF):
                    fns = [_mm(nc, bt[:, 2 * f:2 * f + 2], wup[:, kt, f * 128:(f + 1) * 128], h1T[:, kt, 254:256], kt == 0, kt == 7)
                           for kt in range(8)]
                    k.op("pe", fns, reads=wupb + [h1Tb], writes=[bb])
                k.op("dve", lambda bt=bt: nc.vector.tensor_scalar(halo[:, :, :].rearrange("p a b -> p (a b)"), bt[:, 0:2 * NF],
                                                                  hp[:, 0:1], None, op0=ALU.mult),
                     reads=[bb, gbuf], writes=[halob])
                if NOWN > 1:
                    transposes(1)
                if 2 < NOWN:
                    load_h1(2)
                continue

        G = (NOWN - 1) * NP
        halobs = [Buf(f"halo{p}") for p in range(NP)]
        for hb_ in halobs:
            hb_.w = halob.w

        def halo_in(gs):
            p = gs % NP
            r_t, r_b = raw[gs % CVN]
            k.op("pool", lambda: nc.gpsimd.tensor_copy(r_t[:, :, 0:2], halo[:, p::NP, :]), reads=[halobs[p]], writes=[r_b])

        def up_stage(j, p, gs):
            h1T, h1Tb = h1Ts[j % 2]
            bt, bb = upbank()
            for v, f in enumerate((p, NP + p)):
                fns = [_mm(nc, bt[:, v * 256:(v + 1) * 256], wup[:, kt, f * 128:(f + 1) * 128], h1T[:, kt, :], kt == 0, kt == 7)
                       for kt in range(8)]
                k.op("pe", fns, reads=wupb + [h1Tb], writes=[bb])
            r_t, r_b = raw[gs % CVN]
            c_t, c_b = cv[gs % CVN]
            if gs == 0:
                halo_in(gs)
            k.op("act", lambda: nc.scalar.copy(r_t[:, :, 2:258], bt[:, :].rearrange("p (a b) -> p a b", b=256)),
                 reads=[bb], writes=[r_b])
            for v, f in enumerate((p, NP + p)):
                k.op("act", lambda v=v, f=f: nc.scalar.activation(out=c_t[:, v, :], in_=bt[:, v * 256:(v + 1) * 256],
                                                                  func=AF.Identity, scale=cw[:, f, 2:3], bias=cbi[:, f:f + 1]),
                     reads=[bb, gbuf], writes=[c_b])
            k.op("pool", lambda: nc.gpsimd.tensor_copy(halo[:, p::NP, :], r_t[:, :, 256:258]), reads=[r_b], writes=[halobs[p]])
            if gs + 1 < G:
                halo_in(gs + 1)
            for jj in (1, 0):
                k.op("dve", lambda jj=jj: nc.vector.scalar_tensor_tensor(
                    out=c_t[:, 0, :], in0=r_t[:, 0, jj:jj + 256], scalar=cw[:, p, jj:jj + 1], in1=c_t[:, 0, :],
                    op0=ALU.mult, op1=ALU.add), reads=[r_b, gbuf, c_b], writes=[c_b])
            t_t, t_b = ptmp[gs % 2]
            k.op("pool", lambda: nc.gpsimd.tensor_scalar(
                t_t[:], r_t[:, 1, 1:257], cw[:, NP + p, 1:2], 0.0, op0=ALU.mult, op1=ALU.add),
                reads=[r_b, gbuf], writes=[t_b])
            k.op("pool", lambda: nc.gpsimd.tensor_tensor(out=c_t[:, 1, :], in0=c_t[:, 1, :], in1=t_t[:], op=ALU.add),
                 reads=[t_b, c_b], writes=[c_b])
            k.op("dve", lambda: nc.vector.scalar_tensor_tensor(
                out=c_t[:, 1, :], in0=r_t[:, 1, 0:256], scalar=cw[:, NP + p, 0:1], in1=c_t[:, 1, :],
                op0=ALU.mult, op1=ALU.add), reads=[r_b, gbuf, c_b], writes=[c_b])

        def up_stage_b(j, p, gs):
            c_t, c_b = cv[gs % CVN]
            g_t, g_b = gl[gs % 2]
            k.op("act", lambda: nc.scalar.activation(out=g_t[:], in_=c_t[:, 1, :], func=AF.Gelu_apprx_tanh),
                 reads=[c_b], writes=[g_b])
            k.op("dve", lambda: nc.vector.tensor_tensor(out=actT[:, p, :], in0=g_t[:], in1=c_t[:, 0, :], op=ALU.mult),
                 reads=[g_b, c_b], writes=[actTb[p]])

        def down_stage(j, p):
            fns = []
            for tt in range(2):
                for half in range(2):
                    bt, bb = accbanks[tt * 2 + half]
                    fns.append(_mm(nc, bt[:, :], actT[:, p, tt * 128:(tt + 1) * 128], wdn[:, p, half * 512:(half + 1) * 512],
                                   p == 0, p == NP - 1))
            k.op("pe", fns, reads=[wdnb[p], actTb[p]], writes=[b_[1] for b_ in accbanks])
            if p == NP - 1:
                tail(j)

        def tail(j):
            h_t, h_b = h1[j % 2]
            for tt in range(2):
                ts_t, ts_b = tsums[tt]
                for half in range(2):
                    bt, bb = accbanks[tt * 2 + half]
                    k.op("dve", lambda bt=bt, tt=tt, half=half, ts_t=ts_t: nc.vector.scalar_tensor_tensor(
                        out=ts_t[:, half * 512:(half + 1) * 512], in0=h_t[:, tt, half * 512:(half + 1) * 512], scalar=ALPHA,
                        in1=bt[:, :], op0=ALU.mult, op1=ALU.add), reads=[bb, h_b], writes=[ts_b])
            for tt in range(2):
                ts_t, ts_b = tsums[tt]
                ln_tile(k, ts_t, ts_b, g2, b2, gbuf, o_[:, :], ob, st, mv, lb)
                r0 = (j - 1) * 256 + tt * 128
                k.dma(out[r0:r0 + 128, :], o_[:], reads=[ob])
            if j + 2 < NOWN:
                load_h1(j + 2)

        for gs in range(G + LAG + 2):
            if gs < G:
                j, p = 1 + gs // NP, gs % NP
                if p == NP - 8 and j + 1 < NOWN:
                    transposes(j + 1)
                up_stage(j, p, gs)
            g1_ = gs - 1
            if 0 <= g1_ < G:
                up_stage_b(1 + g1_ // NP, g1_ % NP, g1_)
            g2_ = gs - 1 - LAG
            if 0 <= g2_ < G:
                down_stage(1 + g2_ // NP, g2_ % NP)


_CACHE = {}


def _consts(NBLK, NCTX, half, full):
    T = NBLK * 256
    NOWN = NBLK - NCTX
    off = 0 if (half == 1 or not full) else -(NBLK // 2) * 256
    if not full:
        off = 0
    pos = (np.arange(T, dtype=np.float64) + off)
    inv = 500000.0 ** (-np.arange(0, 16, 2, dtype=np.float64) / 16.0)
    ang = pos[None, :] * inv[:, None]
    cos = np.ones((64, T)); sin = np.zeros((64, T))
    cos[0:8] = np.cos(ang); cos[8:16] = np.cos(ang)
    sin[0:8] = np.sin(ang); sin[8:16] = np.sin(ang)
    rot_cos = np.concatenate([cos, cos], 0).astype(np.float32)
    rot_sin = np.concatenate([sin, sin], 0).astype(np.float32)
    first_valid = 0 if (half == 1 or not full) else NBLK // 2
    gb = np.full((NOWN, 32), -1e30, np.float32)
    for j in range(NOWN):
        own = NCTX + j
        for n in range(min(own, 32)):
            if n >= first_valid:
                gb[j, n] = 0.0
    gbias = np.ascontiguousarray(np.broadcast_to(np.tile(gb[:, None, :], (1, 16, 1)).reshape(NOWN, 1, 512), (NOWN, 128, 512)))
    hasp = np.full((128, 1), 1.0 if (half == 1 or not full) else 0.0, np.float32)
    key = np.arange(128)[:, None]
    q = np.arange(256)[None, :]
    cm = np.concatenate([(key <= q), (key + 128 <= q)], 1).astype(np.float32)
    blk = np.zeros((32, T), np.float32)
    for n in range(min(NBLK, 32)):
        blk[n, n * 256:(n + 1) * 256] = 1.0
    return dict(rot_cos=rot_cos, rot_sin=rot_sin, gbias=gbias, has_prev=hasp,
                ident=np.eye(128, dtype=np.float32), cmask=cm.astype(ml_dtypes.bfloat16), blkind=blk.astype(ml_dtypes.bfloat16))


def kernel(**inputs):
    x = np.asarray(inputs["x"], np.float32)
    B, SEQ, _ = x.shape
    NBLK, NCTX = 32, 15
    if "nc" not in _CACHE:
        nc = bass.Bass("TRN2", target_bir_lowering=False)
        build(nc, NBLK, NCTX)
        _CACHE["nc"] = nc
    nc = _CACHE["nc"]
    wnames = ["w_in", "w_attn_proj", "ssm_a_re", "ssm_a_im", "ssm_log_dt", "ssm_b_re", "ssm_b_im", "ssm_c_re", "ssm_c_im",
              "ssm_d", "w_glu", "w_ssm_proj", "w_out", "ln1_g", "ln1_b", "w_up", "conv_w", "conv_b", "w_down", "ln2_g", "ln2_b"]
    w = {n: np.ascontiguousarray(np.asarray(inputs[n], np.float32)[0]) for n in wnames}
    in_maps = []
    for c in range(8):
        b, half = c // 2, c % 2
        if half == 1:
            xcat = np.ascontiguousarray(x[b])
        else:
            xcat = np.concatenate([np.zeros((SEQ // 2, D), np.float32), x[b, :SEQ // 2]], 0)
        m = dict(w)
        m["xcat"] = xcat
        m.update(_consts(NBLK, NCTX, half, True))
        in_maps.append(m)
    res = run_bass_kernel_spmd(nc, in_maps, core_ids=list(range(8)))
    outp = np.empty((B, SEQ, D), np.float32)
    for c in range(8):
        b, half = c // 2, c % 2
        outp[b, half * (SEQ // 2):(half + 1) * (SEQ // 2)] = res.results[c]["out"]
    return outp
```

```python
import math
from contextlib import ExitStack

import numpy as np
import ml_dtypes
import concourse.bass as bass
import concourse.mybir as mybir
from concourse.bass_utils import run_bass_kernel_spmd

F32 = mybir.dt.float32
BF16 = mybir.dt.bfloat16
AF = mybir.ActivationFunctionType
ALU = mybir.AluOpType
AX = mybir.AxisListType

D = 1024
NH = 8
DH = 64
FF = 2816
ALPHA = 2.0 ** 0.25
LN_EPS = 1e-5
GC = 0.7978845608028654


class Buf:
    __slots__ = ("name", "w", "r", "excl")

    def __init__(self, name, excl=False):
        self.name = name
        self.w = None
        self.r = []
        self.excl = excl


class KB:
    def __init__(self, nc, es):
        self.nc = nc
        self.es = es
        self.engs = {"pe": nc.tensor, "act": nc.scalar, "dve": nc.vector, "pool": nc.gpsimd, "sp": nc.sync}
        self.cur = {}
        self.waited = {e: {} for e in self.engs}
        self.nsem = 0
        self.rings = {}
        self.ridx = {}
        self.last = {}
        self.banks = []
        self.bank_i = 0
        self.ninst = 0

    def newsem(self, name):
        self.nsem += 1
        return self.es.enter_context(self.nc.semaphore(f"{name}{self.nsem}"))

    def _wait(self, e, tok):
        sem, val, _ = tok
        w = self.waited[e]
        key = id(sem)
        if w.get(key, 0) >= val:
            return
        self.engs[e].wait_ge(sem, val)
        w[key] = val

    def _deps(self, e, reads, writes):
        deps = []
        for b in reads:
            if b.w is not None:
                deps.append(b.w)
            if b.excl:
                deps.extend(b.r)
        for b in writes:
            if b.w is not None:
                deps.append(b.w)
            deps.extend(b.r)
        for t in deps:
            if e == "pe" and t[2] == "pe":
                continue
            self._wait(e, t)

    def _upd(self, tok, reads, writes):
        for b in reads:
            if b.excl:
                b.w = tok
                b.r = []
            else:
                b.r.append(tok)
        for b in writes:
            b.w = tok
            b.r = []

    def op(self, e, fns, reads=(), writes=()):
        if callable(fns):
            fns = [fns]
        self._deps(e, reads, writes)
        ins = None
        for f in fns:
            ins = f()
            self.ninst += 1
        c = self.cur.get(e)
        if c is None or c[1] >= 30000:
            c = [self.newsem("s" + e), 0]
            self.cur[e] = c
        c[1] += 1
        ins.then_inc(c[0], 1)
        tok = (c[0], c[1], e)
        self.last[e] = tok
        self._upd(tok, reads, writes)
        return tok

    def dma(self, out, in_, reads=(), writes=(), q="sp", **kw):
        ring = self.rings.get(q)
        if ring is None:
            ring = [[self.newsem("d" + q), 0] for _ in range(8)]
            self.rings[q] = ring
        i = self.ridx.get(q, 0)
        self.ridx[q] = i + 1
        slot = ring[i % 8]
        if slot[1] > 0:
            self._wait(q, (slot[0], slot[1], "dma"))
        if slot[1] >= 30000:
            slot[0] = self.newsem("d" + q)
            slot[1] = 0
        self._deps(q, reads, writes)
        ins = self.engs[q].dma_start(out=out, in_=in_, **kw)
        self.ninst += 1
        slot[1] += 16
        ins.then_inc(slot[0], 16)
        tok = (slot[0], slot[1], "dma")
        self._upd(tok, reads, writes)
        return tok

    def barrier(self):
        toks = list(self.last.values())
        for q, ring in self.rings.items():
            for s in ring:
                if s[1] > 0:
                    toks.append((s[0], s[1], "dma"))
        for e in self.engs:
            for t in toks:
                self._wait(e, t)

    def bank(self):
        b = self.banks[self.bank_i % len(self.banks)]
        self.bank_i += 1
        return b


def _mm(nc, out, lhsT, rhs, start, stop):
    return lambda: nc.tensor.matmul(out, lhsT, rhs, start=start, stop=stop)


def build(nc, NBLK=32, NCTX=15, dbg=(), phases=("proj", "attn", "ssm", "merge", "ffn")):
    NOWN = NBLK - NCTX
    T = NBLK * 256
    TO = NOWN * 256
    NOUT = (NOWN - 1) * 256

    def din(name, shape, dt=F32):
        return nc.dram_tensor(name, list(shape), dt, kind="ExternalInput").ap()

    def dscr(name, shape, dt):
        kind = "ExternalOutput" if name in dbg else "Internal"
        return nc.dram_tensor(name, list(shape), dt, kind=kind).ap()

    I = dict(
        xcat=din("xcat", [T, D]), w_in=din("w_in", [D, 4096]), w_attn_proj=din("w_attn_proj", [512, D]),
        ssm_a_re=din("ssm_a_re", [32, 64]), ssm_a_im=din("ssm_a_im", [32, 64]), ssm_log_dt=din("ssm_log_dt", [32]),
        ssm_b_re=din("ssm_b_re", [32, 64, 16]), ssm_b_im=din("ssm_b_im", [32, 64, 16]),
        ssm_c_re=din("ssm_c_re", [32, 16, 64]), ssm_c_im=din("ssm_c_im", [32, 16, 64]),
        ssm_d=din("ssm_d", [32, 16]), w_glu=din("w_glu", [512, 512]), w_ssm_proj=din("w_ssm_proj", [512, D]),
        w_out=din("w_out", [D, D]), ln1_g=din("ln1_g", [D]), ln1_b=din("ln1_b", [D]),
        w_up=din("w_up", [D, 2 * FF]), conv_w=din("conv_w", [3, 2 * FF]), conv_b=din("conv_b", [2 * FF]),
        w_down=din("w_down", [FF, D]), ln2_g=din("ln2_g", [D]), ln2_b=din("ln2_b", [D]),
        rot_cos=din("rot_cos", [128, T]), rot_sin=din("rot_sin", [128, T]),
        gbias=din("gbias", [NOWN, 128, 512]), has_prev=din("has_prev", [128, 1]),
        ident=din("ident", [128, 128]), cmask=din("cmask", [128, 512], BF16), blkind=din("blkind", [32, T], BF16),
    )
    out = nc.dram_tensor("out", [NOUT, D], F32, kind="ExternalOutput").ap()
    S = dict(
        KT=dscr("KT", [4, 128, T], BF16), QT=dscr("QT", [4, 128, TO], BF16),
        VP=dscr("VP", [T, NH, 65], BF16), UT=dscr("UT", [4, 128, T], BF16),
        SG=dscr("SG", [16, 128, TO], BF16), SEL=dscr("SEL", [NH, 128, NOWN * 64], F32),
        ATT=dscr("ATT", [TO, 512], BF16), SSMY=dscr("SSMY", [4, 128, TO], BF16),
        H1=dscr("H1", [TO, D], F32), SBT=dscr("SBT", [NH, 32, TO], BF16),
    )
    SB = {k: Buf("scr_" + k) for k in S}
    cfg = dict(NBLK=NBLK, NCTX=NCTX, NOWN=NOWN, T=T, TO=TO, NOUT=NOUT)

    with ExitStack() as es:
        k = KB(nc, es)
        for i in range(8):
            t = es.enter_context(nc.psum_tensor(f"bank{i}", [128, 512], F32))
            k.banks.append((t, Buf(f"bank{i}", excl=True)))
        identf = es.enter_context(nc.sbuf_tensor("identf", [128, 128], F32))
        identb = es.enter_context(nc.sbuf_tensor("identb", [128, 128], BF16))
        cb = Buf("consts")
        k.dma(identf[:], I["ident"], writes=[cb])
        k.op("dve", lambda: nc.vector.tensor_copy(identb[:], identf[:]), reads=[cb], writes=[cb])
        C = dict(identf=identf, identb=identb, cb=cb)
        with ExitStack() as esx:
            pre = None
            if "ssm" in phases and "proj" in phases:
                pass
            with ExitStack() as esw:
                def load_win():
                    win = esw.enter_context(nc.sbuf_tensor("win", [128, 8, 4096], BF16))
                    winb = [Buf(f"win{kt}") for kt in range(8)]
                    C["win"], C["winb"] = win, winb
                    load_weight_bf16(k, win, winb, I["w_in"], 8, 4096, None, None, None)
                if "ssm" in phases:
                    pre = ssm_pre(k, I, S, SB, C, cfg, esx, after_alloc=(load_win if "proj" in phases else None))
                    k.barrier()
                elif "proj" in phases:
                    load_win()
                if "proj" in phases:
                    phase_proj(k, I, S, SB, C, cfg)
                    k.barrier()
            if "attn" in phases or "ssm" in phases:
                gens = []
                if "attn" in phases:
                    n_att = NH * sum(NCTX + j + 1 for j in range(NOWN)) + 2
                    gens.append([attn_gen(k, I, S, SB, C, cfg, esx, k.banks[0:3], k.banks[3:5]), n_att, 0])
                if "ssm" in phases:
                    gens.append([ssm_gen(k, I, S, SB, C, cfg, pre, esx, k.banks[5:8]), NBLK * 16 + 12, 0])
                while gens:
                    g = min(gens, key=lambda t: t[2] / t[1])
                    try:
                        next(g[0])
                        g[2] += 1
                    except StopIteration:
                        gens.remove(g)
                k.barrier()
        if "merge" in phases:
            phase_merge(k, I, S, SB, C, cfg)
            k.barrier()
        if "ffn" in phases:
            phase_ffn(k, I, S, SB, C, cfg, out)
            k.barrier()
    return nc


def load_weight_bf16(k, dst, dst_bufs, src, nkt, ncols, stage, stage_bufs, cnt, rows=128):
    CH = 2048
    for kt in range(nkt):
        for c0 in range(0, ncols, CH):
            w = min(CH, ncols - c0)
            k.dma(dst[0:rows, kt, c0:c0 + w], src[kt * rows:(kt + 1) * rows, c0:c0 + w], writes=[dst_bufs[kt]], q="pool")


def phase_proj(k, I, S, SB, C, cfg):
    nc = k.nc
    NBLK, NCTX, NOWN = cfg["NBLK"], cfg["NCTX"], cfg["NOWN"]
    identf, cb = C["identf"], C["cb"]
    with ExitStack() as es:
        def sb(n, shp, dt):
            return es.enter_context(nc.sbuf_tensor(n, shp, dt)), Buf(n)
        win, winb = C["win"], C["winb"]
        wsw, wswb = sb("wsw", [128, 8, 1024], BF16)
        k.op("pool", lambda: nc.gpsimd.memset(wsw[:], 0.0), writes=[wswb])
        for kt in range(8):
            src = win[:, kt, 0:1024].rearrange("p (h d) -> p h d", d=64)
            dst = wsw[:, kt, :].rearrange("p (h d) -> p h d", d=64)
            k.op("dve", lambda dst=dst, src=src: nc.vector.tensor_scalar_mul(dst[:, :, 0:8], src[:, :, 8:16], -1.0),
                 reads=[winb[kt]], writes=[wswb])
            k.op("dve", lambda dst=dst, src=src: nc.vector.tensor_copy(dst[:, :, 8:16], src[:, :, 0:8]),
                 reads=[winb[kt]], writes=[wswb])
        xt = [sb(f"xt{i}", [128, 2, 1024], F32) for i in range(2)]
        xT = [sb(f"xT{i}", [128, 8, 256], BF16) for i in range(2)]
        rc = [sb(f"rc{i}", [128, 256], F32) for i in range(2)]
        rs = [sb(f"rs{i}", [128, 256], F32) for i in range(2)]
        t1 = [sb(f"t1_{i}", [128, 256], F32) for i in range(2)]
        t2 = [sb(f"t2_{i}", [128, 256], F32) for i in range(2)]
        rotk = [sb(f"rotk{i}", [128, 256], F32) for i in range(2)]
        rotq = [sb(f"rotq{i}", [128, 256], F32) for i in range(8)]
        kb16 = [sb(f"kb16_{i}", [128, 256], BF16) for i in range(2)]
        ub = [sb(f"ub{i}", [128, 2, 256], BF16) for i in range(2)]
        vp = [sb(f"vp{i}", [128, 8, 65], BF16) for i in range(2)]
        kmeanT, kmb = sb("kmeanT", [128, 4, 64], F32)
        kms, kmsb = sb("kms", [128, 1], F32)
        gb, gbb = sb("gb", [128, 512], F32)
        sc, scb = sb("sc", [128, 512], F32)
        m8, m8b = sb("m8", [128, 16, 8], F32)
        selt, seltb = sb("selt", [128, 512], F32)
        vm, vmb = sb("vm", [128, 512], F32)
        sbT, sbTb = sb("sbT", [32, 16, 128], BF16)
        k.op("dve", lambda: nc.vector.memset(kmeanT[:], 0.0), writes=[kmb])
        for v_, vb_ in vp:
            k.op("pool", lambda v_=v_: nc.gpsimd.memset(v_[:], 1.0), writes=[vb_])
        ctr = [0]

        def evac_eng():
            ctr[0] += 1
            return "act" if ctr[0] % 2 else "dve"

        def copy_op(e, o, i_):
            if e == "act":
                return lambda: nc.scalar.copy(o, i_)
            if e == "dve":
                return lambda: nc.vector.tensor_copy(o, i_)
            return lambda: nc.gpsimd.tensor_copy(o, i_)

        def fm_group(region, W, wb, col0, xTt, xTb, bankb):
            fns = [_mm(nc, region, W[:, kt, col0:col0 + 128], xTt[:, kt, :], kt == 0, kt == 7) for kt in range(8)]
            k.op("pe", fns, reads=list(wb) + [xTb], writes=[bankb])

        import os
        STOP = int(os.environ.get("PROJ_STOP", "9"))

        def gating_a(i):
            j = i - NCTX
            rq = [rotq[(i % 2) * 4 + p] for p in range(4)]
            bt, bb = k.bank()
            fns = []
            for p in range(4):
                for qs in range(2):
                    idx = p * 2 + qs
                    fns.append(_mm(nc, bt[:, idx * 64:(idx + 1) * 64], rq[p][0][:, qs * 128:(qs + 1) * 128],
                                   kmeanT[:, p, :], True, True))
            k.op("pe", fns, reads=[rq[p][1] for p in range(4)] + [kmb], writes=[bb])
            k.dma(gb[:], I["gbias"][j], writes=[gbb])
            k.op("dve", lambda: nc.vector.tensor_tensor(out=sc[:], in0=bt[:, :], in1=gb[:], op=ALU.add),
                 reads=[bb, gbb], writes=[scb])
            for idx in range(16):
                k.op("dve", lambda idx=idx: nc.vector.max(out=m8[:, idx, :], in_=sc[:, idx * 32:(idx + 1) * 32]),
                     reads=[scb], writes=[m8b])
            sc3 = sc[:, :].rearrange("p (a b) -> p a b", b=32)
            sel3 = selt[:, :].rearrange("p (a b) -> p a b", b=32)
            k.op("dve", lambda: nc.vector.tensor_tensor(out=sel3, in0=sc3, in1=m8[:, :, 2:3].to_broadcast([128, 16, 32]),
                                                        op=ALU.is_ge), reads=[scb, m8b], writes=[seltb])
            k.op("dve", lambda: nc.vector.tensor_scalar(vm[:], gb[:], -1.0, None, op0=ALU.is_ge), reads=[gbb], writes=[vmb])
            k.op("dve", lambda: nc.vector.tensor_tensor(out=selt[:], in0=selt[:], in1=vm[:], op=ALU.mult),
                 reads=[vmb, seltb], writes=[seltb])
            k.op("dve", lambda: nc.vector.tensor_scalar(selt[:], selt[:], 30000.0, -30000.0, op0=ALU.mult, op1=ALU.add),
                 reads=[seltb], writes=[seltb])
            k.op("dve", lambda: nc.vector.memset(sel3[:, :, i:i + 1], 0.0), reads=[seltb], writes=[seltb])

        def gating_b(i):
            j = i - NCTX
            for g in range(4):
                bt2, bb2 = k.bank()
                fns = [(lambda q=q: nc.tensor.transpose(out=bt2[0:32, q * 128:(q + 1) * 128],
                                                        in_=selt[:, (4 * g + q) * 32:(4 * g + q + 1) * 32],
                                                        identity=identf[:])) for q in range(4)]
                k.op("pe", fns, reads=[seltb, cb], writes=[bb2])
                k.op("act", lambda bt2=bt2, g=g: nc.scalar.copy(sbT[:, 4 * g:4 * g + 4, :],
                                                              bt2[0:32, :].rearrange("p (a b) -> p a b", b=128)),
                     reads=[bb2], writes=[sbTb])
            for p in range(4):
                for hh in range(2):
                    k.dma(S["SBT"][2 * p + hh, :, j * 256:(j + 1) * 256].rearrange("n (q t) -> n q t", q=2),
                          sbT[:, 4 * p + hh:4 * p + hh + 3:2, :], reads=[sbTb])
        for i in range(NBLK if STOP > 0 else 0):
            own = i >= NCTX
            j = i - NCTX
            s = i % 2
            xt_t, xt_b = xt[s]
            xT_t, xT_b = xT[s]

            def load_blk(i2):
                s2 = i2 % 2
                k.dma(xt[s2][0][:], I["xcat"][i2 * 256:(i2 + 1) * 256, :].rearrange("(t p) f -> p t f", p=128), writes=[xt[s2][1]])
                k.dma(rc[s2][0][:], I["rot_cos"][:, i2 * 256:(i2 + 1) * 256], writes=[rc[s2][1]])
                k.dma(rs[s2][0][:], I["rot_sin"][:, i2 * 256:(i2 + 1) * 256], writes=[rs[s2][1]])

            if i == 0:
                load_blk(0)
            if i + 1 < NBLK:
                load_blk(i + 1)
            for tt in range(2):
                for g in range(2):
                    bt, bb = k.bank()
                    fns = [(lambda q=q: nc.tensor.transpose(out=bt[:, q * 128:(q + 1) * 128],
                                                            in_=xt_t[:, tt, (4 * g + q) * 128:(4 * g + q + 1) * 128],
                                                            identity=identf[:])) for q in range(4)]
                    k.op("pe", fns, reads=[xt_b, cb], writes=[bb])
                    e = evac_eng()
                    k.op(e, copy_op(e, xT_t[:, 4 * g:4 * g + 4, tt * 128:(tt + 1) * 128],
                                    bt[:, :].rearrange("p (a b) -> p a b", b=128)), reads=[bb], writes=[xT_b])
            if STOP < 2:
                continue
            rot_jobs = [("k", p) for p in range(4)] + ([("q", p) for p in range(4)] if own else [])
            for n_, (kind, p) in enumerate(rot_jobs):
                col0 = (512 if kind == "k" else 0) + p * 128
                bt, bb = k.bank()
                fm_group(bt[:, 0:256], win, winb, col0, xT_t, xT_b, bb)
                fm_group(bt[:, 256:512], wsw, [wswb], col0, xT_t, xT_b, bb)
                a1, a1b = t1[n_ % 2]
                a2, a2b = t2[n_ % 2]
                k.op("dve", lambda a1=a1, bt=bt: nc.vector.tensor_tensor(out=a1[:], in0=bt[:, 0:256], in1=rc[s][0][:], op=ALU.mult),
                     reads=[bb, rc[s][1]], writes=[a1b])
                k.op("dve", lambda a2=a2, bt=bt: nc.vector.tensor_tensor(out=a2[:], in0=bt[:, 256:512], in1=rs[s][0][:], op=ALU.mult),
                     reads=[bb, rs[s][1]], writes=[a2b])
                ro, rob = rotk[p % 2] if kind == "k" else rotq[(i % 2) * 4 + p]
                k.op("pool", lambda ro=ro, a1=a1, a2=a2: nc.gpsimd.tensor_tensor(out=ro[:], in0=a1[:], in1=a2[:], op=ALU.add),
                     reads=[a1b, a2b], writes=[rob])
                o16, o16b = kb16[n_ % 2]
                if kind == "k":
                    k.op("act", lambda o16=o16, ro=ro: nc.scalar.copy(o16[:], ro[:]), reads=[rob], writes=[o16b])
                    k.dma(S["KT"][p, :, i * 256:(i + 1) * 256], o16[:], reads=[o16b])
                    k.op("dve", lambda ro=ro: nc.vector.reduce_sum(out=kms[:], in_=ro[:], axis=AX.X), reads=[rob], writes=[kmsb])
                    k.op("dve", lambda p=p: nc.vector.tensor_scalar_mul(kmeanT[0:64, p, i:i + 1], kms[0:64, :], 1.0 / 256.0),
                         reads=[kmsb], writes=[kmb])
                    k.op("dve", lambda p=p: nc.vector.tensor_scalar_mul(kmeanT[64:128, p, 32 + i:32 + i + 1], kms[64:128, :], 1.0 / 256.0),
                         reads=[kmsb], writes=[kmb])
                else:
                    k.op("act", lambda o16=o16, ro=ro: nc.scalar.mul(o16[:], ro[:], 0.125), reads=[rob], writes=[o16b])
                    k.dma(S["QT"][p, :, j * 256:(j + 1) * 256], o16[:], reads=[o16b])
            if STOP >= 5 and i - 1 >= NCTX:
                gating_a(i - 1)
            if STOP < 3:
                continue
            plain = [("u", 0), ("u", 1)] + ([("g", g) for g in range(8)] if own else [])
            for n_, (kind, g) in enumerate(plain):
                col0 = (1536 + g * 256) if kind == "u" else (2048 + g * 256)
                bt, bb = k.bank()
                fm_group(bt[:, 0:256], win, winb, col0, xT_t, xT_b, bb)
                fm_group(bt[:, 256:512], win, winb, col0 + 128, xT_t, xT_b, bb)
                o, ob = ub[n_ % 2]
                ov = o[:, :, :].rearrange("p a b -> p (a b)")
                if kind == "u":
                    k.op("act", lambda ov=ov, bt=bt: nc.scalar.copy(ov, bt[:, :]), reads=[bb], writes=[ob])
                    k.dma(S["UT"][2 * g:2 * g + 2, :, i * 256:(i + 1) * 256].rearrange("u p t -> p u t"), o[:], reads=[ob])
                else:
                    k.op("act", lambda ov=ov, bt=bt: nc.scalar.activation(out=ov, in_=bt[:, :], func=AF.Sigmoid),
                         reads=[bb], writes=[ob])
                    k.dma(S["SG"][2 * g:2 * g + 2, :, j * 256:(j + 1) * 256].rearrange("u p t -> p u t"), o[:], reads=[ob])
            if STOP < 4:
                continue
            for tt in range(2):
                bt, bb = k.bank()
                fns = [_mm(nc, bt[:, :], xT_t[:, kt, tt * 128:(tt + 1) * 128], win[:, kt, 1024:1536], kt == 0, kt == 7)
                       for kt in range(8)]
                k.op("pe", fns, reads=winb + [xT_b], writes=[bb])
                v_, vb_ = vp[tt]
                e = evac_eng()
                k.op(e, copy_op(e, v_[:, :, 0:64], bt[:, :].rearrange("p (h d) -> p h d", d=64)), reads=[bb], writes=[vb_])
                r0 = (i * 2 + tt) * 128
                k.dma(S["VP"][r0:r0 + 128, :, :], v_[:], reads=[vb_])
            if own and STOP >= 5 and i - 1 >= NCTX:
                gating_b(i - 1)
        if STOP >= 5:
            gating_a(NBLK - 1)
            gating_b(NBLK - 1)


def attn_gen(k, I, S, SB, C, cfg, es, sbanks, obanks):
    nc = k.nc
    NBLK, NCTX, NOWN, T, TO = cfg["NBLK"], cfg["NCTX"], cfg["NOWN"], cfg["T"], cfg["TO"]
    if True:
        def sb(n, shp, dt):
            return es.enter_context(nc.sbuf_tensor(n, shp, dt)), Buf(n)
        ktsb = [sb(f"ktsb{i}", [96, T], BF16) for i in range(1)]
        qtsb = [sb(f"qtsb{i}", [96, TO], BF16) for i in range(1)]
        vpsb = [sb(f"vpsb{i}", [128, NBLK * 2, 65], BF16) for i in range(2)]
        cm, cmb = sb("cm", [128, 512], BF16)
        pT = [sb(f"pT{i}", [128, 512], BF16) for i in range(4)]
        rcp, rcpb = sb("rcp", [128, 2], F32)
        ostg = [sb(f"ostg{i}", [128, 2, 64], BF16) for i in range(2)]
        k.dma(cm[:], I["cmask"], writes=[cmb])
        for kt_t, kt_b in ktsb:
            k.dma(kt_t[64:96, :], I["blkind"], writes=[kt_b])
        NSB = len(sbanks)
        LA = 2
        its = [(h, j, n) for h in range(NH) for j in range(NOWN) for n in range(NCTX + j + 1)]
        hbuf = {}
        jcount = [0]
        jslot = {}

        def load_head(h):
            s = 0
            p, hh = h // 2, h % 2
            kt_t, kt_b = ktsb[s]
            qt_t, qt_b = qtsb[s]
            vp_t, vp_b = vpsb[h % 2]
            k.dma(kt_t[0:64, :], S["KT"][p, hh * 64:(hh + 1) * 64, :], writes=[kt_b])
            k.dma(qt_t[0:64, :], S["QT"][p, hh * 64:(hh + 1) * 64, :], writes=[qt_b])
            k.dma(qt_t[64:96, :], S["SBT"][h], writes=[qt_b])
            vsrc = S["VP"][:, h, :].rearrange("(t p) c -> p t c", p=128)
            for t0 in range(0, NBLK * 2, 8):
                k.dma(vp_t[:, t0:t0 + 8, :], vsrc[:, t0:t0 + 8, :], writes=[vp_b])
            hbuf[h] = (kt_t, kt_b, qt_t, qt_b, vp_t, vp_b)

        def emit_front(idx):
            h, j, n = its[idx]
            if h not in hbuf:
                load_head(h)
            kt_t, kt_b, qt_t, qt_b, vp_t, vp_b = hbuf[h]
            diag = NCTX + j
            bs, bsb = sbanks[idx % NSB]
            fns = [_mm(nc, bs[:, kt * 256:(kt + 1) * 256], kt_t[:, n * 256 + kt * 128:n * 256 + (kt + 1) * 128],
                       qt_t[:, j * 256:(j + 1) * 256], True, True) for kt in range(2)]
            k.op("pe", fns, reads=[kt_b, qt_b], writes=[bsb])
            pt_t, pt_b = pT[idx % 4]
            k.op("act", lambda: nc.scalar.activation(out=pt_t[:], in_=bs[:, :], func=AF.Exp), reads=[bsb], writes=[pt_b])
            if n == diag:
                k.op("pool", lambda: nc.gpsimd.tensor_tensor(out=pt_t[:], in0=pt_t[:], in1=cm[:], op=ALU.mult),
                     reads=[cmb, pt_b], writes=[pt_b])

        def emit_back(idx):
            h, j, n = its[idx]
            kt_t, kt_b, qt_t, qt_b, vp_t, vp_b = hbuf[h]
            diag = NCTX + j
            pt_t, pt_b = pT[idx % 4]
            if n == 0:
                jslot[(h, j)] = jcount[0] % (len(obanks) // 2)
                jcount[0] += 1
            sl = jslot[(h, j)]
            fns = []
            for qs in range(2):
                bo, bob = obanks[sl * 2 + qs]
                fns += [_mm(nc, bo[:, 0:65], pt_t[:, kt * 256 + qs * 128:kt * 256 + (qs + 1) * 128], vp_t[:, n * 2 + kt, :],
                            n == 0 and kt == 0, n == diag and kt == 1) for kt in range(2)]
            k.op("pe", fns, reads=[pt_b, vp_b], writes=[obanks[sl * 2][1], obanks[sl * 2 + 1][1]])
            if n == diag:
                og_t, og_b = ostg[jcount[0] % 2]
                for qs in range(2):
                    bo, bob = obanks[sl * 2 + qs]
                    k.op("dve", lambda bo=bo, qs=qs: nc.vector.reciprocal(rcp[:, qs:qs + 1], bo[:, 64:65]), reads=[bob], writes=[rcpb])
                    k.op("dve", lambda bo=bo, qs=qs: nc.vector.tensor_scalar(
                        og_t[:, qs, :], bo[:, 0:64], rcp[:, qs:qs + 1], None, op0=ALU.mult),
                        reads=[bob, rcpb], writes=[og_b])
                k.dma(S["ATT"][j * 256:(j + 1) * 256, h * 64:(h + 1) * 64].rearrange("(q p) c -> p q c", p=128), og_t[:], reads=[og_b])

        for idx in range(len(its) + LA):
            if idx < len(its):
                emit_front(idx)
            if idx - LA >= 0:
                emit_back(idx - LA)
            yield


def ssm_pre(k, I, S, SB, C, cfg, esp, after_alloc=None):
    nc = k.nc
    NBLK, NCTX, NOWN = cfg["NBLK"], cfg["NCTX"], cfg["NOWN"]
    identf, cb = C["identf"], C["cb"]
    N = 256
    with ExitStack() as es:
        def sb(n, shp, dt):
            return es.enter_context(nc.sbuf_tensor(n, shp, dt)), Buf(n)

        def sbp(n, shp, dt):
            return esp.enter_context(nc.sbuf_tensor(n, shp, dt)), Buf(n)
        P = Buf("ssm_pre")
        mag = sbp("mag", [128, 16], F32)[0]
        BzT = [sbp(f"BzT{i}", [128, 16, 128], BF16)[0] for i in range(2)]
        CT = [sbp(f"CT{i}", [128, 16, 128], BF16)[0] for i in range(3)]
        dcol, _ = sbp("dcol", [128, 4], F32)
        Ec, _ = sbp("Ec", [128, 16, N + 1], F32)
        Es, _ = sbp("Es", [128, 16, N + 1], F32)
        if after_alloc is not None:
            after_alloc()

        def small(n):
            return sb(n, [128, 16], F32)[0]
        a_re, a_im, ldt, dt_, ang = [small(n) for n in ("a_re", "a_im", "ldt", "dt_", "ang")]
        cc, ss, cs, c_, s_ = [small(n) for n in ("cc", "ss", "cs", "c_", "s_")]
        lr, li, den, nre, zr, zi, q1, q2 = [small(n) for n in ("lr", "li", "den", "nre", "zr", "zi", "q1", "q2")]
        Bre, _ = sb("Bre", [128, 16, 16], F32)
        Bim, _ = sb("Bim", [128, 16, 16], F32)
        Bzr, _ = sb("Bzr", [128, 16, 16], F32)
        Bzi, _ = sb("Bzi", [128, 16, 16], F32)
        Bt1, _ = sb("Bt1", [128, 16, 16], F32)
        Bt2, _ = sb("Bt2", [128, 16, 16], F32)
        ZP, _ = sb("ZP", [128, 16, 128], F32)
        tmp = [sb(f"tmpE{i}", [128, 16, 128], F32)[0] for i in range(4)]

        def dv(f, r=(), w=()):
            k.op("dve", f, reads=[P] + list(r), writes=[P] + list(w))

        def tt(o, a, b, op):
            dv(lambda: nc.vector.tensor_tensor(out=o, in0=a, in1=b, op=op))

        def ts(o, a, s1, op0, s2=None, op1=None):
            if op1 is None:
                dv(lambda: nc.vector.tensor_scalar(o, a, s1, None, op0=op0))
            else:
                dv(lambda: nc.vector.tensor_scalar(o, a, s1, s2, op0=op0, op1=op1))

        def act(o, a, func, **kw):
            k.op("act", lambda: nc.scalar.activation(out=o, in_=a, func=func, **kw), reads=[P], writes=[P])

        for gp in range(2):
            rows = slice(gp * 64, (gp + 1) * 64)
            k.dma(a_re[rows, :], I["ssm_a_re"].rearrange("(i two) p -> two p i", two=2)[gp], writes=[P],
                  allow_slow_non_contiguous=True)
            k.dma(a_im[rows, :], I["ssm_a_im"].rearrange("(i two) p -> two p i", two=2)[gp], writes=[P],
                  allow_slow_non_contiguous=True)
            k.dma(ldt[rows, :], I["ssm_log_dt"].rearrange("(i two) -> two i", two=2)[gp].partition_broadcast(64), writes=[P],
                  allow_slow_non_contiguous=True)
            k.dma(Bre[rows, :, :], I["ssm_b_re"].rearrange("(i two) p h -> two p i h", two=2)[gp], writes=[P])
            k.dma(Bim[rows, :, :], I["ssm_b_im"].rearrange("(i two) p h -> two p i h", two=2)[gp], writes=[P])
        k.dma(dcol[:, :], I["ssm_d"].rearrange("(ut gl) h -> (gl h) ut", gl=8), writes=[P], allow_slow_non_contiguous=True)
        act(dt_[:], ldt[:], AF.Exp)
        tt(q1[:], a_re[:], dt_[:], ALU.mult)
        act(mag[:], q1[:], AF.Exp)
        tt(ang[:], a_im[:], dt_[:], ALU.mult)
        act(s_[:], ang[:], AF.Sin, scale=1.0 / 16.0)
        ts(q2[:], ang[:], -1.0 / 16.0, ALU.mult, math.pi / 2.0, ALU.add)
        act(c_[:], q2[:], AF.Sin)
        for _ in range(4):
            tt(cc[:], c_[:], c_[:], ALU.mult)
            tt(ss[:], s_[:], s_[:], ALU.mult)
            tt(cs[:], c_[:], s_[:], ALU.mult)
            tt(c_[:], cc[:], ss[:], ALU.subtract)
            ts(s_[:], cs[:], 2.0, ALU.mult)
        tt(lr[:], mag[:], c_[:], ALU.mult)
        tt(li[:], mag[:], s_[:], ALU.mult)
        tt(q1[:], a_re[:], a_re[:], ALU.mult)
        tt(q2[:], a_im[:], a_im[:], ALU.mult)
        tt(den[:], q1[:], q2[:], ALU.add)
        dv(lambda: nc.vector.reciprocal(den[:], den[:]))
        ts(nre[:], lr[:], -1.0, ALU.add)
        tt(q1[:], nre[:], a_re[:], ALU.mult)
        tt(q2[:], li[:], a_im[:], ALU.mult)
        tt(q1[:], q1[:], q2[:], ALU.add)
        tt(zr[:], q1[:], den[:], ALU.mult)
        tt(q1[:], li[:], a_re[:], ALU.mult)
        tt(q2[:], nre[:], a_im[:], ALU.mult)
        tt(q1[:], q1[:], q2[:], ALU.subtract)
        tt(zi[:], q1[:], den[:], ALU.mult)
        zrb = zr[:, :].unsqueeze(2).to_broadcast([128, 16, 16])
        zib = zi[:, :].unsqueeze(2).to_broadcast([128, 16, 16])
        tt(Bt1[:], Bre[:], zrb, ALU.mult)
        tt(Bt2[:], Bim[:], zib, ALU.mult)
        tt(Bzr[:], Bt1[:], Bt2[:], ALU.subtract)
        tt(Bt1[:], Bim[:], zrb, ALU.mult)
        tt(Bt2[:], Bre[:], zib, ALU.mult)
        tt(Bzi[:], Bt1[:], Bt2[:], ALU.add)

        def transpose16(src, dst, scale=None):
            for g in range(4):
                bt, bb = k.bank()
                fns = [(lambda q=q: nc.tensor.transpose(out=bt[:, q * 128:(q + 1) * 128], in_=src[:, 4 * g + q, :],
                                                        identity=identf[:])) for q in range(4)]
                k.op("pe", fns, reads=[P, cb], writes=[bb])
                o = dst[:, 4 * g:4 * g + 4, :]
                i_ = bt[:, :].rearrange("p (a b) -> p a b", b=128)
                if scale is None:
                    k.op("dve", lambda o=o, i_=i_: nc.vector.tensor_copy(o, i_), reads=[bb, P], writes=[P])
                else:
                    k.op("dve", lambda o=o, i_=i_: nc.vector.tensor_scalar_mul(o, i_, scale), reads=[bb, P], writes=[P])

        for ri, Bz in enumerate((Bzr, Bzi)):
            dv(lambda: nc.vector.memset(ZP[:], 0.0))
            for gp in range(2):
                for kq in range(4):
                    o = ZP[gp * 64:(gp + 1) * 64, kq::4, 32 * kq + 16 * gp:32 * kq + 16 * gp + 16]
                    i_ = Bz[gp * 64:(gp + 1) * 64, kq::4, :]
                    dv(lambda o=o, i_=i_: nc.vector.tensor_copy(o, i_))
            transpose16(ZP, BzT[ri])
        for ri, nm in enumerate(("ssm_c_re", "ssm_c_im")):
            dv(lambda: nc.vector.memset(ZP[:], 0.0))
            for kq in range(4):
                for gp in range(2):
                    r0 = (2 * kq + gp) * 16
                    k.dma(ZP[r0:r0 + 16, kq::4, gp * 64:(gp + 1) * 64],
                          I[nm].rearrange("(ut r) h p -> r h ut p", r=8)[2 * kq + gp], reads=[P], writes=[P])
            transpose16(ZP, CT[ri], scale=(None if ri == 0 else -1.0))
            if ri == 0:
                transpose16(ZP, CT[2], scale=-1.0)
        dv(lambda: nc.vector.memset(Ec[:, :, 0:1], 1.0))
        dv(lambda: nc.vector.memset(Es[:, :, 0:1], 0.0))
        dv(lambda: nc.vector.tensor_copy(Ec[:, :, 1], c_[:]))
        dv(lambda: nc.vector.tensor_copy(Es[:, :, 1], s_[:]))
        m = 1
        while m < N:
            cmb_ = Ec[:, :, m:m + 1].to_broadcast([128, 16, m])
            smb_ = Es[:, :, m:m + 1].to_broadcast([128, 16, m])
            e1c = Ec[:, :, 1:m + 1]
            e1s = Es[:, :, 1:m + 1]
            tA, tB, tC, tD = [t[:, :, 0:m] for t in tmp]
            tt(tA, e1c, cmb_, ALU.mult)
            tt(tB, e1s, smb_, ALU.mult)
            tt(tC, e1c, smb_, ALU.mult)
            tt(tD, e1s, cmb_, ALU.mult)
            tt(Ec[:, :, m + 1:2 * m + 1], tA, tB, ALU.subtract)
            tt(Es[:, :, m + 1:2 * m + 1], tC, tD, ALU.add)
            m *= 2

    return dict(P=P, mag=mag, BzT=BzT, CT=CT, dcol=dcol, Ec=Ec, Es=Es)


def ssm_gen(k, I, S, SB, C, cfg, pre, es, banks):
    nc = k.nc
    NBLK, NCTX, NOWN = cfg["NBLK"], cfg["NCTX"], cfg["NOWN"]
    N = 256
    P, mag, BzT, CT, dcol, Ec, Es = pre["P"], pre["mag"], pre["BzT"], pre["CT"], pre["dcol"], pre["Ec"], pre["Es"]
    bki = [0]

    xbanks = banks[0:len(banks) - 1]

    def nbank():
        bki[0] += 1
        return xbanks[bki[0] % len(xbanks)]
    if True:
        def sb(n, shp, dt):
            return es.enter_context(nc.sbuf_tensor(n, shp, dt)), Buf(n)
        wg_t, _ = sb("wglu", [128, 4, 512], BF16)
        wgb = [Buf(f"wglu{i}") for i in range(4)]
        load_weight_bf16(k, wg_t, wgb, I["w_glu"], 4, 512, None, None, None)
        NS = 3
        uc = [sb(f"uc{i}", [128, 4, N], BF16) for i in range(3)]
        xs = [sb(f"xs{i}", [128, 4, N], F32) for i in range(NS)]
        p1 = [sb(f"p1_{i}", [128, 2, N], F32) for i in range(NS)]
        p2 = [sb(f"p2_{i}", [128, 2, N], F32) for i in range(NS)]
        RBN = 6
        YD = 3
        rb = [sb(f"rb{i}", [128, 2, N], BF16) for i in range(RBN)]
        rb2 = [sb(f"rb2{i}", [128, 2, N], BF16) for i in range(RBN)]
        wi_ = [sb(f"win_{i}", [128, 2, N], F32) for i in range(NS)]
        W, _ = sb("Wst", [128, 16, 2, N], F32)
        wb = [Buf(f"W{i}") for i in range(16)]
        car = [sb(f"car{i}", [128, 16], F32) for i in range(2)]
        cai = [sb(f"cai{i}", [128, 16], F32) for i in range(2)]
        ct = [sb(f"ct{i}", [128, 16], F32) for i in range(4)]
        yv = [sb(f"yv{i}", [128, N], F32) for i in range(2)]
        ygf = [sb(f"ygf{i}", [128, 4, N], F32) for i in range(2)]
        ygb16, ygbb = sb("ygb16", [128, 4, N], BF16)
        sg = [sb(f"sg{i}", [128, N], F32) for i in range(2)]
        so, sob = sb("so", [128, 4, N], BF16)
        k.op("dve", lambda: nc.vector.memset(car[0][0][:], 0.0), writes=[car[0][1]])
        k.op("dve", lambda: nc.vector.memset(cai[0][0][:], 0.0), writes=[cai[0][1]])

        def load_u(c):
            u_t, u_b = uc[c % 3]
            k.dma(u_t[:], S["UT"][:, :, c * N:(c + 1) * N].rearrange("u p t -> p u t"), writes=[u_b])

        def in_a(g):
            c, i = divmod(g, 16)
            own = c >= NCTX
            if i == 0 and c + 1 < NBLK:
                load_u(c + 1)
            u_t, u_b = uc[c % 3]
            bt, bb = nbank()
            fns = [_mm(nc, bt[:, 0:N], BzT[0][:, i, :], u_t[:, i // 4, :], True, True),
                   _mm(nc, bt[:, N:2 * N], BzT[1][:, i, :], u_t[:, i // 4, :], True, True)]
            k.op("pe", fns, reads=[P, u_b], writes=[bb])
            x_t, x_b = xs[g % NS]
            k.op("act", lambda: nc.scalar.copy(x_t[:, 0:2, :].rearrange("p a b -> p (a b)"), bt[:, :]), reads=[bb], writes=[x_b])
            k.op("act", lambda: nc.scalar.copy(x_t[:, 2, :], bt[:, N:2 * N]), reads=[bb], writes=[x_b])
            k.op("act", lambda: nc.scalar.mul(x_t[:, 3, :], bt[:, 0:N], -1.0), reads=[bb], writes=[x_b])
            ecb = Ec[:, i:i + 1, 0:N].to_broadcast([128, 2, N])
            esb = Es[:, i:i + 1, 0:N].to_broadcast([128, 2, N])
            a_t, a_b = p1[g % NS]
            b_t, b_b = p2[g % NS]
            k.op("pool", lambda: nc.gpsimd.tensor_tensor(out=a_t[:], in0=x_t[:, 0:2, :], in1=ecb, op=ALU.mult),
                 reads=[x_b, P], writes=[a_b])
            k.op("pool", lambda: nc.gpsimd.tensor_tensor(out=b_t[:], in0=x_t[:, 2:4, :], in1=esb, op=ALU.mult),
                 reads=[x_b, P], writes=[b_b])

        def in_b(g):
            a_t, a_b = p1[g % NS]
            b_t, b_b = p2[g % NS]
            w_t, w_b = wi_[g % NS]
            k.op("dve", lambda: nc.vector.tensor_tensor(out=w_t[:], in0=a_t[:], in1=b_t[:], op=ALU.add),
                 reads=[a_b, b_b], writes=[w_b])

        def in_c(g):
            c, i = divmod(g, 16)
            cr_t, cr_b = car[c % 2]
            ci_t, ci_b = cai[c % 2]
            w_t, w_b = wi_[g % NS]
            magb = mag[:, i:i + 1].to_broadcast([128, N])
            k.op("dve", lambda: nc.vector.tensor_tensor_scan(
                out=W[:, i, 0, :], data0=magb, data1=w_t[:, 0, :], initial=cr_t[:, i:i + 1], op0=ALU.mult, op1=ALU.add),
                reads=[w_b, cr_b, P], writes=[wb[i]])
            k.op("dve", lambda: nc.vector.tensor_tensor_scan(
                out=W[:, i, 1, :], data0=magb, data1=w_t[:, 1, :], initial=ci_t[:, i:i + 1], op0=ALU.mult, op1=ALU.add),
                reads=[w_b, ci_b, P], writes=[wb[i]])
            if i == 15:
                ncr_t, ncr_b = car[(c + 1) % 2]
                nci_t, nci_b = cai[(c + 1) % 2]
                wlr = W[:, :, 0, N - 1]
                wli = W[:, :, 1, N - 1]
                enc = Ec[:, :, N]
                ens = Es[:, :, N]
                cts = [t[0] for t in ct]
                ctb = ct[0][1]
                k.op("dve", lambda: nc.vector.tensor_tensor(out=cts[0][:], in0=wlr, in1=enc, op=ALU.mult), reads=wb + [P], writes=[ctb])
                k.op("dve", lambda: nc.vector.tensor_tensor(out=cts[1][:], in0=wli, in1=ens, op=ALU.mult), reads=wb + [P, ctb], writes=[ctb])
                k.op("dve", lambda: nc.vector.tensor_tensor(out=cts[2][:], in0=wlr, in1=ens, op=ALU.mult), reads=wb + [P, ctb], writes=[ctb])
                k.op("dve", lambda: nc.vector.tensor_tensor(out=cts[3][:], in0=wli, in1=enc, op=ALU.mult), reads=wb + [P, ctb], writes=[ctb])
                k.op("dve", lambda: nc.vector.tensor_tensor(out=ncr_t[:], in0=cts[0][:], in1=cts[1][:], op=ALU.subtract),
                     reads=[ctb], writes=[ncr_b])
                k.op("dve", lambda: nc.vector.tensor_tensor(out=nci_t[:], in0=cts[2][:], in1=cts[3][:], op=ALU.add),
                     reads=[ctb], writes=[nci_b])

        def out_p(g):
            c, i = divmod(g, 16)
            b_t, b_b = rb[g % RBN]
            b2_t, b2_b = rb2[g % RBN]
            ecb = Ec[:, i:i + 1, 0:N].to_broadcast([128, 2, N])
            esb = Es[:, i:i + 1, 0:N].to_broadcast([128, 2, N])
            k.op("pool", lambda: nc.gpsimd.tensor_tensor(out=b_t[:], in0=W[:, i, :, :], in1=ecb, op=ALU.mult),
                 reads=[wb[i], P], writes=[b_b])
            k.op("dve", lambda: nc.vector.tensor_tensor(out=b2_t[:], in0=W[:, i, :, :], in1=esb, op=ALU.mult),
                 reads=[wb[i], P], writes=[b2_b])

        def out_y(g):
            c, i = divmod(g, 16)
            jc = c - NCTX
            u_t, u_b = uc[c % 3]
            b_t, b_b = rb[g % RBN]
            b2_t, b2_b = rb2[g % RBN]
            yg_t, yg_b = ygf[c % 2]
            ut, kq = i // 4, i % 4
            bt, bb = banks[-1]
            fns = [_mm(nc, bt[:, 0:N], CT[0][:, i, :], b_t[:, 0, :], kq == 0, False),
                   _mm(nc, bt[:, 0:N], CT[1][:, i, :], b_t[:, 1, :], False, False),
                   _mm(nc, bt[:, 0:N], CT[1][:, i, :], b2_t[:, 0, :], False, False),
                   _mm(nc, bt[:, 0:N], CT[2][:, i, :], b2_t[:, 1, :], False, kq == 3)]
            k.op("pe", fns, reads=[P, b_b, b2_b], writes=[bb])
            if kq == 3:
                y_t, y_b = yv[ut % 2]
                k.op("dve", lambda: nc.vector.scalar_tensor_tensor(
                    out=y_t[:], in0=u_t[:, ut, :], scalar=dcol[:, ut:ut + 1], in1=bt[:, 0:N], op0=ALU.mult, op1=ALU.add),
                    reads=[bb, u_b, P], writes=[y_b])
                k.op("act", lambda: nc.scalar.activation(out=yg_t[:, ut, :], in_=y_t[:], func=AF.Gelu_apprx_tanh),
                     reads=[y_b], writes=[yg_b])

        def glu(c):
            jc = c - NCTX
            yg_t, yg_b = ygf[c % 2]
            k.op("pool", lambda: nc.gpsimd.tensor_copy(ygb16[:], yg_t[:]), reads=[yg_b], writes=[ygbb])
            for mt in range(4):
                bt, bb = nbank()
                fns = [_mm(nc, bt[:, 0:N], wg_t[:, ut, mt * 128:(mt + 1) * 128], ygb16[:, ut, :], ut == 0, ut == 3) for ut in range(4)]
                k.op("pe", fns, reads=wgb + [ygbb], writes=[bb])
                s_t, s_b = sg[mt % 2]
                k.op("act", lambda s_t=s_t, bt=bt: nc.scalar.activation(out=s_t[:], in_=bt[:, 0:N], func=AF.Sigmoid),
                     reads=[bb], writes=[s_b])
                k.op("dve", lambda mt=mt, s_t=s_t: nc.vector.tensor_tensor(out=so[:, mt, :], in0=yg_t[:, mt, :], in1=s_t[:], op=ALU.mult),
                     reads=[s_b, yg_b], writes=[sob])
            k.dma(S["SSMY"][:, :, jc * N:(jc + 1) * N].rearrange("u p t -> p u t"), so[:], reads=[sob])

        G = NBLK * 16
        g0own = NCTX * 16
        load_u(0)
        GD = 4
        for step in range(G + 3 + YD + GD + 1):
            if step < G:
                in_a(step)
            if 0 <= step - 1 < G:
                in_b(step - 1)
            if 0 <= step - 2 < G:
                in_c(step - 2)
            g = step - 3
            if g0own <= g < G:
                out_p(g)
            g = step - 3 - YD
            if g0own <= g < G:
                out_y(g)
            g = step - 3 - YD - GD
            if g0own <= g < G and g % 16 == 15:
                glu(g // 16)
            yield


def ln_tile(k, ts_t, ts_b, g_t, b_t, gbuf, o_t, o_b, st, mv, lb):
    nc = k.nc
    st_t, mv_t = st, mv
    k.op("dve", lambda: nc.vector.bn_stats(out=st_t[:, 0, :], in_=ts_t[:, 0:512]), reads=[ts_b], writes=[lb])
    k.op("dve", lambda: nc.vector.bn_stats(out=st_t[:, 1, :], in_=ts_t[:, 512:1024]), reads=[ts_b, lb], writes=[lb])
    k.op("dve", lambda: nc.vector.bn_aggr(out=mv_t[:, 0:2], in_=st_t[:, :, :].rearrange("p a b -> p (a b)")), reads=[lb], writes=[lb])
    k.op("dve", lambda: nc.vector.tensor_scalar(mv_t[:, 2:3], mv_t[:, 1:2], LN_EPS, None, op0=ALU.add), reads=[lb], writes=[lb])
    k.op("act", lambda: nc.scalar.sqrt(mv_t[:, 3:4], mv_t[:, 2:3]), reads=[lb], writes=[lb])
    k.op("dve", lambda: nc.vector.reciprocal(mv_t[:, 4:5], mv_t[:, 3:4]), reads=[lb], writes=[lb])
    k.op("dve", lambda: nc.vector.tensor_scalar(o_t, ts_t[:, :], mv_t[:, 0:1], mv_t[:, 4:5], op0=ALU.subtract, op1=ALU.mult),
         reads=[ts_b, lb], writes=[o_b])
    k.op("dve", lambda: nc.vector.tensor_tensor(out=o_t, in0=o_t, in1=g_t[:], op=ALU.mult), reads=[gbuf, o_b], writes=[o_b])
    k.op("dve", lambda: nc.vector.tensor_tensor(out=o_t, in0=o_t, in1=b_t[:], op=ALU.add), reads=[gbuf, o_b], writes=[o_b])


def phase_merge(k, I, S, SB, C, cfg):
    nc = k.nc
    NBLK, NCTX, NOWN = cfg["NBLK"], cfg["NCTX"], cfg["NOWN"]
    identb, cb = C["identb"], C["cb"]
    with ExitStack() as es:
        def sb(n, shp, dt):
            return es.enter_context(nc.sbuf_tensor(n, shp, dt)), Buf(n)
        wap, _ = sb("wap", [128, 4, 1024], BF16)
        wsp, _ = sb("wsp", [128, 4, 1024], BF16)
        wout, _ = sb("wout", [128, 8, 1024], BF16)
        wapb = [Buf(f"wap{i}") for i in range(4)]
        wspb = [Buf(f"wsp{i}") for i in range(4)]
        woutb = [Buf(f"wout{i}") for i in range(8)]
        load_weight_bf16(k, wap, wapb, I["w_attn_proj"], 4, 1024, None, None, None)
        load_weight_bf16(k, wsp, wspb, I["w_ssm_proj"], 4, 1024, None, None, None)
        load_weight_bf16(k, wout, woutb, I["w_out"], 8, 1024, None, None, None)
        g1, gbuf = sb("g1", [128, 1024], F32)
        b1, _ = sb("b1", [128, 1024], F32)
        k.dma(g1[:], I["ln1_g"].partition_broadcast(128), writes=[gbuf])
        k.dma(b1[:], I["ln1_b"].partition_broadcast(128), writes=[gbuf])
        att = [sb(f"att{i}", [128, 2, 512], BF16) for i in range(2)]
        attT = [sb(f"attT{i}", [128, 4, 256], BF16) for i in range(2)]
        ssmy = [sb(f"ssmy{i}", [128, 4, 256], BF16) for i in range(2)]
        sgt = [sb(f"sgt{i}", [128, 16, 256], BF16) for i in range(2)]
        xo = [sb(f"xo{i}", [128, 2, 1024], F32) for i in range(2)]
        tA = [sb(f"tA{i}", [128, 256], F32) for i in range(2)]
        tB = [sb(f"tB{i}", [128, 256], F32) for i in range(2)]
        mg = [sb(f"mg{i}", [128, 8, 256], BF16) for i in range(2)]
        tsum = [sb(f"tsum{i}", [128, 1024], F32) for i in range(2)]
        h1 = [sb(f"h1_{i}", [128, 1024], F32) for i in range(2)]
        st, lb = sb("lnst", [128, 2, 6], F32)
        mv, _ = sb("lnmv", [128, 8], F32)
        def load_m(j):
            s = j % 2
            k.dma(att[s][0][:], S["ATT"][j * 256:(j + 1) * 256, :].rearrange("(t p) c -> p t c", p=128), writes=[att[s][1]])
            k.dma(ssmy[s][0][:], S["SSMY"][:, :, j * 256:(j + 1) * 256].rearrange("u p t -> p u t"), writes=[ssmy[s][1]])
            for u0 in range(0, 16, 4):
                k.dma(sgt[s][0][:, u0:u0 + 4, :], S["SG"][u0:u0 + 4, :, j * 256:(j + 1) * 256].rearrange("u p t -> p u t"),
                      writes=[sgt[s][1]])
            k.dma(xo[s][0][:], I["xcat"][(NCTX + j) * 256:(NCTX + j + 1) * 256, :].rearrange("(t p) f -> p t f", p=128),
                  writes=[xo[s][1]])

        load_m(0)
        for j in range(NOWN):
            s = j % 2
            if j + 1 < NOWN:
                load_m(j + 1)
            for tt in range(2):
                bt, bb = k.bank()
                bv = bt[:, :].bitcast(BF16)
                fns = [(lambda q=q: nc.tensor.transpose(out=bv[:, q * 128:(q + 1) * 128], in_=att[s][0][:, tt, q * 128:(q + 1) * 128],
                                                        identity=identb[:])) for q in range(4)]
                k.op("pe", fns, reads=[att[s][1], cb], writes=[bb])
                k.op("act", lambda bv=bv, tt=tt: nc.scalar.copy(attT[s][0][:, :, tt * 128:(tt + 1) * 128],
                                                                 bv[:, 0:512].rearrange("p (a b) -> p a b", b=128)),
                     reads=[bb], writes=[attT[s][1]])
            for m in range(8):
                bt, bb = k.bank()
                fa = [_mm(nc, bt[:, 0:256], wap[:, kt, m * 128:(m + 1) * 128], attT[s][0][:, kt, :], kt == 0, kt == 3) for kt in range(4)]
                k.op("pe", fa, reads=wapb + [attT[s][1]], writes=[bb])
                fb = [_mm(nc, bt[:, 256:512], wsp[:, kt, m * 128:(m + 1) * 128], ssmy[s][0][:, kt, :], kt == 0, kt == 3) for kt in range(4)]
                k.op("pe", fb, reads=wspb + [ssmy[s][1]], writes=[bb])
                a_t, a_b = tA[m % 2]
                b_t, b_b = tB[m % 2]
                k.op("dve", lambda a_t=a_t, bt=bt, m=m: nc.vector.tensor_tensor(out=a_t[:], in0=bt[:, 0:256], in1=sgt[s][0][:, m, :], op=ALU.mult),
                     reads=[bb, sgt[s][1]], writes=[a_b])
                k.op("dve", lambda b_t=b_t, bt=bt, m=m: nc.vector.tensor_tensor(out=b_t[:], in0=bt[:, 256:512], in1=sgt[s][0][:, 8 + m, :], op=ALU.mult),
                     reads=[bb, sgt[s][1]], writes=[b_b])
                k.op("dve", lambda a_t=a_t, b_t=b_t, m=m: nc.vector.tensor_tensor(out=mg[s][0][:, m, :], in0=a_t[:], in1=b_t[:], op=ALU.add),
                     reads=[a_b, b_b], writes=[mg[s][1]])
            for tt in range(2):
                ts_t, ts_b = tsum[tt]
                for half in range(2):
                    bt, bb = k.bank()
                    fns = [_mm(nc, bt[:, :], mg[s][0][:, kt, tt * 128:(tt + 1) * 128], wout[:, kt, half * 512:(half + 1) * 512], kt == 0, kt == 7)
                           for kt in range(8)]
                    k.op("pe", fns, reads=woutb + [mg[s][1]], writes=[bb])
                    k.op("dve", lambda bt=bt, tt=tt, half=half, ts_t=ts_t: nc.vector.scalar_tensor_tensor(
                        out=ts_t[:, half * 512:(half + 1) * 512], in0=xo[s][0][:, tt, half * 512:(half + 1) * 512], scalar=ALPHA,
                        in1=bt[:, :], op0=ALU.mult, op1=ALU.add), reads=[bb, xo[s][1]], writes=[ts_b])
                h_t, h_b = h1[tt]
                ln_tile(k, ts_t, ts_b, g1, b1, gbuf, h_t[:, :], h_b, st, mv, lb)
                r0 = j * 256 + tt * 128
                k.dma(S["H1"][r0:r0 + 128, :], h_t[:], reads=[h_b])


def phase_ffn(k, I, S, SB, C, cfg, out):
    nc = k.nc
    NBLK, NCTX, NOWN = cfg["NBLK"], cfg["NCTX"], cfg["NOWN"]
    identf, cb = C["identf"], C["cb"]
    NF = 2 * FF // 128
    NP = FF // 128
    with ExitStack() as es:
        def sb(n, shp, dt):
            return es.enter_context(nc.sbuf_tensor(n, shp, dt)), Buf(n)
        wup, _ = sb("wup", [128, 8, 2 * FF], BF16)
        wdn, _ = sb("wdn", [128, NP, 1024], BF16)
        wupb = [Buf(f"wup{i}") for i in range(8)]
        wdnb = [Buf(f"wdn{i}") for i in range(NP)]
        load_weight_bf16(k, wup, wupb, I["w_up"], 8, 2 * FF, None, None, None)
        load_weight_bf16(k, wdn, wdnb, I["w_down"], NP, 1024, None, None, None)
        g2, gbuf = sb("g2", [128, 1024], F32)
        b2, _ = sb("b2", [128, 1024], F32)
        cw, _ = sb("cw", [128, NF, 3], F32)
        cbi, _ = sb("cbi", [128, NF], F32)
        hp, _ = sb("hp", [128, 1], F32)
        k.dma(g2[:], I["ln2_g"].partition_broadcast(128), writes=[gbuf])
        k.dma(b2[:], I["ln2_b"].partition_broadcast(128), writes=[gbuf])
        for jj in range(3):
            for f0 in range(0, NF, 11):
                k.dma(cw[:, f0:f0 + 11, jj], I["conv_w"][jj].rearrange("(f p) -> p f", p=128)[:, f0:f0 + 11], writes=[gbuf],
                      allow_slow_non_contiguous=True)
        for f0 in range(0, NF, 11):
            k.dma(cbi[:, f0:f0 + 11], I["conv_b"].rearrange("(f p) -> p f", p=128)[:, f0:f0 + 11], writes=[gbuf],
                  allow_slow_non_contiguous=True)
        k.dma(hp[:, :], I["has_prev"], writes=[gbuf])
        halo, halob = sb("halo", [128, NF, 2], F32)
        h1 = [sb(f"h1f{i}", [128, 2, 1024], F32) for i in range(2)]
        h1Ts = [sb(f"h1T{i}", [128, 8, 256], BF16) for i in range(2)]
        CVN = 3
        raw = [sb(f"raw{i}", [128, 2, 258], F32) for i in range(CVN)]
        cv = [sb(f"cv{i}", [128, 2, 256], F32) for i in range(CVN)]
        gl = [sb(f"gl{i}", [128, 256], F32) for i in range(2)]
        actT, _ = sb("actT", [128, NP, 256], BF16)
        actTb = [Buf(f"actT{p}") for p in range(NP)]
        ptmp = [sb(f"ptmp{i}", [128, 256], F32) for i in range(2)]
        tsums = [sb(f"tsumf{i}", [128, 1024], F32) for i in range(2)]
        o_, ob = sb("of", [128, 1024], F32)
        st, lb = sb("lnst2", [128, 2, 6], F32)
        mv, _ = sb("lnmv2", [128, 8], F32)
        upbanks = k.banks[0:4]
        accbanks = k.banks[4:8]
        ubi = [0]

        def upbank():
            ubi[0] += 1
            return upbanks[ubi[0] % 4]

        def load_h1(j):
            h_t, h_b = h1[j % 2]
            k.dma(h_t[:], S["H1"][j * 256:(j + 1) * 256, :].rearrange("(t p) f -> p t f", p=128), writes=[h_b])

        LAG = 9
        load_h1(0)
        if NOWN > 1:
            load_h1(1)

        def transposes(j):
            h_t, h_b = h1[j % 2]
            hT, hTb = h1Ts[j % 2]
            for tt in range(2):
                for g in range(2):
                    bt, bb = upbank()
                    fns = [(lambda q=q: nc.tensor.transpose(out=bt[:, q * 128:(q + 1) * 128],
                                                            in_=h_t[:, tt, (4 * g + q) * 128:(4 * g + q + 1) * 128],
                                                            identity=identf[:])) for q in range(4)]
                    k.op("pe", fns, reads=[h_b, cb], writes=[bb])
                    k.op("act", lambda bt=bt, g=g, tt=tt: nc.scalar.copy(hT[:, 4 * g:4 * g + 4, tt * 128:(tt + 1) * 128],
                                                                      bt[:, :].rearrange("p (a b) -> p a b", b=128)),
                         reads=[bb], writes=[hTb])

        transposes(0)
        for j in range(1):
            s = j % 2
            h_t, h_b = h1[s]
            h1T, h1Tb = h1Ts[s]
            if j == 0:
                bt, bb = upbank()
                for f in range(NF):
                    fns = [_mm(nc, bt[:, 2 * f:2 * f + 2], wup[:, kt, f * 128:(f + 1) * 128], h1T[:, kt, 254:256], kt == 0, kt == 7)
                           for kt in range(8)]
                    k.op("pe", fns, reads=wupb + [h1Tb], writes=[bb])
                k.op("dve", lambda bt=bt: nc.vector.tensor_scalar(halo[:, :, :].rearrange("p a b -> p (a b)"), bt[:, 0:2 * NF],
                                                                  hp[:, 0:1], None, op0=ALU.mult),
                     reads=[bb, gbuf], writes=[halob])
                if NOWN > 1:
                    transposes(1)
                if 2 < NOWN:
                    load_h1(2)
                continue

        G = (NOWN - 1) * NP
        halobs = [Buf(f"halo{p}") for p in range(NP)]
        for hb_ in halobs:
            hb_.w = halob.w

        def halo_in(gs):
            p = gs % NP
            r_t, r_b = raw[gs % CVN]
            k.op("pool", lambda: nc.gpsimd.tensor_copy(r_t[:, :, 0:2], halo[:, p::NP, :]), reads=[halobs[p]], writes=[r_b])

        def up_stage(j, p, gs):
            h1T, h1Tb = h1Ts[j % 2]
            bt, bb = upbank()
            for v, f in enumerate((p, NP + p)):
                fns = [_mm(nc, bt[:, v * 256:(v + 1) * 256], wup[:, kt, f * 128:(f + 1) * 128], h1T[:, kt, :], kt == 0, kt == 7)
                       for kt in range(8)]
                k.op("pe", fns, reads=wupb + [h1Tb], writes=[bb])
            r_t, r_b = raw[gs % CVN]
            c_t, c_b = cv[gs % CVN]
            if gs == 0:
                halo_in(gs)
            k.op("act", lambda: nc.scalar.copy(r_t[:, :, 2:258], bt[:, :].rearrange("p (a b) -> p a b", b=256)),
                 reads=[bb], writes=[r_b])
            for v, f in enumerate((p, NP + p)):
                k.op("act", lambda v=v, f=f: nc.scalar.activation(out=c_t[:, v, :], in_=bt[:, v * 256:(v + 1) * 256],
                                                                  func=AF.Identity, scale=cw[:, f, 2:3], bias=cbi[:, f:f + 1]),
                     reads=[bb, gbuf], writes=[c_b])
            k.op("pool", lambda: nc.gpsimd.tensor_copy(halo[:, p::NP, :], r_t[:, :, 256:258]), reads=[r_b], writes=[halobs[p]])
            if gs + 1 < G:
                halo_in(gs + 1)
            for jj in (1, 0):
                k.op("dve", lambda jj=jj: nc.vector.scalar_tensor_tensor(
                    out=c_t[:, 0, :], in0=r_t[:, 0, jj:jj + 256], scalar=cw[:, p, jj:jj + 1], in1=c_t[:, 0, :],
                    op0=ALU.mult, op1=ALU.add), reads=[r_b, gbuf, c_b], writes=[c_b])
            k.op("dve", lambda: nc.vector.scalar_tensor_tensor(
                out=c_t[:, 1, :], in0=r_t[:, 1, 1:257], scalar=cw[:, NP + p, 1:2], in1=c_t[:, 1, :],
                op0=ALU.mult, op1=ALU.add), reads=[r_b, gbuf, c_b], writes=[c_b])
            k.op("dve", lambda: nc.vector.scalar_tensor_tensor(
                out=c_t[:, 1, :], in0=r_t[:, 1, 0:256], scalar=cw[:, NP + p, 0:1], in1=c_t[:, 1, :],
                op0=ALU.mult, op1=ALU.add), reads=[r_b, gbuf, c_b], writes=[c_b])

        def up_stage_b(j, p, gs):
            c_t, c_b = cv[gs % CVN]
            g_t, g_b = gl[gs % 2]
            k.op("act", lambda: nc.scalar.activation(out=g_t[:], in_=c_t[:, 1, :], func=AF.Gelu_apprx_tanh),
                 reads=[c_b], writes=[g_b])
            k.op("dve", lambda: nc.vector.tensor_tensor(out=actT[:, p, :], in0=g_t[:], in1=c_t[:, 0, :], op=ALU.mult),
                 reads=[g_b, c_b], writes=[actTb[p]])

        def down_stage(j, p):
            fns = []
            for tt in range(2):
                for half in range(2):
                    bt, bb = accbanks[tt * 2 + half]
                    fns.append(_mm(nc, bt[:, :], actT[:, p, tt * 128:(tt + 1) * 128], wdn[:, p, half * 512:(half + 1) * 512],
                                   p == 0, p == NP - 1))
            k.op("pe", fns, reads=[wdnb[p], actTb[p]], writes=[b_[1] for b_ in accbanks])
            if p == NP - 1:
                tail(j)

        def tail(j):
            h_t, h_b = h1[j % 2]
            for tt in range(2):
                ts_t, ts_b = tsums[tt]
                for half in range(2):
                    bt, bb = accbanks[tt * 2 + half]
                    k.op("dve", lambda bt=bt, tt=tt, half=half, ts_t=ts_t: nc.vector.scalar_tensor_tensor(
                        out=ts_t[:, half * 512:(half + 1) * 512], in0=h_t[:, tt, half * 512:(half + 1) * 512], scalar=ALPHA,
                        in1=bt[:, :], op0=ALU.mult, op1=ALU.add), reads=[bb, h_b], writes=[ts_b])
            for tt in range(2):
                ts_t, ts_b = tsums[tt]
                ln_tile(k, ts_t, ts_b, g2, b2, gbuf, o_[:, :], ob, st, mv, lb)
                r0 = (j - 1) * 256 + tt * 128
                k.dma(out[r0:r0 + 128, :], o_[:], reads=[ob])
            if j + 2 < NOWN:
                load_h1(j + 2)

        for gs in range(G + LAG + 2):
            if gs < G:
                j, p = 1 + gs // NP, gs % NP
                if p == NP - 8 and j + 1 < NOWN:
                    transposes(j + 1)
                up_stage(j, p, gs)
            g1_ = gs - 1
            if 0 <= g1_ < G:
                up_stage_b(1 + g1_ // NP, g1_ % NP, g1_)
            g2_ = gs - 1 - LAG
            if 0 <= g2_ < G:
                down_stage(1 + g2_ // NP, g2_ % NP)


_CACHE = {}


def _consts(NBLK, NCTX, half, full):
    T = NBLK * 256
    NOWN = NBLK - NCTX
    off = 0 if (half == 1 or not full) else -(NBLK // 2) * 256
    if not full:
        off = 0
    pos = (np.arange(T, dtype=np.float64) + off)
    inv = 500000.0 ** (-np.arange(0, 16, 2, dtype=np.float64) / 16.0)
    ang = pos[None, :] * inv[:, None]
    cos = np.ones((64, T)); sin = np.zeros((64, T))
    cos[0:8] = np.cos(ang); cos[8:16] = np.cos(ang)
    sin[0:8] = np.sin(ang); sin[8:16] = np.sin(ang)
    rot_cos = np.concatenate([cos, cos], 0).astype(np.float32)
    rot_sin = np.concatenate([sin, sin], 0).astype(np.float32)
    first_valid = 0 if (half == 1 or not full) else NBLK // 2
    gb = np.full((NOWN, 32), -1e30, np.float32)
    for j in range(NOWN):
        own = NCTX + j
        for n in range(min(own, 32)):
            if n >= first_valid:
                gb[j, n] = 0.0
    gbias = np.ascontiguousarray(np.broadcast_to(np.tile(gb[:, None, :], (1, 16, 1)).reshape(NOWN, 1, 512), (NOWN, 128, 512)))
    hasp = np.full((128, 1), 1.0 if (half == 1 or not full) else 0.0, np.float32)
    key = np.arange(128)[:, None]
    q = np.arange(256)[None, :]
    cm = np.concatenate([(key <= q), (key + 128 <= q)], 1).astype(np.float32)
    blk = np.zeros((32, T), np.float32)
    for n in range(min(NBLK, 32)):
        blk[n, n * 256:(n + 1) * 256] = 1.0
    return dict(rot_cos=rot_cos, rot_sin=rot_sin, gbias=gbias, has_prev=hasp,
                ident=np.eye(128, dtype=np.float32), cmask=cm.astype(ml_dtypes.bfloat16), blkind=blk.astype(ml_dtypes.bfloat16))


def kernel(**inputs):
    x = np.asarray(inputs["x"], np.float32)
    B, SEQ, _ = x.shape
    NBLK, NCTX = 32, 15
    if "nc" not in _CACHE:
        nc = bass.Bass("TRN2", target_bir_lowering=False)
        build(nc, NBLK, NCTX)
        _CACHE["nc"] = nc
    nc = _CACHE["nc"]
    wnames = ["w_in", "w_attn_proj", "ssm_a_re", "ssm_a_im", "ssm_log_dt", "ssm_b_re", "ssm_b_im", "ssm_c_re", "ssm_c_im",
              "ssm_d", "w_glu", "w_ssm_proj", "w_out", "ln1_g", "ln1_b", "w_up", "conv_w", "conv_b", "w_down", "ln2_g", "ln2_b"]
    w = {n: np.ascontiguousarray(np.asarray(inputs[n], np.float32)[0]) for n in wnames}
    in_maps = []
    for c in range(8):
        b, half = c // 2, c % 2
        if half == 1:
            xcat = np.ascontiguousarray(x[b])
        else:
            xcat = np.concatenate([np.zeros((SEQ // 2, D), np.float32), x[b, :SEQ // 2]], 0)
        m = dict(w)
        m["xcat"] = xcat
        m.update(_consts(NBLK, NCTX, half, True))
        in_maps.append(m)
    res = run_bass_kernel_spmd(nc, in_maps, core_ids=list(range(8)))
    outp = np.empty((B, SEQ, D), np.float32)
    for c in range(8):
        b, half = c // 2, c % 2
        outp[b, half * (SEQ // 2):(half + 1) * (SEQ // 2)] = res.results[c]["out"]
    return outp
```

```python
import math
from contextlib import ExitStack

import numpy as np
import ml_dtypes
import concourse.bass as bass
import concourse.mybir as mybir
from concourse.bass_utils import run_bass_kernel_spmd

F32 = mybir.dt.float32
BF16 = mybir.dt.bfloat16
AF = mybir.ActivationFunctionType
ALU = mybir.AluOpType
AX = mybir.AxisListType

D = 1024
NH = 8
DH = 64
FF = 2816
ALPHA = 2.0 ** 0.25
LN_EPS = 1e-5
GC = 0.7978845608028654


class Buf:
    __slots__ = ("name", "w", "r", "excl")

    def __init__(self, name, excl=False):
        self.name = name
        self.w = None
        self.r = []
        self.excl = excl


class KB:
    def __init__(self, nc, es):
        self.nc = nc
        self.es = es
        self.engs = {"pe": nc.tensor, "act": nc.scalar, "dve": nc.vector, "pool": nc.gpsimd, "sp": nc.sync}
        self.cur = {}
        self.waited = {e: {} for e in self.engs}
        self.nsem = 0
        self.rings = {}
        self.ridx = {}
        self.last = {}
        self.banks = []
        self.bank_i = 0
        self.ninst = 0

    def newsem(self, name):
        self.nsem += 1
        return self.es.enter_context(self.nc.semaphore(f"{name}{self.nsem}"))

    def _wait(self, e, tok):
        sem, val, _ = tok
        w = self.waited[e]
        key = id(sem)
        if w.get(key, 0) >= val:
            return
        self.engs[e].wait_ge(sem, val)
        w[key] = val

    def _deps(self, e, reads, writes):
        deps = []
        for b in reads:
            if b.w is not None:
                deps.append(b.w)
            if b.excl:
                deps.extend(b.r)
        for b in writes:
            if b.w is not None:
                deps.append(b.w)
            deps.extend(b.r)
        for t in deps:
            if e == "pe" and t[2] == "pe":
                continue
            self._wait(e, t)

    def _upd(self, tok, reads, writes):
        for b in reads:
            if b.excl:
                b.w = tok
                b.r = []
            else:
                b.r.append(tok)
        for b in writes:
            b.w = tok
            b.r = []

    def op(self, e, fns, reads=(), writes=()):
        if callable(fns):
            fns = [fns]
        self._deps(e, reads, writes)
        ins = None
        for f in fns:
            ins = f()
            self.ninst += 1
        c = self.cur.get(e)
        if c is None or c[1] >= 30000:
            c = [self.newsem("s" + e), 0]
            self.cur[e] = c
        c[1] += 1
        ins.then_inc(c[0], 1)
        tok = (c[0], c[1], e)
        self.last[e] = tok
        self._upd(tok, reads, writes)
        return tok

    def dma(self, out, in_, reads=(), writes=(), q="sp", **kw):
        ring = self.rings.get(q)
        if ring is None:
            ring = [[self.newsem("d" + q), 0] for _ in range(8)]
            self.rings[q] = ring
        i = self.ridx.get(q, 0)
        self.ridx[q] = i + 1
        slot = ring[i % 8]
        if slot[1] > 0:
            self._wait(q, (slot[0], slot[1], "dma"))
        if slot[1] >= 30000:
            slot[0] = self.newsem("d" + q)
            slot[1] = 0
        self._deps(q, reads, writes)
        ins = self.engs[q].dma_start(out=out, in_=in_, **kw)
        self.ninst += 1
        slot[1] += 16
        ins.then_inc(slot[0], 16)
        tok = (slot[0], slot[1], "dma")
        self._upd(tok, reads, writes)
        return tok

    def barrier(self):
        toks = list(self.last.values())
        for q, ring in self.rings.items():
            for s in ring:
                if s[1] > 0:
                    toks.append((s[0], s[1], "dma"))
        for e in self.engs:
            for t in toks:
                self._wait(e, t)

    def bank(self):
        b = self.banks[self.bank_i % len(self.banks)]
        self.bank_i += 1
        return b


def _mm(nc, out, lhsT, rhs, start, stop):
    return lambda: nc.tensor.matmul(out, lhsT, rhs, start=start, stop=stop)


def build(nc, NBLK=32, NCTX=15, dbg=(), phases=("proj", "attn", "ssm", "merge", "ffn")):
    NOWN = NBLK - NCTX
    T = NBLK * 256
    TO = NOWN * 256
    NOUT = (NOWN - 1) * 256

    def din(name, shape, dt=F32):
        return nc.dram_tensor(name, list(shape), dt, kind="ExternalInput").ap()

    def dscr(name, shape, dt):
        kind = "ExternalOutput" if name in dbg else "Internal"
        return nc.dram_tensor(name, list(shape), dt, kind=kind).ap()

    I = dict(
        xcat=din("xcat", [T, D]), w_in=din("w_in", [D, 4096]), w_attn_proj=din("w_attn_proj", [512, D]),
        ssm_a_re=din("ssm_a_re", [32, 64]), ssm_a_im=din("ssm_a_im", [32, 64]), ssm_log_dt=din("ssm_log_dt", [32]),
        ssm_b_re=din("ssm_b_re", [32, 64, 16]), ssm_b_im=din("ssm_b_im", [32, 64, 16]),
        ssm_c_re=din("ssm_c_re", [32, 16, 64]), ssm_c_im=din("ssm_c_im", [32, 16, 64]),
        ssm_d=din("ssm_d", [32, 16]), w_glu=din("w_glu", [512, 512]), w_ssm_proj=din("w_ssm_proj", [512, D]),
        w_out=din("w_out", [D, D]), ln1_g=din("ln1_g", [D]), ln1_b=din("ln1_b", [D]),
        w_up=din("w_up", [D, 2 * FF]), conv_w=din("conv_w", [3, 2 * FF]), conv_b=din("conv_b", [2 * FF]),
        w_down=din("w_down", [FF, D]), ln2_g=din("ln2_g", [D]), ln2_b=din("ln2_b", [D]),
        rot_cos=din("rot_cos", [128, T]), rot_sin=din("rot_sin", [128, T]),
        gbias=din("gbias", [NOWN, 128, 512]), has_prev=din("has_prev", [128, 1]),
        ident=din("ident", [128, 128]), cmask=din("cmask", [128, 512], BF16), blkind=din("blkind", [32, T], BF16),
    )
    out = nc.dram_tensor("out", [NOUT, D], F32, kind="ExternalOutput").ap()
    S = dict(
        KT=dscr("KT", [4, 128, T], BF16), QT=dscr("QT", [4, 128, TO], BF16),
        VP=dscr("VP", [T, NH, 65], BF16), UT=dscr("UT", [4, 128, T], BF16),
        SG=dscr("SG", [16, 128, TO], BF16), SEL=dscr("SEL", [NH, 128, NOWN * 64], F32),
        ATT=dscr("ATT", [TO, 512], BF16), SSMY=dscr("SSMY", [4, 128, TO], BF16),
        H1=dscr("H1", [TO, D], F32), SBT=dscr("SBT", [NH, 32, TO], BF16),
    )
    SB = {k: Buf("scr_" + k) for k in S}
    cfg = dict(NBLK=NBLK, NCTX=NCTX, NOWN=NOWN, T=T, TO=TO, NOUT=NOUT)

    with ExitStack() as es:
        k = KB(nc, es)
        for i in range(8):
            t = es.enter_context(nc.psum_tensor(f"bank{i}", [128, 512], F32))
            k.banks.append((t, Buf(f"bank{i}", excl=True)))
        identf = es.enter_context(nc.sbuf_tensor("identf", [128, 128], F32))
        identb = es.enter_context(nc.sbuf_tensor("identb", [128, 128], BF16))
        cb = Buf("consts")
        k.dma(identf[:], I["ident"], writes=[cb])
        k.op("dve", lambda: nc.vector.tensor_copy(identb[:], identf[:]), reads=[cb], writes=[cb])
        C = dict(identf=identf, identb=identb, cb=cb)
        with ExitStack() as esx:
            pre = None
            if "ssm" in phases and "proj" in phases:
                pass
            with ExitStack() as esw:
                def load_win():
                    win = esw.enter_context(nc.sbuf_tensor("win", [128, 8, 4096], BF16))
                    winb = [Buf(f"win{kt}") for kt in range(8)]
                    C["win"], C["winb"] = win, winb
                    load_weight_bf16(k, win, winb, I["w_in"], 8, 4096, None, None, None)
                if "ssm" in phases:
                    pre = ssm_pre(k, I, S, SB, C, cfg, esx, after_alloc=(load_win if "proj" in phases else None))
                    k.barrier()
                elif "proj" in phases:
                    load_win()
                if "proj" in phases:
                    phase_proj(k, I, S, SB, C, cfg)
                    k.barrier()
            if "attn" in phases or "ssm" in phases:
                gens = []
                if "attn" in phases:
                    n_att = NH * sum(NCTX + j + 1 for j in range(NOWN)) + 2
                    gens.append([attn_gen(k, I, S, SB, C, cfg, esx, k.banks[0:3], k.banks[3:5]), n_att, 0])
                if "ssm" in phases:
                    gens.append([ssm_gen(k, I, S, SB, C, cfg, pre, esx, k.banks[5:8]), NBLK * 16 + 12, 0])
                while gens:
                    g = min(gens, key=lambda t: t[2] / t[1])
                    try:
                        next(g[0])
                        g[2] += 1
                    except StopIteration:
                        gens.remove(g)
                k.barrier()
        if "merge" in phases:
            phase_merge(k, I, S, SB, C, cfg)
            k.barrier()
        if "ffn" in phases:
            phase_ffn(k, I, S, SB, C, cfg, out)
            k.barrier()
    return nc


def load_weight_bf16(k, dst, dst_bufs, src, nkt, ncols, stage, stage_bufs, cnt, rows=128):
    CH = 2048
    for kt in range(nkt):
        for c0 in range(0, ncols, CH):
            w = min(CH, ncols - c0)
            k.dma(dst[0:rows, kt, c0:c0 + w], src[kt * rows:(kt + 1) * rows, c0:c0 + w], writes=[dst_bufs[kt]], q="pool")


def phase_proj(k, I, S, SB, C, cfg):
    nc = k.nc
    NBLK, NCTX, NOWN = cfg["NBLK"], cfg["NCTX"], cfg["NOWN"]
    identf, cb = C["identf"], C["cb"]
    with ExitStack() as es:
        def sb(n, shp, dt):
            return es.enter_context(nc.sbuf_tensor(n, shp, dt)), Buf(n)
        win, winb = C["win"], C["winb"]
        wsw, wswb = sb("wsw", [128, 8, 1024], BF16)
        k.op("pool", lambda: nc.gpsimd.memset(wsw[:], 0.0), writes=[wswb])
        for kt in range(8):
            src = win[:, kt, 0:1024].rearrange("p (h d) -> p h d", d=64)
            dst = wsw[:, kt, :].rearrange("p (h d) -> p h d", d=64)
            k.op("dve", lambda dst=dst, src=src: nc.vector.tensor_scalar_mul(dst[:, :, 0:8], src[:, :, 8:16], -1.0),
                 reads=[winb[kt]], writes=[wswb])
            k.op("dve", lambda dst=dst, src=src: nc.vector.tensor_copy(dst[:, :, 8:16], src[:, :, 0:8]),
                 reads=[winb[kt]], writes=[wswb])
        xt = [sb(f"xt{i}", [128, 2, 1024], F32) for i in range(2)]
        xT = [sb(f"xT{i}", [128, 8, 256], BF16) for i in range(2)]
        rc = [sb(f"rc{i}", [128, 256], F32) for i in range(2)]
        rs = [sb(f"rs{i}", [128, 256], F32) for i in range(2)]
        t1 = [sb(f"t1_{i}", [128, 256], F32) for i in range(2)]
        t2 = [sb(f"t2_{i}", [128, 256], F32) for i in range(2)]
        rotk = [sb(f"rotk{i}", [128, 256], F32) for i in range(2)]
        rotq = [sb(f"rotq{i}", [128, 256], F32) for i in range(8)]
        kb16 = [sb(f"kb16_{i}", [128, 256], BF16) for i in range(2)]
        ub = [sb(f"ub{i}", [128, 2, 256], BF16) for i in range(2)]
        vp = [sb(f"vp{i}", [128, 8, 65], BF16) for i in range(2)]
        kmeanT, kmb = sb("kmeanT", [128, 4, 64], F32)
        kms, kmsb = sb("kms", [128, 1], F32)
        gb, gbb = sb("gb", [128, 512], F32)
        sc, scb = sb("sc", [128, 512], F32)
        m8, m8b = sb("m8", [128, 16, 8], F32)
        selt, seltb = sb("selt", [128, 512], F32)
        vm, vmb = sb("vm", [128, 512], F32)
        sbT, sbTb = sb("sbT", [32, 16, 128], BF16)
        k.op("dve", lambda: nc.vector.memset(kmeanT[:], 0.0), writes=[kmb])
        for v_, vb_ in vp:
            k.op("pool", lambda v_=v_: nc.gpsimd.memset(v_[:], 1.0), writes=[vb_])
        ctr = [0]

        def evac_eng():
            ctr[0] += 1
            return "act" if ctr[0] % 2 else "dve"

        def copy_op(e, o, i_):
            if e == "act":
                return lambda: nc.scalar.copy(o, i_)
            if e == "dve":
                return lambda: nc.vector.tensor_copy(o, i_)
            return lambda: nc.gpsimd.tensor_copy(o, i_)

        def fm_group(region, W, wb, col0, xTt, xTb, bankb):
            fns = [_mm(nc, region, W[:, kt, col0:col0 + 128], xTt[:, kt, :], kt == 0, kt == 7) for kt in range(8)]
            k.op("pe", fns, reads=list(wb) + [xTb], writes=[bankb])

        import os
        STOP = int(os.environ.get("PROJ_STOP", "9"))

        def gating_a(i):
            j = i - NCTX
            rq = [rotq[(i % 2) * 4 + p] for p in range(4)]
            bt, bb = k.bank()
            fns = []
            for p in range(4):
                for qs in range(2):
                    idx = p * 2 + qs
                    fns.append(_mm(nc, bt[:, idx * 64:(idx + 1) * 64], rq[p][0][:, qs * 128:(qs + 1) * 128],
                                   kmeanT[:, p, :], True, True))
            k.op("pe", fns, reads=[rq[p][1] for p in range(4)] + [kmb], writes=[bb])
            k.dma(gb[:], I["gbias"][j], writes=[gbb])
            k.op("dve", lambda: nc.vector.tensor_tensor(out=sc[:], in0=bt[:, :], in1=gb[:], op=ALU.add),
                 reads=[bb, gbb], writes=[scb])
            for idx in range(16):
                k.op("dve", lambda idx=idx: nc.vector.max(out=m8[:, idx, :], in_=sc[:, idx * 32:(idx + 1) * 32]),
                     reads=[scb], writes=[m8b])
            sc3 = sc[:, :].rearrange("p (a b) -> p a b", b=32)
            sel3 = selt[:, :].rearrange("p (a b) -> p a b", b=32)
            k.op("dve", lambda: nc.vector.tensor_tensor(out=sel3, in0=sc3, in1=m8[:, :, 2:3].to_broadcast([128, 16, 32]),
                                                        op=ALU.is_ge), reads=[scb, m8b], writes=[seltb])
            k.op("dve", lambda: nc.vector.tensor_scalar(vm[:], gb[:], -1.0, None, op0=ALU.is_ge), reads=[gbb], writes=[vmb])
            k.op("dve", lambda: nc.vector.tensor_tensor(out=selt[:], in0=selt[:], in1=vm[:], op=ALU.mult),
                 reads=[vmb, seltb], writes=[seltb])
            k.op("dve", lambda: nc.vector.tensor_scalar(selt[:], selt[:], 30000.0, -30000.0, op0=ALU.mult, op1=ALU.add),
                 reads=[seltb], writes=[seltb])
            k.op("dve", lambda: nc.vector.memset(sel3[:, :, i:i + 1], 0.0), reads=[seltb], writes=[seltb])

        def gating_b(i):
            j = i - NCTX
            for g in range(4):
                bt2, bb2 = k.bank()
                fns = [(lambda q=q: nc.tensor.transpose(out=bt2[0:32, q * 128:(q + 1) * 128],
                                                        in_=selt[:, (4 * g + q) * 32:(4 * g + q + 1) * 32],
                                                        identity=identf[:])) for q in range(4)]
                k.op("pe", fns, reads=[seltb, cb], writes=[bb2])
                k.op("act", lambda bt2=bt2, g=g: nc.scalar.copy(sbT[:, 4 * g:4 * g + 4, :],
                                                              bt2[0:32, :].rearrange("p (a b) -> p a b", b=128)),
                     reads=[bb2], writes=[sbTb])
            for p in range(4):
                for hh in range(2):
                    k.dma(S["SBT"][2 * p + hh, :, j * 256:(j + 1) * 256].rearrange("n (q t) -> n q t", q=2),
                          sbT[:, 4 * p + hh:4 * p + hh + 3:2, :], reads=[sbTb])
        for i in range(NBLK if STOP > 0 else 0):
            own = i >= NCTX
            j = i - NCTX
            s = i % 2
            xt_t, xt_b = xt[s]
            xT_t, xT_b = xT[s]

            def load_blk(i2):
                s2 = i2 % 2
                k.dma(xt[s2][0][:], I["xcat"][i2 * 256:(i2 + 1) * 256, :].rearrange("(t p) f -> p t f", p=128), writes=[xt[s2][1]])
                k.dma(rc[s2][0][:], I["rot_cos"][:, i2 * 256:(i2 + 1) * 256], writes=[rc[s2][1]])
                k.dma(rs[s2][0][:], I["rot_sin"][:, i2 * 256:(i2 + 1) * 256], writes=[rs[s2][1]])

            if i == 0:
                load_blk(0)
            if i + 1 < NBLK:
                load_blk(i + 1)
            for tt in range(2):
                for g in range(2):
                    bt, bb = k.bank()
                    fns = [(lambda q=q: nc.tensor.transpose(out=bt[:, q * 128:(q + 1) * 128],
                                                            in_=xt_t[:, tt, (4 * g + q) * 128:(4 * g + q + 1) * 128],
                                                            identity=identf[:])) for q in range(4)]
                    k.op("pe", fns, reads=[xt_b, cb], writes=[bb])
                    e = evac_eng()
                    k.op(e, copy_op(e, xT_t[:, 4 * g:4 * g + 4, tt * 128:(tt + 1) * 128],
                                    bt[:, :].rearrange("p (a b) -> p a b", b=128)), reads=[bb], writes=[xT_b])
            if STOP < 2:
                continue
            rot_jobs = [("k", p) for p in range(4)] + ([("q", p) for p in range(4)] if own else [])
            for n_, (kind, p) in enumerate(rot_jobs):
                col0 = (512 if kind == "k" else 0) + p * 128
                bt, bb = k.bank()
                fm_group(bt[:, 0:256], win, winb, col0, xT_t, xT_b, bb)
                fm_group(bt[:, 256:512], wsw, [wswb], col0, xT_t, xT_b, bb)
                a1, a1b = t1[n_ % 2]
                a2, a2b = t2[n_ % 2]
                k.op("dve", lambda a1=a1, bt=bt: nc.vector.tensor_tensor(out=a1[:], in0=bt[:, 0:256], in1=rc[s][0][:], op=ALU.mult),
                     reads=[bb, rc[s][1]], writes=[a1b])
                k.op("dve", lambda a2=a2, bt=bt: nc.vector.tensor_tensor(out=a2[:], in0=bt[:, 256:512], in1=rs[s][0][:], op=ALU.mult),
                     reads=[bb, rs[s][1]], writes=[a2b])
                ro, rob = rotk[p % 2] if kind == "k" else rotq[(i % 2) * 4 + p]
                k.op("dve", lambda ro=ro, a1=a1, a2=a2: nc.vector.tensor_tensor(out=ro[:], in0=a1[:], in1=a2[:], op=ALU.add),
                     reads=[a1b, a2b], writes=[rob])
                o16, o16b = kb16[n_ % 2]
                if kind == "k":
                    k.op("act", lambda o16=o16, ro=ro: nc.scalar.copy(o16[:], ro[:]), reads=[rob], writes=[o16b])
                    k.dma(S["KT"][p, :, i * 256:(i + 1) * 256], o16[:], reads=[o16b])
                    k.op("dve", lambda ro=ro: nc.vector.reduce_sum(out=kms[:], in_=ro[:], axis=AX.X), reads=[rob], writes=[kmsb])
                    k.op("dve", lambda p=p: nc.vector.tensor_scalar_mul(kmeanT[0:64, p, i:i + 1], kms[0:64, :], 1.0 / 256.0),
                         reads=[kmsb], writes=[kmb])
                    k.op("dve", lambda p=p: nc.vector.tensor_scalar_mul(kmeanT[64:128, p, 32 + i:32 + i + 1], kms[64:128, :], 1.0 / 256.0),
                         reads=[kmsb], writes=[kmb])
                else:
                    k.op("act", lambda o16=o16, ro=ro: nc.scalar.mul(o16[:], ro[:], 0.125), reads=[rob], writes=[o16b])
                    k.dma(S["QT"][p, :, j * 256:(j + 1) * 256], o16[:], reads=[o16b])
            if STOP >= 5 and i - 1 >= NCTX:
                gating_a(i - 1)
            if STOP < 3:
                continue
            plain = [("u", 0), ("u", 1)] + ([("g", g) for g in range(8)] if own else [])
            for n_, (kind, g) in enumerate(plain):
                col0 = (1536 + g * 256) if kind == "u" else (2048 + g * 256)
                bt, bb = k.bank()
                fm_group(bt[:, 0:256], win, winb, col0, xT_t, xT_b, bb)
                fm_group(bt[:, 256:512], win, winb, col0 + 128, xT_t, xT_b, bb)
                o, ob = ub[n_ % 2]
                ov = o[:, :, :].rearrange("p a b -> p (a b)")
                if kind == "u":
                    k.op("act", lambda ov=ov, bt=bt: nc.scalar.copy(ov, bt[:, :]), reads=[bb], writes=[ob])
                    k.dma(S["UT"][2 * g:2 * g + 2, :, i * 256:(i + 1) * 256].rearrange("u p t -> p u t"), o[:], reads=[ob])
                else:
                    k.op("act", lambda ov=ov, bt=bt: nc.scalar.activation(out=ov, in_=bt[:, :], func=AF.Sigmoid),
                         reads=[bb], writes=[ob])
                    k.dma(S["SG"][2 * g:2 * g + 2, :, j * 256:(j + 1) * 256].rearrange("u p t -> p u t"), o[:], reads=[ob])
            if STOP < 4:
                continue
            for tt in range(2):
                bt, bb = k.bank()
                fns = [_mm(nc, bt[:, :], xT_t[:, kt, tt * 128:(tt + 1) * 128], win[:, kt, 1024:1536], kt == 0, kt == 7)
                       for kt in range(8)]
                k.op("pe", fns, reads=winb + [xT_b], writes=[bb])
                v_, vb_ = vp[tt]
                e = evac_eng()
                k.op(e, copy_op(e, v_[:, :, 0:64], bt[:, :].rearrange("p (h d) -> p h d", d=64)), reads=[bb], writes=[vb_])
                r0 = (i * 2 + tt) * 128
                k.dma(S["VP"][r0:r0 + 128, :, :], v_[:], reads=[vb_])
            if own and STOP >= 5 and i - 1 >= NCTX:
                gating_b(i - 1)
        if STOP >= 5:
            gating_a(NBLK - 1)
            gating_b(NBLK - 1)


def attn_gen(k, I, S, SB, C, cfg, es, sbanks, obanks):
    nc = k.nc
    NBLK, NCTX, NOWN, T, TO = cfg["NBLK"], cfg["NCTX"], cfg["NOWN"], cfg["T"], cfg["TO"]
    if True:
        def sb(n, shp, dt):
            return es.enter_context(nc.sbuf_tensor(n, shp, dt)), Buf(n)
        ktsb = [sb(f"ktsb{i}", [96, T], BF16) for i in range(1)]
        qtsb = [sb(f"qtsb{i}", [96, TO], BF16) for i in range(1)]
        vpsb = [sb(f"vpsb{i}", [128, NBLK * 2, 65], BF16) for i in range(2)]
        cm, cmb = sb("cm", [128, 512], BF16)
        pT = [sb(f"pT{i}", [128, 512], BF16) for i in range(4)]
        rcp, rcpb = sb("rcp", [128, 2], F32)
        ostg = [sb(f"ostg{i}", [128, 2, 64], BF16) for i in range(2)]
        k.dma(cm[:], I["cmask"], writes=[cmb])
        for kt_t, kt_b in ktsb:
            k.dma(kt_t[64:96, :], I["blkind"], writes=[kt_b])
        NSB = len(sbanks)
        LA = 2
        its = [(h, j, n) for h in range(NH) for j in range(NOWN) for n in range(NCTX + j + 1)]
        hbuf = {}
        jcount = [0]
        jslot = {}

        def load_head(h):
            s = 0
            p, hh = h // 2, h % 2
            kt_t, kt_b = ktsb[s]
            qt_t, qt_b = qtsb[s]
            vp_t, vp_b = vpsb[h % 2]
            k.dma(kt_t[0:64, :], S["KT"][p, hh * 64:(hh + 1) * 64, :], writes=[kt_b])
            k.dma(qt_t[0:64, :], S["QT"][p, hh * 64:(hh + 1) * 64, :], writes=[qt_b])
            k.dma(qt_t[64:96, :], S["SBT"][h], writes=[qt_b])
            vsrc = S["VP"][:, h, :].rearrange("(t p) c -> p t c", p=128)
            for t0 in range(0, NBLK * 2, 8):
                k.dma(vp_t[:, t0:t0 + 8, :], vsrc[:, t0:t0 + 8, :], writes=[vp_b])
            hbuf[h] = (kt_t, kt_b, qt_t, qt_b, vp_t, vp_b)

        def emit_front(idx):
            h, j, n = its[idx]
            if h not in hbuf:
                load_head(h)
            kt_t, kt_b, qt_t, qt_b, vp_t, vp_b = hbuf[h]
            diag = NCTX + j
            bs, bsb = sbanks[idx % NSB]
            fns = [_mm(nc, bs[:, kt * 256:(kt + 1) * 256], kt_t[:, n * 256 + kt * 128:n * 256 + (kt + 1) * 128],
                       qt_t[:, j * 256:(j + 1) * 256], True, True) for kt in range(2)]
            k.op("pe", fns, reads=[kt_b, qt_b], writes=[bsb])
            pt_t, pt_b = pT[idx % 4]
            k.op("act", lambda: nc.scalar.activation(out=pt_t[:], in_=bs[:, :], func=AF.Exp), reads=[bsb], writes=[pt_b])
            if n == diag:
                k.op("pool", lambda: nc.gpsimd.tensor_tensor(out=pt_t[:], in0=pt_t[:], in1=cm[:], op=ALU.mult),
                     reads=[cmb, pt_b], writes=[pt_b])

        def emit_back(idx):
            h, j, n = its[idx]
            kt_t, kt_b, qt_t, qt_b, vp_t, vp_b = hbuf[h]
            diag = NCTX + j
            pt_t, pt_b = pT[idx % 4]
            if n == 0:
                jslot[(h, j)] = jcount[0] % (len(obanks) // 2)
                jcount[0] += 1
            sl = jslot[(h, j)]
            fns = []
            for qs in range(2):
                bo, bob = obanks[sl * 2 + qs]
                fns += [_mm(nc, bo[:, 0:65], pt_t[:, kt * 256 + qs * 128:kt * 256 + (qs + 1) * 128], vp_t[:, n * 2 + kt, :],
                            n == 0 and kt == 0, n == diag and kt == 1) for kt in range(2)]
            k.op("pe", fns, reads=[pt_b, vp_b], writes=[obanks[sl * 2][1], obanks[sl * 2 + 1][1]])
            if n == diag:
                og_t, og_b = ostg[jcount[0] % 2]
                for qs in range(2):
                    bo, bob = obanks[sl * 2 + qs]
                    k.op("dve", lambda bo=bo, qs=qs: nc.vector.reciprocal(rcp[:, qs:qs + 1], bo[:, 64:65]), reads=[bob], writes=[rcpb])
                    k.op("dve", lambda bo=bo, qs=qs: nc.vector.tensor_scalar(
                        og_t[:, qs, :], bo[:, 0:64], rcp[:, qs:qs + 1], None, op0=ALU.mult),
                        reads=[bob, rcpb], writes=[og_b])
                k.dma(S["ATT"][j * 256:(j + 1) * 256, h * 64:(h + 1) * 64].rearrange("(q p) c -> p q c", p=128), og_t[:], reads=[og_b])

        for idx in range(len(its) + LA):
            if idx < len(its):
                emit_front(idx)
            if idx - LA >= 0:
                emit_back(idx - LA)
            yield


def ssm_pre(k, I, S, SB, C, cfg, esp, after_alloc=None):
    nc = k.nc
    NBLK, NCTX, NOWN = cfg["NBLK"], cfg["NCTX"], cfg["NOWN"]
    identf, cb = C["identf"], C["cb"]
    N = 256
    with ExitStack() as es:
        def sb(n, shp, dt):
            return es.enter_context(nc.sbuf_tensor(n, shp, dt)), Buf(n)

        def sbp(n, shp, dt):
            return esp.enter_context(nc.sbuf_tensor(n, shp, dt)), Buf(n)
        P = Buf("ssm_pre")
        mag = sbp("mag", [128, 16], F32)[0]
        BzT = [sbp(f"BzT{i}", [128, 16, 128], BF16)[0] for i in range(2)]
        CT = [sbp(f"CT{i}", [128, 16, 128], BF16)[0] for i in range(3)]
        dcol, _ = sbp("dcol", [128, 4], F32)
        Ec, _ = sbp("Ec", [128, 16, N + 1], F32)
        Es, _ = sbp("Es", [128, 16, N + 1], F32)
        if after_alloc is not None:
            after_alloc()

        def small(n):
            return sb(n, [128, 16], F32)[0]
        a_re, a_im, ldt, dt_, ang = [small(n) for n in ("a_re", "a_im", "ldt", "dt_", "ang")]
        cc, ss, cs, c_, s_ = [small(n) for n in ("cc", "ss", "cs", "c_", "s_")]
        lr, li, den, nre, zr, zi, q1, q2 = [small(n) for n in ("lr", "li", "den", "nre", "zr", "zi", "q1", "q2")]
        Bre, _ = sb("Bre", [128, 16, 16], F32)
        Bim, _ = sb("Bim", [128, 16, 16], F32)
        Bzr, _ = sb("Bzr", [128, 16, 16], F32)
        Bzi, _ = sb("Bzi", [128, 16, 16], F32)
        Bt1, _ = sb("Bt1", [128, 16, 16], F32)
        Bt2, _ = sb("Bt2", [128, 16, 16], F32)
        ZP, _ = sb("ZP", [128, 16, 128], F32)
        tmp = [sb(f"tmpE{i}", [128, 16, 128], F32)[0] for i in range(4)]

        def dv(f, r=(), w=()):
            k.op("dve", f, reads=[P] + list(r), writes=[P] + list(w))

        def tt(o, a, b, op):
            dv(lambda: nc.vector.tensor_tensor(out=o, in0=a, in1=b, op=op))

        def ts(o, a, s1, op0, s2=None, op1=None):
            if op1 is None:
                dv(lambda: nc.vector.tensor_scalar(o, a, s1, None, op0=op0))
            else:
                dv(lambda: nc.vector.tensor_scalar(o, a, s1, s2, op0=op0, op1=op1))

        def act(o, a, func, **kw):
            k.op("act", lambda: nc.scalar.activation(out=o, in_=a, func=func, **kw), reads=[P], writes=[P])

        for gp in range(2):
            rows = slice(gp * 64, (gp + 1) * 64)
            k.dma(a_re[rows, :], I["ssm_a_re"].rearrange("(i two) p -> two p i", two=2)[gp], writes=[P],
                  allow_slow_non_contiguous=True)
            k.dma(a_im[rows, :], I["ssm_a_im"].rearrange("(i two) p -> two p i", two=2)[gp], writes=[P],
                  allow_slow_non_contiguous=True)
            k.dma(ldt[rows, :], I["ssm_log_dt"].rearrange("(i two) -> two i", two=2)[gp].partition_broadcast(64), writes=[P],
                  allow_slow_non_contiguous=True)
            k.dma(Bre[rows, :, :], I["ssm_b_re"].rearrange("(i two) p h -> two p i h", two=2)[gp], writes=[P])
            k.dma(Bim[rows, :, :], I["ssm_b_im"].rearrange("(i two) p h -> two p i h", two=2)[gp], writes=[P])
        k.dma(dcol[:, :], I["ssm_d"].rearrange("(ut gl) h -> (gl h) ut", gl=8), writes=[P], allow_slow_non_contiguous=True)
        act(dt_[:], ldt[:], AF.Exp)
        tt(q1[:], a_re[:], dt_[:], ALU.mult)
        act(mag[:], q1[:], AF.Exp)
        tt(ang[:], a_im[:], dt_[:], ALU.mult)
        act(s_[:], ang[:], AF.Sin, scale=1.0 / 16.0)
        ts(q2[:], ang[:], -1.0 / 16.0, ALU.mult, math.pi / 2.0, ALU.add)
        act(c_[:], q2[:], AF.Sin)
        for _ in range(4):
            tt(cc[:], c_[:], c_[:], ALU.mult)
            tt(ss[:], s_[:], s_[:], ALU.mult)
            tt(cs[:], c_[:], s_[:], ALU.mult)
            tt(c_[:], cc[:], ss[:], ALU.subtract)
            ts(s_[:], cs[:], 2.0, ALU.mult)
        tt(lr[:], mag[:], c_[:], ALU.mult)
        tt(li[:], mag[:], s_[:], ALU.mult)
        tt(q1[:], a_re[:], a_re[:], ALU.mult)
        tt(q2[:], a_im[:], a_im[:], ALU.mult)
        tt(den[:], q1[:], q2[:], ALU.add)
        dv(lambda: nc.vector.reciprocal(den[:], den[:]))
        ts(nre[:], lr[:], -1.0, ALU.add)
        tt(q1[:], nre[:], a_re[:], ALU.mult)
        tt(q2[:], li[:], a_im[:], ALU.mult)
        tt(q1[:], q1[:], q2[:], ALU.add)
        tt(zr[:], q1[:], den[:], ALU.mult)
        tt(q1[:], li[:], a_re[:], ALU.mult)
        tt(q2[:], nre[:], a_im[:], ALU.mult)
        tt(q1[:], q1[:], q2[:], ALU.subtract)
        tt(zi[:], q1[:], den[:], ALU.mult)
        zrb = zr[:, :].unsqueeze(2).to_broadcast([128, 16, 16])
        zib = zi[:, :].unsqueeze(2).to_broadcast([128, 16, 16])
        tt(Bt1[:], Bre[:], zrb, ALU.mult)
        tt(Bt2[:], Bim[:], zib, ALU.mult)
        tt(Bzr[:], Bt1[:], Bt2[:], ALU.subtract)
        tt(Bt1[:], Bim[:], zrb, ALU.mult)
        tt(Bt2[:], Bre[:], zib, ALU.mult)
        tt(Bzi[:], Bt1[:], Bt2[:], ALU.add)

        def transpose16(src, dst, scale=None):
            for g in range(4):
                bt, bb = k.bank()
                fns = [(lambda q=q: nc.tensor.transpose(out=bt[:, q * 128:(q + 1) * 128], in_=src[:, 4 * g + q, :],
                                                        identity=identf[:])) for q in range(4)]
                k.op("pe", fns, reads=[P, cb], writes=[bb])
                o = dst[:, 4 * g:4 * g + 4, :]
                i_ = bt[:, :].rearrange("p (a b) -> p a b", b=128)
                if scale is None:
                    k.op("dve", lambda o=o, i_=i_: nc.vector.tensor_copy(o, i_), reads=[bb, P], writes=[P])
                else:
                    k.op("dve", lambda o=o, i_=i_: nc.vector.tensor_scalar_mul(o, i_, scale), reads=[bb, P], writes=[P])

        for ri, Bz in enumerate((Bzr, Bzi)):
            dv(lambda: nc.vector.memset(ZP[:], 0.0))
            for gp in range(2):
                for kq in range(4):
                    o = ZP[gp * 64:(gp + 1) * 64, kq::4, 32 * kq + 16 * gp:32 * kq + 16 * gp + 16]
                    i_ = Bz[gp * 64:(gp + 1) * 64, kq::4, :]
                    dv(lambda o=o, i_=i_: nc.vector.tensor_copy(o, i_))
            transpose16(ZP, BzT[ri])
        for ri, nm in enumerate(("ssm_c_re", "ssm_c_im")):
            dv(lambda: nc.vector.memset(ZP[:], 0.0))
            for kq in range(4):
                for gp in range(2):
                    r0 = (2 * kq + gp) * 16
                    k.dma(ZP[r0:r0 + 16, kq::4, gp * 64:(gp + 1) * 64],
                          I[nm].rearrange("(ut r) h p -> r h ut p", r=8)[2 * kq + gp], reads=[P], writes=[P])
            transpose16(ZP, CT[ri], scale=(None if ri == 0 else -1.0))
            if ri == 0:
                transpose16(ZP, CT[2], scale=-1.0)
        dv(lambda: nc.vector.memset(Ec[:, :, 0:1], 1.0))
        dv(lambda: nc.vector.memset(Es[:, :, 0:1], 0.0))
        dv(lambda: nc.vector.tensor_copy(Ec[:, :, 1], c_[:]))
        dv(lambda: nc.vector.tensor_copy(Es[:, :, 1], s_[:]))
        m = 1
        while m < N:
            cmb_ = Ec[:, :, m:m + 1].to_broadcast([128, 16, m])
            smb_ = Es[:, :, m:m + 1].to_broadcast([128, 16, m])
            e1c = Ec[:, :, 1:m + 1]
            e1s = Es[:, :, 1:m + 1]
            tA, tB, tC, tD = [t[:, :, 0:m] for t in tmp]
            tt(tA, e1c, cmb_, ALU.mult)
            tt(tB, e1s, smb_, ALU.mult)
            tt(tC, e1c, smb_, ALU.mult)
            tt(tD, e1s, cmb_, ALU.mult)
            tt(Ec[:, :, m + 1:2 * m + 1], tA, tB, ALU.subtract)
            tt(Es[:, :, m + 1:2 * m + 1], tC, tD, ALU.add)
            m *= 2

    return dict(P=P, mag=mag, BzT=BzT, CT=CT, dcol=dcol, Ec=Ec, Es=Es)


def ssm_gen(k, I, S, SB, C, cfg, pre, es, banks):
    nc = k.nc
    NBLK, NCTX, NOWN = cfg["NBLK"], cfg["NCTX"], cfg["NOWN"]
    N = 256
    P, mag, BzT, CT, dcol, Ec, Es = pre["P"], pre["mag"], pre["BzT"], pre["CT"], pre["dcol"], pre["Ec"], pre["Es"]
    bki = [0]

    xbanks = banks[0:len(banks) - 1]

    def nbank():
        bki[0] += 1
        return xbanks[bki[0] % len(xbanks)]
    if True:
        def sb(n, shp, dt):
            return es.enter_context(nc.sbuf_tensor(n, shp, dt)), Buf(n)
        wg_t, _ = sb("wglu", [128, 4, 512], BF16)
        wgb = [Buf(f"wglu{i}") for i in range(4)]
        load_weight_bf16(k, wg_t, wgb, I["w_glu"], 4, 512, None, None, None)
        NS = 3
        uc = [sb(f"uc{i}", [128, 4, N], BF16) for i in range(3)]
        xs = [sb(f"xs{i}", [128, 4, N], F32) for i in range(NS)]
        p1 = [sb(f"p1_{i}", [128, 2, N], F32) for i in range(NS)]
        p2 = [sb(f"p2_{i}", [128, 2, N], F32) for i in range(NS)]
        RBN = 6
        YD = 3
        rb = [sb(f"rb{i}", [128, 2, N], BF16) for i in range(RBN)]
        rb2 = [sb(f"rb2{i}", [128, 2, N], BF16) for i in range(RBN)]
        wi_ = [sb(f"win_{i}", [128, 2, N], F32) for i in range(NS)]
        W, _ = sb("Wst", [128, 16, 2, N], F32)
        wb = [Buf(f"W{i}") for i in range(16)]
        car = [sb(f"car{i}", [128, 16], F32) for i in range(2)]
        cai = [sb(f"cai{i}", [128, 16], F32) for i in range(2)]
        ct = [sb(f"ct{i}", [128, 16], F32) for i in range(4)]
        yv = [sb(f"yv{i}", [128, N], F32) for i in range(2)]
        ygf = [sb(f"ygf{i}", [128, 4, N], F32) for i in range(2)]
        ygb16, ygbb = sb("ygb16", [128, 4, N], BF16)
        sg = [sb(f"sg{i}", [128, N], F32) for i in range(2)]
        so, sob = sb("so", [128, 4, N], BF16)
        k.op("dve", lambda: nc.vector.memset(car[0][0][:], 0.0), writes=[car[0][1]])
        k.op("dve", lambda: nc.vector.memset(cai[0][0][:], 0.0), writes=[cai[0][1]])

        def load_u(c):
            u_t, u_b = uc[c % 3]
            k.dma(u_t[:], S["UT"][:, :, c * N:(c + 1) * N].rearrange("u p t -> p u t"), writes=[u_b])

        def in_a(g):
            c, i = divmod(g, 16)
            own = c >= NCTX
            if i == 0 and c + 1 < NBLK:
                load_u(c + 1)
            u_t, u_b = uc[c % 3]
            bt, bb = nbank()
            fns = [_mm(nc, bt[:, 0:N], BzT[0][:, i, :], u_t[:, i // 4, :], True, True),
                   _mm(nc, bt[:, N:2 * N], BzT[1][:, i, :], u_t[:, i // 4, :], True, True)]
            k.op("pe", fns, reads=[P, u_b], writes=[bb])
            x_t, x_b = xs[g % NS]
            k.op("act", lambda: nc.scalar.copy(x_t[:, 0:2, :].rearrange("p a b -> p (a b)"), bt[:, :]), reads=[bb], writes=[x_b])
            k.op("act", lambda: nc.scalar.copy(x_t[:, 2, :], bt[:, N:2 * N]), reads=[bb], writes=[x_b])
            k.op("act", lambda: nc.scalar.mul(x_t[:, 3, :], bt[:, 0:N], -1.0), reads=[bb], writes=[x_b])
            ecb = Ec[:, i:i + 1, 0:N].to_broadcast([128, 2, N])
            esb = Es[:, i:i + 1, 0:N].to_broadcast([128, 2, N])
            a_t, a_b = p1[g % NS]
            b_t, b_b = p2[g % NS]
            k.op("pool", lambda: nc.gpsimd.tensor_tensor(out=a_t[:], in0=x_t[:, 0:2, :], in1=ecb, op=ALU.mult),
                 reads=[x_b, P], writes=[a_b])
            k.op("pool", lambda: nc.gpsimd.tensor_tensor(out=b_t[:], in0=x_t[:, 2:4, :], in1=esb, op=ALU.mult),
                 reads=[x_b, P], writes=[b_b])

        def in_b(g):
            a_t, a_b = p1[g % NS]
            b_t, b_b = p2[g % NS]
            w_t, w_b = wi_[g % NS]
            k.op("dve", lambda: nc.vector.tensor_tensor(out=w_t[:], in0=a_t[:], in1=b_t[:], op=ALU.add),
                 reads=[a_b, b_b], writes=[w_b])

        def in_c(g):
            c, i = divmod(g, 16)
            cr_t, cr_b = car[c % 2]
            ci_t, ci_b = cai[c % 2]
            w_t, w_b = wi_[g % NS]
            magb = mag[:, i:i + 1].to_broadcast([128, N])
            k.op("dve", lambda: nc.vector.tensor_tensor_scan(
                out=W[:, i, 0, :], data0=magb, data1=w_t[:, 0, :], initial=cr_t[:, i:i + 1], op0=ALU.mult, op1=ALU.add),
                reads=[w_b, cr_b, P], writes=[wb[i]])
            k.op("dve", lambda: nc.vector.tensor_tensor_scan(
                out=W[:, i, 1, :], data0=magb, data1=w_t[:, 1, :], initial=ci_t[:, i:i + 1], op0=ALU.mult, op1=ALU.add),
                reads=[w_b, ci_b, P], writes=[wb[i]])
            if i == 15:
                ncr_t, ncr_b = car[(c + 1) % 2]
                nci_t, nci_b = cai[(c + 1) % 2]
                wlr = W[:, :, 0, N - 1]
                wli = W[:, :, 1, N - 1]
                enc = Ec[:, :, N]
                ens = Es[:, :, N]
                cts = [t[0] for t in ct]
                ctb = ct[0][1]
                k.op("dve", lambda: nc.vector.tensor_tensor(out=cts[0][:], in0=wlr, in1=enc, op=ALU.mult), reads=wb + [P], writes=[ctb])
                k.op("dve", lambda: nc.vector.tensor_tensor(out=cts[1][:], in0=wli, in1=ens, op=ALU.mult), reads=wb + [P, ctb], writes=[ctb])
                k.op("dve", lambda: nc.vector.tensor_tensor(out=cts[2][:], in0=wlr, in1=ens, op=ALU.mult), reads=wb + [P, ctb], writes=[ctb])
                k.op("dve", lambda: nc.vector.tensor_tensor(out=cts[3][:], in0=wli, in1=enc, op=ALU.mult), reads=wb + [P, ctb], writes=[ctb])
                k.op("dve", lambda: nc.vector.tensor_tensor(out=ncr_t[:], in0=cts[0][:], in1=cts[1][:], op=ALU.subtract),
                     reads=[ctb], writes=[ncr_b])
                k.op("dve", lambda: nc.vector.tensor_tensor(out=nci_t[:], in0=cts[2][:], in1=cts[3][:], op=ALU.add),
                     reads=[ctb], writes=[nci_b])

        def out_p(g):
            c, i = divmod(g, 16)
            b_t, b_b = rb[g % RBN]
            b2_t, b2_b = rb2[g % RBN]
            ecb = Ec[:, i:i + 1, 0:N].to_broadcast([128, 2, N])
            esb = Es[:, i:i + 1, 0:N].to_broadcast([128, 2, N])
            k.op("pool", lambda: nc.gpsimd.tensor_tensor(out=b_t[:], in0=W[:, i, :, :], in1=ecb, op=ALU.mult),
                 reads=[wb[i], P], writes=[b_b])
            k.op("dve", lambda: nc.vector.tensor_tensor(out=b2_t[:], in0=W[:, i, :, :], in1=esb, op=ALU.mult),
                 reads=[wb[i], P], writes=[b2_b])

        def out_y(g):
            c, i = divmod(g, 16)
            jc = c - NCTX
            u_t, u_b = uc[c % 3]
            b_t, b_b = rb[g % RBN]
            b2_t, b2_b = rb2[g % RBN]
            yg_t, yg_b = ygf[c % 2]
            ut, kq = i // 4, i % 4
            bt, bb = banks[-1]
            fns = [_mm(nc, bt[:, 0:N], CT[0][:, i, :], b_t[:, 0, :], kq == 0, False),
                   _mm(nc, bt[:, 0:N], CT[1][:, i, :], b_t[:, 1, :], False, False),
                   _mm(nc, bt[:, 0:N], CT[1][:, i, :], b2_t[:, 0, :], False, False),
                   _mm(nc, bt[:, 0:N], CT[2][:, i, :], b2_t[:, 1, :], False, kq == 3)]
            k.op("pe", fns, reads=[P, b_b, b2_b], writes=[bb])
            if kq == 3:
                y_t, y_b = yv[ut % 2]
                k.op("dve", lambda: nc.vector.scalar_tensor_tensor(
                    out=y_t[:], in0=u_t[:, ut, :], scalar=dcol[:, ut:ut + 1], in1=bt[:, 0:N], op0=ALU.mult, op1=ALU.add),
                    reads=[bb, u_b, P], writes=[y_b])
                k.op("act", lambda: nc.scalar.activation(out=yg_t[:, ut, :], in_=y_t[:], func=AF.Gelu_apprx_tanh),
                     reads=[y_b], writes=[yg_b])

        def glu(c):
            jc = c - NCTX
            yg_t, yg_b = ygf[c % 2]
            k.op("pool", lambda: nc.gpsimd.tensor_copy(ygb16[:], yg_t[:]), reads=[yg_b], writes=[ygbb])
            for mt in range(4):
                bt, bb = nbank()
                fns = [_mm(nc, bt[:, 0:N], wg_t[:, ut, mt * 128:(mt + 1) * 128], ygb16[:, ut, :], ut == 0, ut == 3) for ut in range(4)]
                k.op("pe", fns, reads=wgb + [ygbb], writes=[bb])
                s_t, s_b = sg[mt % 2]
                k.op("act", lambda s_t=s_t, bt=bt: nc.scalar.activation(out=s_t[:], in_=bt[:, 0:N], func=AF.Sigmoid),
                     reads=[bb], writes=[s_b])
                k.op("dve", lambda mt=mt, s_t=s_t: nc.vector.tensor_tensor(out=so[:, mt, :], in0=yg_t[:, mt, :], in1=s_t[:], op=ALU.mult),
                     reads=[s_b, yg_b], writes=[sob])
            k.dma(S["SSMY"][:, :, jc * N:(jc + 1) * N].rearrange("u p t -> p u t"), so[:], reads=[sob])

        G = NBLK * 16
        g0own = NCTX * 16
        load_u(0)
        GD = 4
        for step in range(G + 3 + YD + GD + 1):
            if step < G:
                in_a(step)
            if 0 <= step - 1 < G:
                in_b(step - 1)
            if 0 <= step - 2 < G:
                in_c(step - 2)
            g = step - 3
            if g0own <= g < G:
                out_p(g)
            g = step - 3 - YD
            if g0own <= g < G:
                out_y(g)
            g = step - 3 - YD - GD
            if g0own <= g < G and g % 16 == 15:
                glu(g // 16)
            yield


def ln_tile(k, ts_t, ts_b, g_t, b_t, gbuf, o_t, o_b, st, mv, lb):
    nc = k.nc
    st_t, mv_t = st, mv
    k.op("dve", lambda: nc.vector.bn_stats(out=st_t[:, 0, :], in_=ts_t[:, 0:512]), reads=[ts_b], writes=[lb])
    k.op("dve", lambda: nc.vector.bn_stats(out=st_t[:, 1, :], in_=ts_t[:, 512:1024]), reads=[ts_b, lb], writes=[lb])
    k.op("dve", lambda: nc.vector.bn_aggr(out=mv_t[:, 0:2], in_=st_t[:, :, :].rearrange("p a b -> p (a b)")), reads=[lb], writes=[lb])
    k.op("dve", lambda: nc.vector.tensor_scalar(mv_t[:, 2:3], mv_t[:, 1:2], LN_EPS, None, op0=ALU.add), reads=[lb], writes=[lb])
    k.op("act", lambda: nc.scalar.sqrt(mv_t[:, 3:4], mv_t[:, 2:3]), reads=[lb], writes=[lb])
    k.op("dve", lambda: nc.vector.reciprocal(mv_t[:, 4:5], mv_t[:, 3:4]), reads=[lb], writes=[lb])
    k.op("dve", lambda: nc.vector.tensor_scalar(o_t, ts_t[:, :], mv_t[:, 0:1], mv_t[:, 4:5], op0=ALU.subtract, op1=ALU.mult),
         reads=[ts_b, lb], writes=[o_b])
    k.op("dve", lambda: nc.vector.tensor_tensor(out=o_t, in0=o_t, in1=g_t[:], op=ALU.mult), reads=[gbuf, o_b], writes=[o_b])
    k.op("dve", lambda: nc.vector.tensor_tensor(out=o_t, in0=o_t, in1=b_t[:], op=ALU.add), reads=[gbuf, o_b], writes=[o_b])


def phase_merge(k, I, S, SB, C, cfg):
    nc = k.nc
    NBLK, NCTX, NOWN = cfg["NBLK"], cfg["NCTX"], cfg["NOWN"]
    identb, cb = C["identb"], C["cb"]
    with ExitStack() as es:
        def sb(n, shp, dt):
            return es.enter_context(nc.sbuf_tensor(n, shp, dt)), Buf(n)
        wap, _ = sb("wap", [128, 4, 1024], BF16)
        wsp, _ = sb("wsp", [128, 4, 1024], BF16)
        wout, _ = sb("wout", [128, 8, 1024], BF16)
        wapb = [Buf(f"wap{i}") for i in range(4)]
        wspb = [Buf(f"wsp{i}") for i in range(4)]
        woutb = [Buf(f"wout{i}") for i in range(8)]
        load_weight_bf16(k, wap, wapb, I["w_attn_proj"], 4, 1024, None, None, None)
        load_weight_bf16(k, wsp, wspb, I["w_ssm_proj"], 4, 1024, None, None, None)
        load_weight_bf16(k, wout, woutb, I["w_out"], 8, 1024, None, None, None)
        g1, gbuf = sb("g1", [128, 1024], F32)
        b1, _ = sb("b1", [128, 1024], F32)
        k.dma(g1[:], I["ln1_g"].partition_broadcast(128), writes=[gbuf])
        k.dma(b1[:], I["ln1_b"].partition_broadcast(128), writes=[gbuf])
        att = [sb(f"att{i}", [128, 2, 512], BF16) for i in range(2)]
        attT = [sb(f"attT{i}", [128, 4, 256], BF16) for i in range(2)]
        ssmy = [sb(f"ssmy{i}", [128, 4, 256], BF16) for i in range(2)]
        sgt = [sb(f"sgt{i}", [128, 16, 256], BF16) for i in range(2)]
        xo = [sb(f"xo{i}", [128, 2, 1024], F32) for i in range(2)]
        tA = [sb(f"tA{i}", [128, 256], F32) for i in range(2)]
        tB = [sb(f"tB{i}", [128, 256], F32) for i in range(2)]
        mg = [sb(f"mg{i}", [128, 8, 256], BF16) for i in range(2)]
        tsum = [sb(f"tsum{i}", [128, 1024], F32) for i in range(2)]
        h1 = [sb(f"h1_{i}", [128, 1024], F32) for i in range(2)]
        st, lb = sb("lnst", [128, 2, 6], F32)
        mv, _ = sb("lnmv", [128, 8], F32)
        def load_m(j):
            s = j % 2
            k.dma(att[s][0][:], S["ATT"][j * 256:(j + 1) * 256, :].rearrange("(t p) c -> p t c", p=128), writes=[att[s][1]])
            k.dma(ssmy[s][0][:], S["SSMY"][:, :, j * 256:(j + 1) * 256].rearrange("u p t -> p u t"), writes=[ssmy[s][1]])
            for u0 in range(0, 16, 4):
                k.dma(sgt[s][0][:, u0:u0 + 4, :], S["SG"][u0:u0 + 4, :, j * 256:(j + 1) * 256].rearrange("u p t -> p u t"),
                      writes=[sgt[s][1]])
            k.dma(xo[s][0][:], I["xcat"][(NCTX + j) * 256:(NCTX + j + 1) * 256, :].rearrange("(t p) f -> p t f", p=128),
                  writes=[xo[s][1]])

        load_m(0)
        for j in range(NOWN):
            s = j % 2
            if j + 1 < NOWN:
                load_m(j + 1)
            for tt in range(2):
                bt, bb = k.bank()
                bv = bt[:, :].bitcast(BF16)
                fns = [(lambda q=q: nc.tensor.transpose(out=bv[:, q * 128:(q + 1) * 128], in_=att[s][0][:, tt, q * 128:(q + 1) * 128],
                                                        identity=identb[:])) for q in range(4)]
                k.op("pe", fns, reads=[att[s][1], cb], writes=[bb])
                k.op("act", lambda bv=bv, tt=tt: nc.scalar.copy(attT[s][0][:, :, tt * 128:(tt + 1) * 128],
                                                                 bv[:, 0:512].rearrange("p (a b) -> p a b", b=128)),
                     reads=[bb], writes=[attT[s][1]])
            for m in range(8):
                bt, bb = k.bank()
                fa = [_mm(nc, bt[:, 0:256], wap[:, kt, m * 128:(m + 1) * 128], attT[s][0][:, kt, :], kt == 0, kt == 3) for kt in range(4)]
                k.op("pe", fa, reads=wapb + [attT[s][1]], writes=[bb])
                fb = [_mm(nc, bt[:, 256:512], wsp[:, kt, m * 128:(m + 1) * 128], ssmy[s][0][:, kt, :], kt == 0, kt == 3) for kt in range(4)]
                k.op("pe", fb, reads=wspb + [ssmy[s][1]], writes=[bb])
                a_t, a_b = tA[m % 2]
                b_t, b_b = tB[m % 2]
                k.op("dve", lambda a_t=a_t, bt=bt, m=m: nc.vector.tensor_tensor(out=a_t[:], in0=bt[:, 0:256], in1=sgt[s][0][:, m, :], op=ALU.mult),
                     reads=[bb, sgt[s][1]], writes=[a_b])
                k.op("dve", lambda b_t=b_t, bt=bt, m=m: nc.vector.tensor_tensor(out=b_t[:], in0=bt[:, 256:512], in1=sgt[s][0][:, 8 + m, :], op=ALU.mult),
                     reads=[bb, sgt[s][1]], writes=[b_b])
                k.op("dve", lambda a_t=a_t, b_t=b_t, m=m: nc.vector.tensor_tensor(out=mg[s][0][:, m, :], in0=a_t[:], in1=b_t[:], op=ALU.add),
                     reads=[a_b, b_b], writes=[mg[s][1]])
            for tt in range(2):
                ts_t, ts_b = tsum[tt]
                for half in range(2):
                    bt, bb = k.bank()
                    fns = [_mm(nc, bt[:, :], mg[s][0][:, kt, tt * 128:(tt + 1) * 128], wout[:, kt, half * 512:(half + 1) * 512], kt == 0, kt == 7)
                           for kt in range(8)]
                    k.op("pe", fns, reads=woutb + [mg[s][1]], writes=[bb])
                    k.op("dve", lambda bt=bt, tt=tt, half=half, ts_t=ts_t: nc.vector.scalar_tensor_tensor(
                        out=ts_t[:, half * 512:(half + 1) * 512], in0=xo[s][0][:, tt, half * 512:(half + 1) * 512], scalar=ALPHA,
                        in1=bt[:, :], op0=ALU.mult, op1=ALU.add), reads=[bb, xo[s][1]], writes=[ts_b])
                h_t, h_b = h1[tt]
                ln_tile(k, ts_t, ts_b, g1, b1, gbuf, h_t[:, :], h_b, st, mv, lb)
                r0 = j * 256 + tt * 128
                k.dma(S["H1"][r0:r0 + 128, :], h_t[:], reads=[h_b])


def phase_ffn(k, I, S, SB, C, cfg, out):
    nc = k.nc
    NBLK, NCTX, NOWN = cfg["NBLK"], cfg["NCTX"], cfg["NOWN"]
    identf, cb = C["identf"], C["cb"]
    NF = 2 * FF // 128
    NP = FF // 128
    with ExitStack() as es:
        def sb(n, shp, dt):
            return es.enter_context(nc.sbuf_tensor(n, shp, dt)), Buf(n)
        wup, _ = sb("wup", [128, 8, 2 * FF], BF16)
        wdn, _ = sb("wdn", [128, NP, 1024], BF16)
        wupb = [Buf(f"wup{i}") for i in range(8)]
        wdnb = [Buf(f"wdn{i}") for i in range(NP)]
        load_weight_bf16(k, wup, wupb, I["w_up"], 8, 2 * FF, None, None, None)
        load_weight_bf16(k, wdn, wdnb, I["w_down"], NP, 1024, None, None, None)
        g2, gbuf = sb("g2", [128, 1024], F32)
        b2, _ = sb("b2", [128, 1024], F32)
        cw, _ = sb("cw", [128, NF, 3], F32)
        cbi, _ = sb("cbi", [128, NF], F32)
        hp, _ = sb("hp", [128, 1], F32)
        k.dma(g2[:], I["ln2_g"].partition_broadcast(128), writes=[gbuf])
        k.dma(b2[:], I["ln2_b"].partition_broadcast(128), writes=[gbuf])
        for jj in range(3):
            for f0 in range(0, NF, 11):
                k.dma(cw[:, f0:f0 + 11, jj], I["conv_w"][jj].rearrange("(f p) -> p f", p=128)[:, f0:f0 + 11], writes=[gbuf],
                      allow_slow_non_contiguous=True)
        for f0 in range(0, NF, 11):
            k.dma(cbi[:, f0:f0 + 11], I["conv_b"].rearrange("(f p) -> p f", p=128)[:, f0:f0 + 11], writes=[gbuf],
                  allow_slow_non_contiguous=True)
        k.dma(hp[:, :], I["has_prev"], writes=[gbuf])
        halo, halob = sb("halo", [128, NF, 2], F32)
        h1 = [sb(f"h1f{i}", [128, 2, 1024], F32) for i in range(2)]
        h1Ts = [sb(f"h1T{i}", [128, 8, 256], BF16) for i in range(2)]
        CVN = 3
        raw = [sb(f"raw{i}", [128, 2, 258], F32) for i in range(CVN)]
        cv = [sb(f"cv{i}", [128, 2, 256], F32) for i in range(CVN)]
        gl = [sb(f"gl{i}", [128, 256], F32) for i in range(2)]
        actT, _ = sb("actT", [128, NP, 256], BF16)
        actTb = [Buf(f"actT{p}") for p in range(NP)]
        ptmp = [sb(f"ptmp{i}", [128, 256], F32) for i in range(2)]
        tsums = [sb(f"tsumf{i}", [128, 1024], F32) for i in range(2)]
        o_, ob = sb("of", [128, 1024], F32)
        st, lb = sb("lnst2", [128, 2, 6], F32)
        mv, _ = sb("lnmv2", [128, 8], F32)
        upbanks = k.banks[0:4]
        accbanks = k.banks[4:8]
        ubi = [0]

        def upbank():
            ubi[0] += 1
            return upbanks[ubi[0] % 4]

        def load_h1(j):
            h_t, h_b = h1[j % 2]
            k.dma(h_t[:], S["H1"][j * 256:(j + 1) * 256, :].rearrange("(t p) f -> p t f", p=128), writes=[h_b])

        LAG = 9
        load_h1(0)
        if NOWN > 1:
            load_h1(1)

        def transposes(j):
            h_t, h_b = h1[j % 2]
            hT, hTb = h1Ts[j % 2]
            for tt in range(2):
                for g in range(2):
                    bt, bb = upbank()
                    fns = [(lambda q=q: nc.tensor.transpose(out=bt[:, q * 128:(q + 1) * 128],
                                                            in_=h_t[:, tt, (4 * g + q) * 128:(4 * g + q + 1) * 128],
                                                            identity=identf[:])) for q in range(4)]
                    k.op("pe", fns, reads=[h_b, cb], writes=[bb])
                    k.op("act", lambda bt=bt, g=g, tt=tt: nc.scalar.copy(hT[:, 4 * g:4 * g + 4, tt * 128:(tt + 1) * 128],
                                                                      bt[:, :].rearrange("p (a b) -> p a b", b=128)),
                         reads=[bb], writes=[hTb])

        transposes(0)
        for j in range(1):
            s = j % 2
            h_t, h_b = h1[s]
            h1T, h1Tb = h1Ts[s]
            if j == 0:
                bt, bb = upbank()
                for f in range(NF):
                    fns = [_mm(nc, bt[:, 2 * f:2 * f + 2], wup[:, kt, f * 128:(f + 1) * 128], h1T[:, kt, 254:256], kt == 0, kt == 7)
                           for kt in range(8)]
                    k.op("pe", fns, reads=wupb + [h1Tb], writes=[bb])
                k.op("dve", lambda bt=bt: nc.vector.tensor_scalar(halo[:, :, :].rearrange("p a b -> p (a b)"), bt[:, 0:2 * NF],
                                                                  hp[:, 0:1], None, op0=ALU.mult),
                     reads=[bb, gbuf], writes=[halob])
                if NOWN > 1:
                    transposes(1)
                if 2 < NOWN:
                    load_h1(2)
                continue

        G = (NOWN - 1) * NP
        halobs = [Buf(f"halo{p}") for p in range(NP)]
        for hb_ in halobs:
            hb_.w = halob.w

        def halo_in(gs):
            p = gs % NP
            r_t, r_b = raw[gs % CVN]
            k.op("pool", lambda: nc.gpsimd.tensor_copy(r_t[:, :, 0:2], halo[:, p::NP, :]), reads=[halobs[p]], writes=[r_b])

        def up_stage(j, p, gs):
            h1T, h1Tb = h1Ts[j % 2]
            bt, bb = upbank()
            for v, f in enumerate((p, NP + p)):
                fns = [_mm(nc, bt[:, v * 256:(v + 1) * 256], wup[:, kt, f * 128:(f + 1) * 128], h1T[:, kt, :], kt == 0, kt == 7)
                       for kt in range(8)]
                k.op("pe", fns, reads=wupb + [h1Tb], writes=[bb])
            r_t, r_b = raw[gs % CVN]
            c_t, c_b = cv[gs % CVN]
            if gs == 0:
                halo_in(gs)
            k.op("act", lambda: nc.scalar.copy(r_t[:, :, 2:258], bt[:, :].rearrange("p (a b) -> p a b", b=256)),
                 reads=[bb], writes=[r_b])
            for v, f in enumerate((p, NP + p)):
                k.op("act", lambda v=v, f=f: nc.scalar.activation(out=c_t[:, v, :], in_=bt[:, v * 256:(v + 1) * 256],
                                                                  func=AF.Identity, scale=cw[:, f, 2:3], bias=cbi[:, f:f + 1]),
                     reads=[bb, gbuf], writes=[c_b])
            k.op("pool", lambda: nc.gpsimd.tensor_copy(halo[:, p::NP, :], r_t[:, :, 256:258]), reads=[r_b], writes=[halobs[p]])
            if gs + 1 < G:
                halo_in(gs + 1)
            for jj in (1, 0):
                k.op("dve", lambda jj=jj: nc.vector.scalar_tensor_tensor(
                    out=c_t[:, 0, :], in0=r_t[:, 0, jj:jj + 256], scalar=cw[:, p, jj:jj + 1], in1=c_t[:, 0, :],
                    op0=ALU.mult, op1=ALU.add), reads=[r_b, gbuf, c_b], writes=[c_b])
            k.op("dve", lambda: nc.vector.scalar_tensor_tensor(
                out=c_t[:, 1, :], in0=r_t[:, 1, 1:257], scalar=cw[:, NP + p, 1:2], in1=c_t[:, 1, :],
                op0=ALU.mult, op1=ALU.add), reads=[r_b, gbuf, c_b], writes=[c_b])
            k.op("dve", lambda: nc.vector.scalar_tensor_tensor(
                out=c_t[:, 1, :], in0=r_t[:, 1, 0:256], scalar=cw[:, NP + p, 0:1], in1=c_t[:, 1, :],
                op0=ALU.mult, op1=ALU.add), reads=[r_b, gbuf, c_b], writes=[c_b])

        def up_stage_b(j, p, gs):
            c_t, c_b = cv[gs % CVN]
            g_t, g_b = gl[gs % 2]
            k.op("act", lambda: nc.scalar.activation(out=g_t[:], in_=c_t[:, 1, :], func=AF.Gelu_apprx_tanh),
                 reads=[c_b], writes=[g_b])
            k.op("dve", lambda: nc.vector.tensor_tensor(out=actT[:, p, :], in0=g_t[:], in1=c_t[:, 0, :], op=ALU.mult),
                 reads=[g_b, c_b], writes=[actTb[p]])

        def down_stage(j, p):
            fns = []
            for tt in range(2):
                for half in range(2):
                    bt, bb = accbanks[tt * 2 + half]
                    fns.append(_mm(nc, bt[:, :], actT[:, p, tt * 128:(tt + 1) * 128], wdn[:, p, half * 512:(half + 1) * 512],
                                   p == 0, p == NP - 1))
            k.op("pe", fns, reads=[wdnb[p], actTb[p]], writes=[b_[1] for b_ in accbanks])
            if p == NP - 1:
                tail(j)

        def tail(j):
            h_t, h_b = h1[j % 2]
            for tt in range(2):
                ts_t, ts_b = tsums[tt]
                for half in range(2):
                    bt, bb = accbanks[tt * 2 + half]
                    k.op("dve", lambda bt=bt, tt=tt, half=half, ts_t=ts_t: nc.vector.scalar_tensor_tensor(
                        out=ts_t[:, half * 512:(half + 1) * 512], in0=h_t[:, tt, half * 512:(half + 1) * 512], scalar=ALPHA,
                        in1=bt[:, :], op0=ALU.mult, op1=ALU.add), reads=[bb, h_b], writes=[ts_b])
            for tt in range(2):
                ts_t, ts_b = tsums[tt]
                ln_tile(k, ts_t, ts_b, g2, b2, gbuf, o_[:, :], ob, st, mv, lb)
                r0 = (j - 1) * 256 + tt * 128
                k.dma(out[r0:r0 + 128, :], o_[:], reads=[ob])
            if j + 2 < NOWN:
                load_h1(j + 2)

        for gs in range(G + LAG + 2):
            if gs < G:
                j, p = 1 + gs // NP, gs % NP
                if p == NP - 8 and j + 1 < NOWN:
                    transposes(j + 1)
                up_stage(j, p, gs)
            g1_ = gs - 1
            if 0 <= g1_ < G:
                up_stage_b(1 + g1_ // NP, g1_ % NP, g1_)
            g2_ = gs - 1 - LAG
            if 0 <= g2_ < G:
                down_stage(1 + g2_ // NP, g2_ % NP)


_CACHE = {}


def _consts(NBLK, NCTX, half, full):
    T = NBLK * 256
    NOWN = NBLK - NCTX
    off = 0 if (half == 1 or not full) else -(NBLK // 2) * 256
    if not full:
        off = 0
    pos = (np.arange(T, dtype=np.float64) + off)
    inv = 500000.0 ** (-np.arange(0, 16, 2, dtype=np.float64) / 16.0)
    ang = pos[None, :] * inv[:, None]
    cos = np.ones((64, T)); sin = np.zeros((64, T))
    cos[0:8] = np.cos(ang); cos[8:16] = np.cos(ang)
    sin[0:8] = np.sin(ang); sin[8:16] = np.sin(ang)
    rot_cos = np.concatenate([cos, cos], 0).astype(np.float32)
    rot_sin = np.concatenate([sin, sin], 0).astype(np.float32)
    first_valid = 0 if (half == 1 or not full) else NBLK // 2
    gb = np.full((NOWN, 32), -1e30, np.float32)
    for j in range(NOWN):
        own = NCTX + j
        for n in range(min(own, 32)):
            if n >= first_valid:
                gb[j, n] = 0.0
    gbias = np.ascontiguousarray(np.broadcast_to(np.tile(gb[:, None, :], (1, 16, 1)).reshape(NOWN, 1, 512), (NOWN, 128, 512)))
    hasp = np.full((128, 1), 1.0 if (half == 1 or not full) else 0.0, np.float32)
    key = np.arange(128)[:, None]
    q = np.arange(256)[None, :]
    cm = np.concatenate([(key <= q), (key + 128 <= q)], 1).astype(np.float32)
    blk = np.zeros((32, T), np.float32)
    for n in range(min(NBLK, 32)):
        blk[n, n * 256:(n + 1) * 256] = 1.0
    return dict(rot_cos=rot_cos, rot_sin=rot_sin, gbias=gbias, has_prev=hasp,
                ident=np.eye(128, dtype=np.float32), cmask=cm.astype(ml_dtypes.bfloat16), blkind=blk.astype(ml_dtypes.bfloat16))


def kernel(**inputs):
    x = np.asarray(inputs["x"], np.float32)
    B, SEQ, _ = x.shape
    NBLK, NCTX = 32, 15
    if "nc" not in _CACHE:
        nc = bass.Bass("TRN2", target_bir_lowering=False)
        build(nc, NBLK, NCTX)
        _CACHE["nc"] = nc
    nc = _CACHE["nc"]
    wnames = ["w_in", "w_attn_proj", "ssm_a_re", "ssm_a_im", "ssm_log_dt", "ssm_b_re", "ssm_b_im", "ssm_c_re", "ssm_c_im",
              "ssm_d", "w_glu", "w_ssm_proj", "w_out", "ln1_g", "ln1_b", "w_up", "conv_w", "conv_b", "w_down", "ln2_g", "ln2_b"]
    w = {n: np.ascontiguousarray(np.asarray(inputs[n], np.float32)[0]) for n in wnames}
    in_maps = []
    for c in range(8):
        b, half = c // 2, c % 2
        if half == 1:
            xcat = np.ascontiguousarray(x[b])
        else:
            xcat = np.concatenate([np.zeros((SEQ // 2, D), np.float32), x[b, :SEQ // 2]], 0)
        m = dict(w)
        m["xcat"] = xcat
        m.update(_consts(NBLK, NCTX, half, True))
        in_maps.append(m)
    res = run_bass_kernel_spmd(nc, in_maps, core_ids=list(range(8)))
    outp = np.empty((B, SEQ, D), np.float32)
    for c in range(8):
        b, half = c // 2, c % 2
        outp[b, half * (SEQ // 2):(half + 1) * (SEQ // 2)] = res.results[c]["out"]
    return outp
```
